# Optimizing a Trainium2 kernel written in Bass

```python
import math
import jax
import jax.numpy as jnp
from jax import lax
import numpy as np

D_MODEL = 1024
BATCH = 4
SEQ = 8192
DEPTH = 1

ATT_HEAD_DIM = 128
ATT_HEADS = 8
ATT_WIDTH = ATT_HEADS * ATT_HEAD_DIM
MOBA_BLOCK = 256
MOBA_TOPK = 3
MOBA_QCHUNK = 64
REL_BUCKETS = 32
REL_MAX_DIST = 2048
MLSTM_HEADS = 4
MLSTM_WIDTH = D_MODEL
MLSTM_HEAD_DIM = MLSTM_WIDTH // MLSTM_HEADS
MLSTM_CHUNK = 64
QKV_BLOCK = 4
CONV_WIDTH = 4
MIX_WIDTH = ATT_WIDTH + MLSTM_WIDTH
IN_COLS = 4 * ATT_WIDTH + 2 * MLSTM_WIDTH
RMS_EPS = 1e-6
LN_EPS = 1e-5

kernel_name = "hymba_moba_mlstm_sandwich"


def rmsnorm(x, g):
    xf = x.astype(jnp.float32)
    y = xf * lax.rsqrt(jnp.mean(xf * xf, axis=-1, keepdims=True) + RMS_EPS)
    return (y * g.astype(jnp.float32)).astype(x.dtype)


def split_heads(a, n_heads):
    b, s, _ = a.shape
    return a.reshape(b, s, n_heads, -1).transpose(0, 2, 1, 3)


def t5_bucket(dist):
    n = jnp.maximum(dist, 0)
    max_exact = REL_BUCKETS // 2
    nf = jnp.maximum(n, 1).astype(jnp.float32)
    large = max_exact + (jnp.log(nf / max_exact) / math.log(REL_MAX_DIST / max_exact)
                         * (REL_BUCKETS - max_exact)).astype(jnp.int32)
    large = jnp.minimum(large, REL_BUCKETS - 1)
    return jnp.where(n < max_exact, n, large)


def moba_attention(q, k, v, rel_bias):
    B, H, S, Dh = q.shape
    nb = -(-S // MOBA_BLOCK)
    pad = nb * MOBA_BLOCK - S
    kb = jnp.pad(k, ((0, 0), (0, 0), (0, pad), (0, 0))).reshape(B, H, nb, MOBA_BLOCK, Dh)
    vb = jnp.pad(v, ((0, 0), (0, 0), (0, pad), (0, 0))).reshape(B, H, nb, MOBA_BLOCK, Dh)
    kmean = jnp.mean(kb.astype(jnp.float32), axis=3)
    ksel = min(MOBA_TOPK, nb)
    scale = Dh ** -0.5
    bias_h = rel_bias.T.astype(jnp.float32)
    b_idx = jnp.arange(B)[:, None, None, None]
    h_idx = jnp.arange(H)[None, :, None, None]
    h_idx5 = h_idx[..., None]
    blk_off = jnp.arange(MOBA_BLOCK)
    nq = S // MOBA_QCHUNK
    qc = jnp.moveaxis(q.reshape(B, H, nq, MOBA_QCHUNK, Dh), 2, 0)

    def one_chunk(args):
        ci, qblk = args
        q_pos = ci * MOBA_QCHUNK + jnp.arange(MOBA_QCHUNK)
        own = (ci * MOBA_QCHUNK) // MOBA_BLOCK
        qf = qblk.astype(jnp.float32) * scale
        gate = jnp.einsum('bhqd,bhnd->bhqn', qf, kmean)
        gate = jnp.where(jnp.arange(nb) < own, gate, -jnp.inf)
        _, gidx = lax.top_k(gate, ksel)
        sel_valid = gidx < own
        k_sel = kb[b_idx, h_idx, gidx].astype(jnp.float32)
        v_sel = vb[b_idx, h_idx, gidx].astype(jnp.float32)
        s_sel = jnp.einsum('bhqd,bhqnkd->bhqnk', qf, k_sel)
        k_pos_sel = gidx[..., None] * MOBA_BLOCK + blk_off
        bucket_sel = t5_bucket(q_pos[None, None, :, None, None] - k_pos_sel)
        s_sel = s_sel + bias_h[h_idx5, bucket_sel]
        s_sel = jnp.where(sel_valid[..., None], s_sel, -jnp.inf)
        k_own = lax.dynamic_index_in_dim(kb, own, axis=2, keepdims=False).astype(jnp.float32)
        v_own = lax.dynamic_index_in_dim(vb, own, axis=2, keepdims=False).astype(jnp.float32)
        s_own = jnp.einsum('bhqd,bhkd->bhqk', qf, k_own)
        dist = q_pos[:, None] - (own * MOBA_BLOCK + blk_off)[None, :]
        s_own = s_own + bias_h[:, t5_bucket(dist)][None]
        s_own = jnp.where(dist >= 0, s_own, -jnp.inf)
        logits = jnp.concatenate(
            [s_sel.reshape(B, H, MOBA_QCHUNK, ksel * MOBA_BLOCK), s_own], axis=-1)
        p = jax.nn.softmax(logits, axis=-1)
        p_sel = p[..., :ksel * MOBA_BLOCK].reshape(B, H, MOBA_QCHUNK, ksel, MOBA_BLOCK)
        p_own = p[..., ksel * MOBA_BLOCK:]
        out = (jnp.einsum('bhqnk,bhqnkd->bhqd', p_sel, v_sel)
               + jnp.einsum('bhqk,bhkd->bhqd', p_own, v_own))
        return out.astype(q.dtype)

    out = lax.map(one_chunk, (jnp.arange(nq), qc))
    return jnp.moveaxis(out, 0, 2).reshape(B, H, S, Dh)


def causal_depthwise_conv(x, w, b):
    c = x.shape[-1]
    y = lax.conv_general_dilated(
        x, w[:, None, :], window_strides=(1,), padding=[(CONV_WIDTH - 1, 0)],
        dimension_numbers=('NWC', 'WIO', 'NWC'), feature_group_count=c)
    return y + b


def blockdiag_proj(x, w):
    b, s, width = x.shape
    xb = x.reshape(b, s, width // QKV_BLOCK, QKV_BLOCK)
    return jnp.einsum('bsni,nio->bsno', xb, w).reshape(b, s, width)


def mlstm_chunkwise(q, k, v, i_pre, f_pre):
    B, H, S, Dh = q.shape
    L = MLSTM_CHUNK
    nc = S // L
    qf = q.astype(jnp.float32)
    kf = k.astype(jnp.float32) * (Dh ** -0.5)
    vf = v.astype(jnp.float32)
    log_i = i_pre.astype(jnp.float32)
    log_f = jax.nn.log_sigmoid(f_pre.astype(jnp.float32))

    def to_chunks(a):
        return jnp.moveaxis(a.reshape((B, H, nc, L) + a.shape[3:]), 2, 0)

    causal = jnp.tril(jnp.ones((L, L), dtype=bool))

    def step(carry, inp):
        C, n, m = carry
        qc, kc, vc, li, lf = inp
        b = jnp.cumsum(lf, axis=-1)
        log_d = b[..., :, None] - b[..., None, :] + li[..., None, :]
        log_d = jnp.where(causal, log_d, -jnp.inf)
        inter = b + m[..., None]
        m_t = jnp.maximum(inter, jnp.max(log_d, axis=-1))
        d = jnp.exp(log_d - m_t[..., None])
        s = jnp.einsum('bhtd,bhsd->bhts', qc, kc) * d
        dec = jnp.exp(inter - m_t)
        num = (jnp.einsum('bhts,bhsd->bhtd', s, vc)
               + dec[..., None] * jnp.einsum('bhtd,bhde->bhte', qc, C))
        den = jnp.sum(s, axis=-1) + dec * jnp.einsum('bhtd,bhd->bht', qc, n)
        h = num / jnp.maximum(jnp.abs(den), jnp.exp(-m_t))[..., None]
        b_last = b[..., -1]
        log_w = b_last[..., None] - b + li
        m_new = jnp.maximum(b_last + m, jnp.max(log_w, axis=-1))
        w = jnp.exp(log_w - m_new[..., None])
        carry_dec = jnp.exp(b_last + m - m_new)
        C_new = carry_dec[..., None, None] * C + jnp.einsum('bhs,bhsd,bhse->bhde', w, kc, vc)
        n_new = carry_dec[..., None] * n + jnp.einsum('bhs,bhsd->bhd', w, kc)
        return (C_new, n_new, m_new), h

    init = (jnp.zeros((B, H, Dh, Dh), jnp.float32),
            jnp.zeros((B, H, Dh), jnp.float32),
            jnp.zeros((B, H), jnp.float32))
    _, hs = lax.scan(step, init, (to_chunks(qf), to_chunks(kf), to_chunks(vf),
                                  to_chunks(log_i), to_chunks(log_f)))
    return jnp.moveaxis(hs, 0, 2).reshape(B, H, S, Dh)


def headwise_layernorm(h, w):
    mu = jnp.mean(h, axis=-1, keepdims=True)
    var = jnp.mean(jnp.square(h - mu), axis=-1, keepdims=True)
    y = (h - mu) * lax.rsqrt(var + LN_EPS)
    b, s = h.shape[:2]
    return y.reshape(b, s, -1) * w.astype(jnp.float32)


def hybrid_layer(x, rel_bias, g_pre, g_post, w_in, conv_w, conv_b, wq_m, wk_m, wv_m,
                 w_if, b_if, mh_norm, skip, w_out):
    B, S, _ = x.shape
    h = rmsnorm(x, g_pre)
    proj = h @ w_in
    aw, mw = ATT_WIDTH, MLSTM_WIDTH
    q_a = proj[..., 0:aw]
    k_a = proj[..., aw:2 * aw]
    v_a = proj[..., 2 * aw:3 * aw]
    g_a = proj[..., 3 * aw:4 * aw]
    x_m = proj[..., 4 * aw:4 * aw + mw]
    z_m = proj[..., 4 * aw + mw:]
    o_a = moba_attention(split_heads(q_a, ATT_HEADS), split_heads(k_a, ATT_HEADS),
                         split_heads(v_a, ATT_HEADS), rel_bias)
    o_a = o_a.transpose(0, 2, 1, 3).reshape(B, S, aw)
    y_a = o_a * jax.nn.silu(g_a)
    x_c = jax.nn.silu(causal_depthwise_conv(x_m, conv_w, conv_b))
    q_m = blockdiag_proj(x_c, wq_m)
    k_m = blockdiag_proj(x_c, wk_m)
    v_m = blockdiag_proj(x_m, wv_m)
    gates = jnp.concatenate([q_m, k_m, v_m], axis=-1) @ w_if + b_if
    i_pre = gates[..., :MLSTM_HEADS].transpose(0, 2, 1)
    f_pre = gates[..., MLSTM_HEADS:].transpose(0, 2, 1)
    h_m = mlstm_chunkwise(split_heads(q_m, MLSTM_HEADS), split_heads(k_m, MLSTM_HEADS),
                          split_heads(v_m, MLSTM_HEADS), i_pre, f_pre)
    h_m = headwise_layernorm(h_m.transpose(0, 2, 1, 3), mh_norm)
    y_m = ((h_m + skip.astype(jnp.float32) * x_c.astype(jnp.float32))
           * jax.nn.silu(z_m.astype(jnp.float32))).astype(x.dtype)
    y = jnp.concatenate([y_a, y_m], axis=-1) @ w_out
    return x + rmsnorm(y, g_post)


def setup_inputs(seed: int = 0) -> dict:
    key = jax.random.key(seed)
    ks = jax.random.split(key, 16)
    f32 = jnp.float32
    nblk = MLSTM_WIDTH // QKV_BLOCK
    x = jax.random.normal(ks[0], (BATCH, SEQ, D_MODEL), f32)
    rel_bias = 0.5 * jax.random.normal(ks[1], (REL_BUCKETS, ATT_HEADS), f32)
    g_pre = 1.0 + 0.02 * jax.random.normal(ks[2], (DEPTH, D_MODEL), f32)
    g_post = 1.0 + 0.02 * jax.random.normal(ks[3], (DEPTH, D_MODEL), f32)
    w_in = jax.random.normal(ks[4], (DEPTH, D_MODEL, IN_COLS), f32) * D_MODEL ** -0.5
    conv_w = jax.random.normal(ks[5], (DEPTH, CONV_WIDTH, MLSTM_WIDTH), f32) * CONV_WIDTH ** -0.5
    conv_b = 0.02 * jax.random.normal(ks[6], (DEPTH, MLSTM_WIDTH), f32)
    wq_m = jax.random.normal(ks[7], (DEPTH, nblk, QKV_BLOCK, QKV_BLOCK), f32) * QKV_BLOCK ** -0.5
    wk_m = jax.random.normal(ks[8], (DEPTH, nblk, QKV_BLOCK, QKV_BLOCK), f32) * QKV_BLOCK ** -0.5
    wv_m = jax.random.normal(ks[9], (DEPTH, nblk, QKV_BLOCK, QKV_BLOCK), f32) * QKV_BLOCK ** -0.5
    w_if = jax.random.normal(ks[10], (DEPTH, 3 * MLSTM_WIDTH, 2 * MLSTM_HEADS), f32) * (3 * MLSTM_WIDTH) ** -0.5
    b_i = 0.1 * jax.random.normal(ks[11], (DEPTH, MLSTM_HEADS), f32)
    b_f = (jnp.linspace(3.0, 6.0, MLSTM_HEADS, dtype=f32)[None, :]
           + 0.1 * jax.random.normal(ks[12], (DEPTH, MLSTM_HEADS), f32))
    b_if = jnp.concatenate([b_i, b_f], axis=-1)
    mh_norm = 1.0 + 0.02 * jax.random.normal(ks[13], (DEPTH, MLSTM_WIDTH), f32)
    skip = 1.0 + 0.02 * jax.random.normal(ks[14], (DEPTH, MLSTM_WIDTH), f32)
    w_out = jax.random.normal(ks[15], (DEPTH, MIX_WIDTH, D_MODEL), f32) * MIX_WIDTH ** -0.5
    return {"x": x, "rel_bias": rel_bias, "g_pre": g_pre, "g_post": g_post, "w_in": w_in,
            "conv_w": conv_w, "conv_b": conv_b, "wq_m": wq_m, "wk_m": wk_m, "wv_m": wv_m,
            "w_if": w_if, "b_if": b_if, "mh_norm": mh_norm, "skip": skip, "w_out": w_out}


def reference(x, rel_bias, g_pre, g_post, w_in, conv_w, conv_b, wq_m, wk_m, wv_m,
              w_if, b_if, mh_norm, skip, w_out):
    for l in range(DEPTH):
        x = hybrid_layer(x, rel_bias, g_pre[l], g_post[l], w_in[l], conv_w[l], conv_b[l],
                         wq_m[l], wk_m[l], wv_m[l], w_if[l], b_if[l], mh_norm[l], skip[l],
                         w_out[l])
    return x
```

```python
import contextlib
import math
import numpy as np
import ml_dtypes
import concourse.bass as bass
import concourse.mybir as mybir
from concourse.bass_utils import run_bass_kernel_spmd

F32 = mybir.dt.float32
BF16 = mybir.dt.bfloat16
AF = mybir.ActivationFunctionType
ALU = mybir.AluOpType
AX = mybir.AxisListType

D = 1024
NCOL = 3584
XW = 2432
SCALE = 128.0 ** -0.5
ENGS = ["pe", "act", "dve", "pool", "sp"]
STQ = "sp"
import os
CUT = int(os.environ.get("K_CUT", "0"))


class Op:
    __slots__ = ("eng", "fn", "deps", "dma_key", "signal", "event", "inc")

    def __init__(self, eng, fn, deps, dma_key, inc):
        self.eng = eng
        self.fn = fn
        self.deps = deps
        self.dma_key = dma_key
        self.signal = False
        self.event = None
        self.inc = inc


class Prog:
    def __init__(self, nc):
        self.nc = nc
        self.ops = []
        self.last_w = {}
        self.readers = {}
        self.root = contextlib.ExitStack()
        self.scopes = [self.root]
        self.last_eng = {}
        self.last_key = {}
        self.rr = 0
        self.keymap = {}

    def sb(self, name, shape, dt):
        return self.scopes[-1].enter_context(self.nc.sbuf_tensor("s_" + name, list(shape), dt))

    def ps(self, name, shape, dt=F32):
        return self.scopes[-1].enter_context(self.nc.psum_tensor("p_" + name, list(shape), dt))

    def push(self):
        st = contextlib.ExitStack()
        self.scopes.append(st)
        return st

    def pop(self):
        self.barrier()
        self.scopes.pop().close()

    def add(self, eng, fn, r=(), w=(), dma_key=None, inc=16, deps=None):
        if deps is None:
            deps = set()
            for t in r:
                x = self.last_w.get(t)
                if x is not None:
                    deps.add(x)
            for t in w:
                x = self.last_w.get(t)
                if x is not None:
                    deps.add(x)
                for y in self.readers.get(t, ()):
                    deps.add(y)
            if eng == "pe":
                deps = {d for d in deps if not (self.ops[d].eng == "pe" and self.ops[d].dma_key is None)}
            best = {}
            keep = set()
            for d in deps:
                o = self.ops[d]
                if o.dma_key is not None:
                    keep.add(d)
                elif best.get(o.eng, -1) < d:
                    best[o.eng] = d
            deps = keep | set(best.values())
        idx = len(self.ops)
        self.ops.append(Op(eng, fn, sorted(deps), dma_key, inc))
        for t in r:
            lst = self.readers.setdefault(t, [])
            if dma_key is None:
                lst[:] = [y for y in lst if not (self.ops[y].eng == eng and self.ops[y].dma_key is None)]
            lst.append(idx)
        for t in w:
            self.last_w[t] = idx
            self.readers[t] = []
        if dma_key is None:
            self.last_eng[eng] = idx
        else:
            self.last_key[dma_key] = idx
        return idx

    def barrier(self):
        deps = set(self.last_eng.values()) | set(self.last_key.values())
        for e in ENGS:
            self.add(e, lambda h: h.nop(), deps=set(deps))
        self.last_w = {}
        self.readers = {}
        self.keymap = {}

    def dma(self, out, in_, r=(), w=(), key=None, q="sp"):
        return self.add(q, lambda e: e.dma_start(out=out, in_=in_), r, w, dma_key=self.mapkey(key))

    def mapkey(self, key):
        if key not in self.keymap:
            self.keymap[key] = "d%d" % len(self.keymap)
        return self.keymap[key]

    def mm(self, out, lhsT, rhs, start=True, stop=True, r=(), w=()):
        return self.add("pe", lambda e: e.matmul(out, lhsT, rhs, start=start, stop=stop), r, w)

    def tr(self, out, in_, ident, r=(), w=()):
        return self.add("pe", lambda e: e.transpose(out, in_, ident), r, w)

    def act(self, out, in_, func, r=(), w=(), **kw):
        return self.add("act", lambda e: e.activation(out, in_, func, **kw), r, w)

    def v(self, name, r=(), w=(), eng="dve", **kw):
        return self.add(eng, lambda e: getattr(e, name)(**kw), r, w)

    def evac(self, out, in_, r=(), w=(), scale=None, eng=None):
        if eng is None:
            self.rr += 1
            eng = "act" if self.rr % 2 else "dve"
        if eng == "act":
            if scale is None:
                return self.act(out, in_, AF.Copy, r, w)
            return self.act(out, in_, AF.Copy, r, w, scale=float(scale))
        if scale is None:
            return self.v("tensor_copy", r, w, out=out, in_=in_)
        return self.v("tensor_scalar", r, w, out=out, in0=in_, scalar1=float(scale), scalar2=None, op0=ALU.mult)

    def emit(self):
        nc = self.nc
        ops = self.ops
        for o in ops:
            for d in o.deps:
                ops[d].signal = True
        esem = {e: self.root.enter_context(nc.semaphore("es_" + e)) for e in ENGS}
        dsem = {}
        cnt = {e: 0 for e in ENGS}
        dcnt = {}
        for o in ops:
            if o.dma_key is not None:
                if o.dma_key not in dsem:
                    dsem[o.dma_key] = self.root.enter_context(nc.semaphore("ds_%d" % len(dsem)))
                    dcnt[o.dma_key] = 0
                dcnt[o.dma_key] += o.inc
                o.event = (dsem[o.dma_key], dcnt[o.dma_key])
            elif o.signal:
                cnt[o.eng] += 1
                o.event = (esem[o.eng], cnt[o.eng])
        self.nsem = len(dsem) + len(ENGS)
        self.cnt = cnt
        per = {e: [o for o in ops if o.eng == e] for e in ENGS}

        def run(ename, handle):
            seen = {}
            for o in per[ename]:
                need = {}
                for d in o.deps:
                    sem, val = ops[d].event
                    k = id(sem)
                    if need.get(k, (None, 0))[1] < val:
                        need[k] = (sem, val)
                for k, (sem, val) in need.items():
                    if seen.get(k, 0) < val:
                        handle.wait_ge(sem, val)
                        seen[k] = val
                ins = o.fn(handle)
                if o.dma_key is not None:
                    ins.then_inc(o.event[0], o.inc)
                elif o.signal:
                    ins.then_inc(o.event[0], 1)

        with nc.Block() as block:
            @block.tensor
            def _(e):
                run("pe", e)

            @block.scalar
            def _(e):
                run("act", e)

            @block.vector
            def _(e):
                run("dve", e)

            @block.gpsimd
            def _(e):
                run("pool", e)

            @block.sync
            def _(e):
                run("sp", e)


def dram_bcast(ap, nparts, n):
    return bass.AP(ap.tensor, ap.offset, [[0, nparts], [1, n]])


def build_program(S, dbg=(), upto=4):
    NG = S // 512
    NT = S // 128
    NCH = S // 128
    nc = bass.Bass("TRN2", target_bir_lowering=False)

    def din(name, shape, dt=F32):
        return nc.dram_tensor(name, list(shape), dt, kind="ExternalInput").ap()

    def dscr(name, shape, dt=BF16):
        kind = "ExternalOutput" if name in dbg else "Internal"
        return nc.dram_tensor(name, list(shape), dt, kind=kind).ap()

    x = din("x", [S, D])
    gpre = din("gpre", [1, D])
    gpost = din("gpost", [1, D])
    w_in = din("w_in", [D, NCOL])
    w_out = din("w_out", [2 * D, D])
    b2 = din("b2", [4, 128, XW])
    bfar = din("bfar", [128, 4])
    cw = din("cw", [128, 32])
    cb = din("cb", [128, 8])
    bd = din("bd", [128, 3 * 4 * 128])
    bdT = din("bdT", [128, 3 * 8 * 128])
    wif = din("wif", [128, 3 * 8 * 4])
    bg = din("bg", [2, 2])
    mhn = din("mhn", [128, 4])
    skp = din("skp", [128, 4])
    identb_d = din("identb", [128, 128], BF16)
    identf_d = din("identf", [128, 128])
    causb_d = din("causb", [128, 128], BF16)
    emat_d = din("emat", [32, 32 * 128], BF16)
    negm_d = din("negm", [128, 32 * 32])
    sel_d = din("sel", [2, 2 * 128])
    out = nc.dram_tensor("out", [S, D], F32, kind="ExternalOutput").ap()

    QT = dscr("QT", [4, 128, S])
    KT = dscr("KT", [4, 128, S])
    Vd = dscr("Vd", [S, 512])
    SGd = dscr("SGd", [S, 512])
    QmT = dscr("QmT", [4, 128, S])
    KmT = dscr("KmT", [4, 128, S])
    Kmd = dscr("Kmd", [S, 512])
    Vmd = dscr("Vmd", [S, 512])
    XSd = dscr("XSd", [4, 128, S])
    SZd = dscr("SZd", [4, 128, S])
    GId = dscr("GId", [2, S], F32)
    GFd = dscr("GFd", [2, S], F32)
    TB = min(S, 1024)
    NB = S // TB
    YTsrc = nc.dram_tensor("YTsrc", [NB, D, TB], BF16).ap()
    YTall = nc.dram_tensor("YTall", [NB, 2 * D, TB], BF16).ap()

    def yt_slice(row0, g):
        blk, off = (g * 512) // TB, (g * 512) % TB
        return YTsrc[blk, row0:row0 + 128, off:off + 512]
    YTdbg = dscr("YTdbg", [D, S]) if "YTdbg" in dbg else None

    P = Prog(nc)
    with P.root:
        identb = P.sb("identb", [128, 128], BF16)
        identf = P.sb("identf", [128, 128], F32)
        causb = P.sb("causb", [128, 128], BF16)
        gpre_b = P.sb("gpre_b", [128, D], F32)
        gpost_b = P.sb("gpost_b", [128, D], F32)
        kms = P.sb("kms", [128, 4, 32], F32)
        bfar_t = P.sb("bfar_t", [128, 4], F32)
        mhn_t = P.sb("mhn_t", [128, 4], F32)
        skp_t = P.sb("skp_t", [128, 4], F32)
        cb_t = P.sb("cb_t", [128, 8], F32)
        bg_t = P.sb("bg_t", [2, 2], F32)
        P.dma(identb[:], identb_d, w=["identb"], key="c0")
        P.dma(identf[:], identf_d, w=["identf"], key="c1")
        P.dma(causb[:], causb_d, w=["causb"], key="c2")
        P.dma(gpre_b[:], dram_bcast(gpre, 128, D), w=["gpre_b"], key="c3")
        P.dma(gpost_b[:], dram_bcast(gpost, 128, D), w=["gpost_b"], key="c4")
        P.dma(bfar_t[:], bfar, w=["bfar"], key="c5")
        P.dma(mhn_t[:], mhn, w=["mhn"], key="c6")
        P.dma(skp_t[:], skp, w=["skp"], key="c7")
        P.dma(cb_t[:], cb, w=["cb"], key="c8")
        P.dma(bg_t[:], bg, w=["bg"], key="c9")
        P.v("memset", w=["kms"], ap=kms[:], constant=0.0)
        zcol = P.sb("zcol", [128, 2], F32)
        P.v("memset", w=["zcol"], ap=zcol[:, 0:1], constant=0.0)
        P.v("memset", w=["zcol"], ap=zcol[:, 1:2], constant=1.0)
        P.barrier()

        if upto >= 1:
            P.push()
            Wb = P.sb("Wb", [128, 8, NCOL], BF16)
            Dg = P.sb("Dg", [128, 8, 4, 128], BF16)
            bd_b = P.sb("bd_b", [128, 12, 128], BF16)
            Wg = P.sb("Wg", [128, 8, 2, 4], BF16)
            pm = [P.ps("pm%d" % i, [128, 512], F32) for i in range(4)]
            P.push()
            wst = [P.sb("wst%d" % i, [128, NCOL], F32) for i in range(2)]
            cw_t = P.sb("cw_t", [128, 8, 4], F32)
            bd_f = P.sb("bd_f", [128, 12, 128], F32)
            bdT_f = P.sb("bdT_f", [128, 24, 128], F32)
            wif_t = P.sb("wif_t", [128, 24, 4], F32)
            P.dma(cw_t[:], cw.rearrange("p (t j) -> p t j", j=4), w=["cw"], key="c10")
            P.dma(bd_f[:], bd.rearrange("p (k c) -> p k c", c=128), w=["bd_f"], key="c11")
            P.dma(bdT_f[:], bdT.rearrange("p (k c) -> p k c", c=128), w=["bdT_f"], key="c12")
            P.dma(wif_t[:], wif.rearrange("p (k c) -> p k c", c=4), w=["wif"], key="c13")
            for c in range(8):
                P.dma(wst[c % 2][:], w_in[c * 128:(c + 1) * 128, :], w=[("wst", c % 2)], key="wst%d" % (c % 2))
                third = NCOL // 4
                P.v("tensor_copy", r=[("wst", c % 2)], w=[("Wb", c, 0)], out=Wb[:, c, 0:2 * third], in_=wst[c % 2][:, 0:2 * third])
                P.v("tensor_copy", r=[("wst", c % 2)], w=[("Wb", c, 1)], eng="pool", out=Wb[:, c, 2 * third:NCOL],
                    in_=wst[c % 2][:, 2 * third:NCOL])
            wb_tok = [("Wb", c, k) for c in range(8) for k in range(2)]
            P.v("tensor_copy", r=["bd_f"], w=["bd_b"], out=bd_b[:], in_=bd_f[:])
            for ti in range(8):
                for j in range(4):
                    P.v("tensor_scalar", r=["identf", "cw"], w=["Dg"], out=Dg[:, ti, j, :], in0=identf[:],
                        scalar1=cw_t[:, ti, j:j + 1], scalar2=None, op0=ALU.mult)
            for ti in range(8):
                pw = pm[ti % 4]
                P.mm(pw[:, 0:4], bdT_f[:, 0 * 8 + ti, :], wif_t[:, 0 * 8 + ti, :], start=True, stop=False,
                     r=["bdT_f", "wif"], w=[("pm", ti % 4)])
                P.mm(pw[:, 0:4], bdT_f[:, 1 * 8 + ti, :], wif_t[:, 1 * 8 + ti, :], start=False, stop=True,
                     r=["bdT_f", "wif"], w=[("pm", ti % 4)])
                P.mm(pw[:, 8:12], bdT_f[:, 2 * 8 + ti, :], wif_t[:, 2 * 8 + ti, :], start=True, stop=True,
                     r=["bdT_f", "wif"], w=[("pm", ti % 4)])
                P.v("tensor_copy", r=[("pm", ti % 4)], w=["Wg"], out=Wg[:, ti, 0, :], in_=pw[:, 0:4])
                P.v("tensor_copy", r=[("pm", ti % 4)], w=["Wg"], out=Wg[:, ti, 1, :], in_=pw[:, 8:12])
            P.pop()
            xbuf = [P.sb("xbuf%d" % i, [128, 4, D], F32) for i in range(2)]
            junk = P.sb("junk", [128, D], BF16)
            ss = [P.sb("ss%d" % i, [128, 4], F32) for i in range(2)]
            rs = [P.sb("rs%d" % i, [128, 4], F32) for i in range(2)]
            hb = P.sb("hb", [128, 4, D], BF16)
            hT = [P.sb("hT%d" % i, [128, 8, 512], BF16) for i in range(2)]
            XM = [P.sb("XM%d" % i, [128, 8, 516], BF16) for i in range(2)]
            XC = P.sb("XC", [128, 8, 512], BF16)
            NST = 6
            stg = [P.sb("stg%d" % i, [128, 512], BF16) for i in range(NST)]
            tst = [P.sb("tst%d" % i, [128, 4, 512], BF16) for i in range(4)]
            gst = [P.sb("gst%d" % i, [2, 512], F32) for i in range(2)]
            pt = [P.ps("pt%d" % i, [128, 1024], BF16) for i in range(2)]
            pgi = P.ps("pgi", [2, 512], F32)
            pgf = P.ps("pgf", [2, 512], F32)

            P.v("memset", w=[("XM", 0), ("XM", 1)], ap=XM[0][:, :, 0:4], constant=0.0)

            sctr = [0]
            pctr = [0]

            def next_pm():
                pctr[0] += 1
                i = pctr[0] % 4
                return pm[i], ("pm", i)

            def next_stg():
                sctr[0] += 1
                i = sctr[0] % NST
                return stg[i], ("stg", i), "stg%d" % i

            def load_x(g):
                P.dma(xbuf[g % 2][:], x[g * 512:(g + 1) * 512, :].rearrange("(t p) c -> p t c", p=128),
                      w=[("x", g % 2)], key="x%d" % (g % 2))

            load_x(0)
            for g in range(NG if CUT != 1 else 0):
                if g + 1 < NG:
                    load_x(g + 1)
                xb = xbuf[g % 2]
                tsl = slice(g * 512, (g + 1) * 512)
                sq, rq = ss[g % 2], rs[g % 2]
                for t in range(4):
                    P.act(junk[:], xb[:, t, :], AF.Square, r=[("x", g % 2)], w=["junk", ("ss", g % 2)],
                          accum_out=sq[:, t:t + 1])
                P.act(rq[:], sq[:], AF.Sqrt, r=[("ss", g % 2)], w=[("rs", g % 2)], scale=1.0 / D, bias=1e-6)
                P.v("reciprocal", r=[("rs", g % 2)], w=[("rs", g % 2)], out=rq[:], in_=rq[:])
                for t in range(4):
                    P.v("scalar_tensor_tensor", r=[("rs", g % 2), ("x", g % 2), "gpre_b"], w=[("hb", t)],
                        out=hb[:, t, :], in0=xb[:, t, :], scalar=rq[:, t:t + 1], in1=gpre_b[:], op0=ALU.mult, op1=ALU.mult)
                h_T = hT[g % 2]
                for c in range(8):
                    ptt = pt[c % 2]
                    for t in range(4):
                        P.tr(ptt[:, t * 128:(t + 1) * 128], hb[:, t, c * 128:(c + 1) * 128], identb[:],
                             r=[("hb", t), "identb"], w=[("pt", c % 2)])
                    P.evac(h_T[:, c, :], ptt[:, 0:512], r=[("pt", c % 2)], w=[("hT", g % 2, c)])
                hT_tok = [("hT", g % 2, c) for c in range(8)]

                def chan_tile(i):
                    pw, ptok = next_pm()
                    for c in range(8):
                        P.mm(pw[:], Wb[:, c, i * 128:(i + 1) * 128], h_T[:, c, :], start=(c == 0), stop=(c == 7),
                             r=wb_tok + hT_tok if c in (0, 7) else (), w=[ptok])
                    return pw, ptok

                xm = XM[g % 2]
                xmp = XM[(g + 1) % 2]
                for ti in range(8):
                    pw, ptok = chan_tile(8 + ti)
                    P.evac(xm[:, ti, 4:516], pw[:], r=[ptok], w=[("XM", g % 2, ti)])
                    if g > 0:
                        P.v("tensor_copy", r=[("XM", (g + 1) % 2, ti)], w=[("XM", g % 2, ti)], eng="pool",
                            out=xm[:, ti, 0:4], in_=xmp[:, ti, 512:516])
                    pc, ctok = next_pm()
                    for j in range(4):
                        P.mm(pc[:], Dg[:, ti, j, :], xm[:, ti, 1 + j:1 + j + 512], start=(j == 0), stop=(j == 3),
                             r=["Dg", ("XM", g % 2, ti)], w=[ctok])
                    P.act(XC[:, ti, :], pc[:], AF.Silu, r=[ctok, "cb"], w=[("XC", ti)], bias=cb_t[:, ti:ti + 1])
                if CUT == 2:
                    continue
                for (pg, c0, nm) in ((pgi, 0, "pgi"), (pgf, 2, "pgf")):
                    for ti in range(8):
                        P.mm(pg[:], Wg[:, ti, 0, c0:c0 + 2], XC[:, ti, :], start=(ti == 0), stop=False,
                             r=["Wg", ("XC", ti)], w=[nm])
                        P.mm(pg[:], Wg[:, ti, 1, c0:c0 + 2], xm[:, ti, 4:516], start=False, stop=(ti == 7),
                             r=["Wg", ("XM", g % 2, ti)], w=[nm])
                P.act(gst[0][:], pgi[:], AF.Identity, r=["pgi", "bg"], w=["gst0"], bias=bg_t[:, 0:1])
                P.dma(GId[:, tsl], gst[0][:], r=["gst0"], w=[("GI", g)], key="gst0", q=STQ)
                P.act(gst[1][:], pgf[:], AF.Identity, r=["pgf", "bg"], w=["gst1"], bias=bg_t[:, 1:2])
                P.dma(GFd[:, tsl], gst[1][:], r=["gst1"], w=[("GF", g)], key="gst1", q=STQ)
                if CUT == 3:
                    continue
                for ti in range(4):
                    pw, ptok = next_pm()
                    P.mm(pw[:], bd_b[:, 0 * 4 + ti, :], XC[:, ti, :], r=["bd_b", ("XC", ti)], w=[ptok])
                    st, stok, skey = next_stg()
                    P.evac(st[:], pw[:], r=[ptok], w=[stok])
                    P.dma(QmT[ti, :, tsl], st[:], r=[stok], w=[("QmT", ti, g)], key=skey, q=STQ)
                    pw, ptok = next_pm()
                    P.mm(pw[:], bd_b[:, 1 * 4 + ti, :], XC[:, ti, :], r=["bd_b", ("XC", ti)], w=[ptok])
                    st, stok, skey = next_stg()
                    P.evac(st[:], pw[:], r=[ptok], w=[stok], scale=1.0 / 16)
                    P.dma(KmT[ti, :, tsl], st[:], r=[stok], w=[("KmT", ti, g)], key=skey, q=STQ)
                    st, stok, skey = next_stg()
                    P.v("tensor_scalar", r=[("XC", ti), "skp"], w=[stok], eng="pool", out=st[:], in0=XC[:, ti, :],
                        scalar1=skp_t[:, ti:ti + 1], scalar2=None, op0=ALU.mult)
                    P.dma(XSd[ti, :, tsl], st[:], r=[stok], w=[("XS", ti, g)], key=skey, q=STQ)
                for (which, tsi, dst, nm) in ((1, 0, Kmd, "Km"), (2, 1, Vmd, "Vm")):
                    tb = tst[tsi]
                    for t in range(4):
                        pw, ptok = next_pm()
                        for ti in range(4):
                            if which == 1:
                                lhsT = XC[:, ti, t * 128:(t + 1) * 128]
                                rr = [("XC", ti)]
                            else:
                                lhsT = xm[:, ti, 4 + t * 128:4 + (t + 1) * 128]
                                rr = [("XM", g % 2, ti)]
                            P.mm(pw[:, ti * 128:(ti + 1) * 128], lhsT, bd_b[:, which * 4 + ti, :],
                                 r=["bd_b"] + rr, w=[ptok])
                        P.evac(tb[:, t, :], pw[:], r=[ptok], w=[("tst", tsi)], scale=(1.0 / 16 if which == 1 else None))
                    P.dma(dst[tsl, :].rearrange("(t p) c -> p t c", p=128), tb[:], r=[("tst", tsi)], w=[(nm, g)],
                          key="tst%d" % tsi, q=STQ)
                if CUT == 4:
                    continue
                for ti in range(4):
                    pw, ptok = chan_tile(16 + ti)
                    st, stok, skey = next_stg()
                    P.act(st[:], pw[:], AF.Silu, r=[ptok, "zcol"], w=[stok], bias=zcol[:, 0:1])
                    P.dma(SZd[ti, :, tsl], st[:], r=[stok], w=[("SZ", ti, g)], key=skey, q=STQ)
                for h in range(4):
                    pw, ptok = chan_tile(h)
                    st, stok, skey = next_stg()
                    P.evac(st[:], pw[:], r=[ptok], w=[stok])
                    P.dma(QT[h, :, tsl], st[:], r=[stok], w=[("QT", h, g)], key=skey, q=STQ)
                for h in range(4):
                    pw, ptok = chan_tile(4 + h)
                    st, stok, skey = next_stg()
                    for bb in range(2):
                        P.act(st[:, bb * 256:(bb + 1) * 256], pw[:, bb * 256:(bb + 1) * 256], AF.Copy, r=[ptok],
                              w=[stok, "kms"], accum_out=kms[:, h, 2 * g + bb:2 * g + bb + 1])
                    P.dma(KT[h, :, tsl], st[:], r=[stok], w=[("KT", h, g)], key=skey, q=STQ)
                for (c0, tsi, dst, nm, silu) in ((2560, 2, Vd, "V", False), (3072, 3, SGd, "SG", True)):
                    tb = tst[tsi]
                    for t in range(4):
                        pw, ptok = next_pm()
                        for c in range(8):
                            P.mm(pw[:], h_T[:, c, t * 128:(t + 1) * 128], Wb[:, c, c0:c0 + 512], start=(c == 0),
                                 stop=(c == 7), r=wb_tok + hT_tok if c in (0, 7) else (), w=[ptok])
                        if silu:
                            P.act(tb[:, t, :], pw[:], AF.Silu, r=[ptok, "zcol"], w=[("tst", tsi)], bias=zcol[:, 0:1])
                        else:
                            P.evac(tb[:, t, :], pw[:], r=[ptok], w=[("tst", tsi)])
                    P.dma(dst[tsl, :].rearrange("(t p) c -> p t c", p=128), tb[:], r=[("tst", tsi)], w=[(nm, g)],
                          key="tst%d" % tsi, q=STQ)
            P.pop()

        if upto >= 2:
            P.push()
            emat = P.sb("emat", [32, 32, 128], BF16)
            negm = P.sb("negm", [128, 32, 32], F32)
            kmb = P.sb("kmb", [128, 4, 32], BF16)
            KTs = [P.sb("KTs%d" % i, [128, S], BF16) for i in range(2)]
            Va = [P.sb("Va%d" % i, [128, NT, 130], BF16) for i in range(2)]
            b2s = P.sb("b2s", [128, XW], F32)
            EB = [P.sb("EB%d" % i, [128, XW], BF16) for i in range(2)]
            QTg = [P.sb("QTg%d" % i, [128, 512], BF16) for i in range(2)]
            SGg = [P.sb("SGg%d" % i, [128, 4, 128], BF16) for i in range(2)]
            gm = P.sb("gm", [128, 4, 32], F32)
            top8 = P.sb("top8", [128, 4, 8], F32)
            Mb = P.sb("Mb", [128, 4, 32], BF16)
            MT = [P.sb("MT%d" % i, [32, 512], BF16) for i in range(2)]
            NPT = 4
            PT = [P.sb("PT%d" % i, [128, 512], BF16) for i in range(NPT)]
            rc = P.sb("rc", [128, 4], F32)
            ya = P.sb("ya", [128, 4, 128], BF16)
            yst = [P.sb("yst%d" % i, [128, 512], BF16) for i in range(2)]
            NPS = 3
            psc = [P.ps("psc%d" % i, [128, 512], F32) for i in range(NPS)]
            po = [P.ps("po%d" % i, [128, 512], F32) for i in range(4)]
            pmisc = P.ps("pmisc", [128, 1024], BF16)
            P.dma(emat[:], emat_d.rearrange("p (n k) -> p n k", k=128), w=["emat"], key="a0")
            P.dma(negm[:], negm_d.rearrange("p (o n) -> p o n", n=32), w=["negm"], key="a1")
            P.v("tensor_copy", r=["kms"], w=["kmb"], out=kmb[:], in_=kms[:])
            for i in range(2):
                P.v("memset", w=[("Va", i)], ap=Va[i][:, :, 128:130], constant=1.0)

            def load_head(h):
                P.dma(KTs[h % 2][:], KT[h], w=[("KTs", h % 2)], key="KTs%d" % (h % 2))
                for q4 in range(4):
                    t0, t1 = q4 * NT // 4, (q4 + 1) * NT // 4
                    P.dma(Va[h % 2][:, t0:t1, 0:128],
                          Vd[t0 * 128:t1 * 128, h * 128:(h + 1) * 128].rearrange("(t p) c -> p t c", p=128),
                          w=[("Va", h % 2)], key="Va%d" % (h % 2))
                P.dma(b2s[:], b2[h], w=["b2s"], key="b2s")
                P.act(EB[h % 2][:], b2s[:], AF.Exp, r=["b2s", "zcol"], w=[("EB", h % 2)], bias=zcol[:, 0:1])

            def load_qg(h, j, par):
                P.dma(QTg[par][:], QT[h, :, j * 512:(j + 1) * 512], w=[("QTg", par)], key="QTg%d" % par)
                P.dma(SGg[par][:], SGd[j * 512:(j + 1) * 512, h * 128:(h + 1) * 128].rearrange("(t p) c -> p t c", p=128),
                      w=[("SGg", par)], key="SGg%d" % par)

            it = 0
            kctr = 0
            load_head(0)
            load_qg(0, 0, 0)
            for h in range(4):
                if h + 1 < 4:
                    load_head(h + 1)
                kts, va, eb = KTs[h % 2], Va[h % 2], EB[h % 2]
                for j in range(NG):
                    par = it % 2
                    it += 1
                    if j + 1 < NG:
                        load_qg(h, j + 1, it % 2)
                    elif h + 1 < 4:
                        load_qg(h + 1, 0, it % 2)
                    qg, sg, mt = QTg[par], SGg[par], MT[par]
                    for t in range(4):
                        own = 2 * j + t // 2
                        P.mm(po[t][:, 256:288],
                             qg[:, t * 128:(t + 1) * 128], kmb[:, h, :], r=[("QTg", par), "kmb"], w=[("po", t)])
                        P.v("tensor_tensor", r=[("po", t), "negm"], w=[("gm", t)], out=gm[:, t, :], in0=po[t][:, 256:288],
                            in1=negm[:, own, :], op=ALU.add)
                        P.v("max", r=[("gm", t)], w=[("top8", t)], out=top8[:, t, :], in_=gm[:, t, :])
                        P.v("tensor_scalar", r=[("gm", t), ("top8", t)], w=[("Mb", t)], out=Mb[:, t, :], in0=gm[:, t, :],
                            scalar1=top8[:, t, 2:3], scalar2=-30000.0, op0=ALU.is_lt, op1=ALU.mult)
                        P.v("memset", r=[], w=[("Mb", t)], ap=Mb[:, t, own:own + 1], constant=0.0)
                        P.tr(pmisc[0:32, t * 128:(t + 1) * 128], Mb[:, t, :], identb[:], r=[("Mb", t), "identb"], w=["pmisc"])
                    P.evac(mt[:], pmisc[0:32, 0:512], r=["pmisc"], w=[("MT", par)])
                    for kt in range(4 * j + 4):
                        n = kt // 2
                        qlo = max(0, 128 * (kt - 4 * j))
                        rel = 512 * j - 128 * kt
                        near = rel <= 1536
                        kctr += 1
                        psi = kctr % NPS
                        pti = kctr % NPT
                        ps_, pt_ = psc[psi], PT[pti]
                        P.mm(ps_[:, qlo:512], kts[:, kt * 128:(kt + 1) * 128], qg[:, qlo:512], start=True, stop=False,
                             r=[("KTs", h % 2), ("QTg", par)], w=[("psc", psi)])
                        P.mm(ps_[:, qlo:512], emat[:, n, :], mt[:, qlo:512], start=False, stop=True,
                             r=["emat", ("MT", par)], w=[("psc", psi)])
                        if near:
                            P.act(pt_[:, qlo:512], ps_[:, qlo:512], AF.Exp, r=[("psc", psi), "zcol"], w=[("PT", pti)], scale=SCALE,
                                  bias=zcol[:, 0:1])
                            o0 = rel + 384 + qlo
                            P.v("tensor_tensor", r=[("PT", pti), ("EB", h % 2)], w=[("PT", pti)], out=pt_[:, qlo:512],
                                in0=pt_[:, qlo:512], in1=eb[:, o0:o0 + 512 - qlo], op=ALU.mult)
                        else:
                            P.act(pt_[:, qlo:512], ps_[:, qlo:512], AF.Exp, r=[("psc", psi), "bfar"], w=[("PT", pti)],
                                  scale=SCALE, bias=bfar_t[:, h:h + 1])
                        for t in range(qlo // 128, 4):
                            P.mm(po[t][:, 0:129], pt_[:, t * 128:(t + 1) * 128], va[:, kt, 0:129], start=(kt == 0),
                                 stop=(kt == 4 * j + t), r=[("PT", pti), ("Va", h % 2)], w=[("po", t)])
                        t = kt - 4 * j
                        if t >= 0:
                            P.v("reciprocal", r=[("po", t)], w=[("rc", t)], out=rc[:, t:t + 1], in_=po[t][:, 128:129])
                            P.v("scalar_tensor_tensor", r=[("po", t), ("rc", t), ("SGg", par)], w=[("ya", t)], out=ya[:, t, :],
                                in0=po[t][:, 0:128], scalar=rc[:, t:t + 1], in1=sg[:, t, :], op0=ALU.mult, op1=ALU.mult)
                            P.tr(pmisc[:, 512 + t * 128:512 + (t + 1) * 128], ya[:, t, :], identb[:], r=[("ya", t), "identb"],
                                 w=["pmisc2"])
                    ys = yst[par]
                    P.evac(ys[:], pmisc[:, 512:1024], r=["pmisc2"], w=[("yst", par)])
                    P.dma(yt_slice(h * 128, j), ys[:], r=[("yst", par)], w=[("YT", h, j)],
                          key="yst%d" % par, q=STQ)
            P.pop()

        if upto >= 3:
            P.push()
            PIECE = min(S, 2048)
            NPC = S // PIECE
            CPP = PIECE // 128
            sel = P.sb("sel", [2, 2, 128], F32)
            gi = P.sb("gi", [2, PIECE], F32)
            gf = P.sb("gf", [2, PIECE], F32)
            cs = P.sb("cs", [2, PIECE], F32)
            aa = P.sb("aa", [2, PIECE], F32)
            gg = P.sb("gg", [2, PIECE], F32)
            zer = P.sb("zer", [2, PIECE], F32)
            uu = P.sb("uu", [2, PIECE], F32)
            ph = P.sb("ph", [2, PIECE], F32)
            carry = P.sb("carry", [2, 2], F32)
            gends = P.sb("gends", [2, NCH + 1], F32)
            brow = P.sb("brow", [2, NCH], F32)
            UT = P.sb("UT", [128, NCH, 2, 2], F32)
            betab = P.sb("betab", [128, 2, NCH], F32)
            pu = P.ps("pu", [128, 512], F32)
            P.dma(sel[:], sel_d.rearrange("p (h k) -> p h k", k=128), w=["sel"], key="m0")
            P.v("memset", w=["zer"], ap=zer[:], constant=0.0)
            P.v("memset", w=["carry"], ap=carry[:], constant=0.0)
            P.v("memset", w=["gends"], ap=gends[:], constant=0.0)
            for pc in range(NPC):
                psl = slice(pc * PIECE, (pc + 1) * PIECE)
                P.dma(gi[:], GId[:, psl], w=["gi"], key="m1")
                P.dma(gf[:], GFd[:, psl], w=["gf"], key="m2")
                P.act(gf[:], gf[:], AF.Exp, r=["gf", "zcol"], w=["gf"], scale=-1.0, bias=zcol[0:2, 0:1])
                P.act(gf[:], gf[:], AF.Ln, r=["gf", "zcol"], w=["gf"], bias=zcol[0:2, 1:2])
                P.v("tensor_tensor_scan", r=["gf", "zer", "carry"], w=["cs"], out=cs[:], data0=gf[:], data1=zer[:],
                    initial=carry[:, 0:1], op0=ALU.add, op1=ALU.add)
                P.v("tensor_tensor", r=["gi", "cs"], w=["aa"], out=aa[:], in0=gi[:], in1=cs[:], op=ALU.add)
                P.v("tensor_tensor_scan", r=["aa", "zer", "carry"], w=["gg"], out=gg[:], data0=aa[:], data1=zer[:],
                    initial=carry[:, 1:2], op0=ALU.max, op1=ALU.max)
                P.v("tensor_copy", r=["cs"], w=["carry"], out=carry[:, 0:1], in_=cs[:, PIECE - 1:PIECE])
                P.v("tensor_copy", r=["gg"], w=["carry"], out=carry[:, 1:2], in_=gg[:, PIECE - 1:PIECE])
                P.v("tensor_copy", r=["gg"], w=["gends"], out=gends[:, 1 + pc * CPP:1 + (pc + 1) * CPP],
                    in_=gg[:].rearrange("p (c k) -> p c k", k=128)[:, :, 127])
                for c in range(CPP):
                    cg = pc * CPP + c
                    csl = slice(c * 128, (c + 1) * 128)
                    P.v("tensor_scalar", r=["aa", "gends"], w=["uu"], out=uu[:, csl], in0=aa[:, csl],
                        scalar1=gends[:, cg + 1:cg + 2], scalar2=None, op0=ALU.subtract)
                    P.v("tensor_scalar", r=["cs", "gends"], w=["ph"], out=ph[:, csl], in0=cs[:, csl],
                        scalar1=gends[:, cg + 1:cg + 2], scalar2=None, op0=ALU.subtract)
                P.act(uu[:], uu[:], AF.Exp, r=["uu", "zcol"], w=["uu"], bias=zcol[0:2, 0:1])
                P.act(ph[:], ph[:], AF.Exp, r=["ph", "zcol"], w=["ph"], bias=zcol[0:2, 0:1])
                for c in range(CPP):
                    csl = slice(c * 128, (c + 1) * 128)
                    P.tr(pu[:, c * 4:c * 4 + 2], uu[:, csl], identf[0:2, 0:2], r=["uu", "identf"], w=["pu"])
                    P.tr(pu[:, c * 4 + 2:c * 4 + 4], ph[:, csl], identf[0:2, 0:2], r=["ph", "identf"], w=["pu"])
                P.v("tensor_copy", r=["pu"], w=["UT"], out=UT[:, pc * CPP:(pc + 1) * CPP, :, :],
                    in_=pu[:, 0:CPP * 4].rearrange("p (c a b) -> p c a b", a=2, b=2))
            P.v("tensor_tensor", r=["gends"], w=["brow"], out=brow[:], in0=gends[:, 0:NCH], in1=gends[:, 1:NCH + 1],
                op=ALU.subtract)
            P.act(brow[:], brow[:], AF.Exp, r=["brow", "zcol"], w=["brow"], bias=zcol[0:2, 0:1])
            for hl in range(2):
                P.mm(pu[:, 0:NCH], sel[:, hl, :], brow[:], r=["sel", "brow", "UT"], w=["pu"])
                P.v("tensor_copy", r=["pu"], w=["betab"], out=betab[:, hl, :], in_=pu[:, 0:NCH])

            qk = [P.sb("qk%d" % i, [128, 4, 512], BF16) for i in range(2)]
            xz = [P.sb("xz%d" % i, [128, 4, 512], BF16) for i in range(2)]
            kv = [P.sb("kv%d" % i, [128, 4, 514], BF16) for i in range(2)]
            SM = [P.sb("SM%d" % i, [128, 128], BF16) for i in range(2)]
            Vp = [P.sb("Vp%d" % i, [128, 258], BF16) for i in range(2)]
            Dst = P.sb("Dst", [128, 2, 257], F32)
            Cb = [P.sb("Cb%d" % i, [128, 2, 258], BF16) for i in range(2)]
            den = P.sb("den", [128, 2], F32)
            hsc = P.sb("hsc", [128, 256], F32)
            bst = P.sb("bst", [128, 6], F32)
            mv = P.sb("mv", [128, 2], F32)
            rstd = P.sb("rstd", [128, 2], F32)
            hn = P.sb("hn", [128, 256], BF16)
            tmp = P.sb("tmp", [128, 2, 128], F32)
            ymst = [P.sb("ymst%d" % i, [128, 2, 512], BF16) for i in range(2)]
            pst = P.ps("pst", [128, 512], F32)
            pn = [P.ps("pn%d" % i, [128, 512], F32) for i in range(2)]
            pdc = [P.ps("pdc%d" % i, [128, 512], F32) for i in range(2)]
            pht = P.ps("pht", [128, 1024], BF16)
            for i in range(2):
                P.v("memset", w=[("kv", i)], ap=kv[i][:, :, 512:514], constant=1.0)

            def load_m(hl, g, par):
                tsl = slice(g * 512, (g + 1) * 512)
                for i in range(2):
                    ti = 2 * hl + i
                    P.dma(qk[par][:, i, :], QmT[ti, :, tsl], w=[("qk", par)], key="qk%d" % par)
                    P.dma(qk[par][:, 2 + i, :], KmT[ti, :, tsl], w=[("qk", par)], key="qk%d" % par)
                    P.dma(xz[par][:, i, :], XSd[ti, :, tsl], w=[("xz", par)], key="xz%d" % par)
                    P.dma(xz[par][:, 2 + i, :], SZd[ti, :, tsl], w=[("xz", par)], key="xz%d" % par)
                P.dma(kv[par][:, :, 0:256], Kmd[tsl, hl * 256:(hl + 1) * 256].rearrange("(c p) d -> p c d", p=128),
                      w=[("kv", par)], key="kv%d" % par)
                P.dma(kv[par][:, :, 256:512], Vmd[tsl, hl * 256:(hl + 1) * 256].rearrange("(c p) d -> p c d", p=128),
                      w=[("kv", par)], key="kv%d" % par)

            it = 0
            load_m(0, 0, 0)
            for hl in range(2):
                for g in range(NG):
                    par = it % 2
                    it += 1
                    if g + 1 < NG:
                        load_m(hl, g + 1, it % 2)
                    elif hl + 1 < 2:
                        load_m(hl + 1, 0, it % 2)
                    q_k, x_z, k_v, ym = qk[par], xz[par], kv[par], ymst[par]
                    for cc in range(4):
                        c = g * 4 + cc
                        csl = slice(cc * 128, (cc + 1) * 128)
                        sm, vp, cbf = SM[c % 2], Vp[c % 2], Cb[c % 2]
                        pnn, pnt = pn[c % 2], ("pn", c % 2)
                        if c > 0:
                            for dc in range(2):
                                P.act(cbf[:, dc, 0:257], Dst[:, dc, :], AF.Copy, r=[("Dst", dc), "betab"], w=[("Cb", c % 2, dc)],
                                      scale=betab[:, hl, c:c + 1])
                        for i in range(2):
                            P.mm(pst[:, 0:128], q_k[:, 2 + i, csl], q_k[:, i, csl], start=(i == 0), stop=(i == 1),
                                 r=[("qk", par)], w=["pst"])
                        P.v("tensor_tensor", r=["pst", "causb"], w=[("SM", c % 2)], out=sm[:], in0=pst[:, 0:128], in1=causb[:],
                            op=ALU.mult)
                        P.act(vp[:, 0:257], k_v[:, cc, 256:513], AF.Copy, r=[("kv", par), "UT"], w=[("Vp", c % 2)],
                              scale=UT[:, c, 0, hl:hl + 1])
                        P.mm(pnn[:, 0:257], sm[:], vp[:, 0:257], start=True, stop=(c == 0), r=[("SM", c % 2), ("Vp", c % 2)],
                             w=[pnt])
                        if c > 0:
                            for dc in range(2):
                                P.mm(pnn[:, 0:257], q_k[:, dc, csl], cbf[:, dc, 0:257], start=False, stop=(dc == 1),
                                     r=[("qk", par), ("Cb", c % 2, dc)], w=[pnt])
                        for dc in range(2):
                            P.mm(pdc[dc][:, 0:257], k_v[:, cc, dc * 128:(dc + 1) * 128], vp[:, 0:257],
                                 r=[("kv", par), ("Vp", c % 2)], w=[("pdc", dc)])
                            if c == 0:
                                P.v("tensor_copy", r=[("pdc", dc)], w=[("Dst", dc)], out=Dst[:, dc, :], in_=pdc[dc][:, 0:257])
                            else:
                                P.v("scalar_tensor_tensor", r=[("pdc", dc), ("Dst", dc), "betab"], w=[("Dst", dc)],
                                    out=Dst[:, dc, :], in0=Dst[:, dc, :], scalar=betab[:, hl, c:c + 1], in1=pdc[dc][:, 0:257],
                                    op0=ALU.mult, op1=ALU.add)
                        P.act(den[:, 0:1], pnn[:, 256:257], AF.Abs, r=[pnt, "zcol"], w=["den"], bias=zcol[:, 0:1])
                        P.v("tensor_tensor", r=["den", "UT"], w=["den"], out=den[:, 0:1], in0=den[:, 0:1],
                            in1=UT[:, c, 1, hl:hl + 1], op=ALU.max)
                        P.v("reciprocal", r=["den"], w=["den"], out=den[:, 1:2], in_=den[:, 0:1])
                        P.act(hsc[:], pnn[:, 0:256], AF.Copy, r=[pnt, "den"], w=["hsc"], scale=den[:, 1:2])
                        P.v("bn_stats", r=["hsc"], w=["bst"], out=bst[:], in_=hsc[:])
                        P.v("bn_aggr", r=["bst"], w=["mv"], out=mv[:], in_=bst[:])
                        P.act(rstd[:, 0:1], mv[:, 1:2], AF.Sqrt, r=["mv"], w=["rstd"], bias=1e-5)
                        P.v("reciprocal", r=["rstd"], w=["rstd"], out=rstd[:, 1:2], in_=rstd[:, 0:1])
                        P.v("tensor_scalar", r=["hsc", "mv", "rstd"], w=["hn"], out=hn[:], in0=hsc[:], scalar1=mv[:, 0:1],
                            scalar2=rstd[:, 1:2], op0=ALU.subtract, op1=ALU.mult)
                        for i in range(2):
                            P.tr(pht[:, i * 128:(i + 1) * 128], hn[:, i * 128:(i + 1) * 128], identb[:], r=["hn", "identb"],
                                 w=["pht"])
                        for i in range(2):
                            ti = 2 * hl + i
                            P.v("scalar_tensor_tensor", r=["pht", "mhn", ("xz", par)], w=[("tmp", i)], out=tmp[:, i, :],
                                in0=pht[:, i * 128:(i + 1) * 128], scalar=mhn_t[:, ti:ti + 1], in1=x_z[:, i, csl],
                                op0=ALU.mult, op1=ALU.add)
                            P.v("tensor_tensor", r=[("tmp", i), ("xz", par)], w=[("ymst", par)], eng="pool", out=ym[:, i, csl],
                                in0=tmp[:, i, :], in1=x_z[:, 2 + i, csl], op=ALU.mult)
                    for i in range(2):
                        ti = 2 * hl + i
                        P.dma(yt_slice(512 + ti * 128, g), ym[:, i, :],
                              r=[("ymst", par)], w=[("YTm", ti, g)], key="ymst%d" % par, q=STQ)
            P.pop()

        if upto >= 4:
            P.push()
            rg = [[0, 1], [2, 3], [4, 5], [6, 7]]
            for blk in range(NB):
                P.add("pool", (lambda e, blk=blk: e.collective_compute("AllGather", ALU.bypass, replica_groups=rg,
                                                                       ins=[YTsrc[blk].opt()], outs=[YTall[blk].opt()])),
                      w=[("YTall", blk)], dma_key=P.mapkey("cc"), inc=1)
            Wo = P.sb("Wo", [128, 16, D], BF16)
            wos = [P.sb("wos%d" % i, [128, D], F32) for i in range(2)]
            ytl = [P.sb("ytl%d" % i, [128, 16, 512], BF16) for i in range(2)]
            xr = [P.sb("xr%d" % i, [128, 4, D], F32) for i in range(2)]
            ot = [P.sb("ot%d" % i, [128, 4, D], F32) for i in range(2)]
            junk3 = P.sb("junk3", [128, 512], BF16)
            s2 = P.sb("s2", [128, 4], F32)
            py = [[P.ps("py%d_%d" % (i, k), [128, 512], F32) for k in range(2)] for i in range(2)]
            for ck in range(16):
                P.dma(wos[ck % 2][:], w_out[ck * 128:(ck + 1) * 128, :], w=[("wos", ck % 2)], key="wos%d" % (ck % 2))
                P.v("tensor_copy", r=[("wos", ck % 2)], w=[("Wo", ck)], eng=("dve" if ck % 2 else "pool"),
                    out=Wo[:, ck, :], in_=wos[ck % 2][:])
            wo_tok = [("Wo", ck) for ck in range(16)]

            def load3(g):
                blk, off = (g * 512) // TB, (g * 512) % TB
                P.dma(ytl[g % 2][:], YTall[blk, :, off:off + 512].rearrange("(k p) t -> p k t", p=128),
                      r=[("YTall", blk)], w=[("ytl", g % 2)], key="ytl%d" % (g % 2))
                P.dma(xr[g % 2][:], x[g * 512:(g + 1) * 512, :].rearrange("(t p) c -> p t c", p=128),
                      w=[("xr", g % 2)], key="xr%d" % (g % 2))

            load3(0)
            n3 = 0
            for g in range(NG):
                if g + 1 < NG:
                    load3(g + 1)
                yt_, xr_, ot_ = ytl[g % 2], xr[g % 2], ot[g % 2]
                for t in range(4):
                    pp = py[n3 % 2]
                    ptk = [("py", n3 % 2, 0), ("py", n3 % 2, 1)]
                    n3 += 1
                    for hf in range(2):
                        for ck in range(16):
                            P.mm(pp[hf][:], yt_[:, ck, t * 128:(t + 1) * 128], Wo[:, ck, hf * 512:(hf + 1) * 512],
                                 start=(ck == 0), stop=(ck == 15), r=wo_tok + [("ytl", g % 2)] if ck in (0, 15) else (), w=[ptk[hf]])
                        P.act(junk3[:], pp[hf][:], AF.Square, r=[ptk[hf]], w=["junk3", ("s2", hf)], accum_out=s2[:, hf:hf + 1])
                    P.v("tensor_tensor", r=[("s2", 0), ("s2", 1)], w=[("s2", 2)], out=s2[:, 2:3], in0=s2[:, 0:1], in1=s2[:, 1:2],
                        op=ALU.add)
                    P.act(s2[:, 3:4], s2[:, 2:3], AF.Sqrt, r=[("s2", 2)], w=[("s2", 3)], scale=1.0 / D, bias=1e-6)
                    P.v("reciprocal", r=[("s2", 3)], w=[("s2", 3)], out=s2[:, 3:4], in_=s2[:, 3:4])
                    for hf in range(2):
                        hs = slice(hf * 512, (hf + 1) * 512)
                        P.v("scalar_tensor_tensor", r=[ptk[hf], ("s2", 3), "gpost_b"], w=[("ot", g % 2, t, hf)],
                            out=ot_[:, t, hs], in0=pp[hf][:], scalar=s2[:, 3:4], in1=gpost_b[:, hs], op0=ALU.mult, op1=ALU.mult)
                        P.v("tensor_tensor", r=[("ot", g % 2, t, hf), ("xr", g % 2)], w=[("ot", g % 2, t, hf)], eng="pool",
                            out=ot_[:, t, hs], in0=ot_[:, t, hs], in1=xr_[:, t, hs], op=ALU.add)
                P.dma(out[g * 512:(g + 1) * 512, :].rearrange("(t p) c -> p t c", p=128), ot_[:],
                      r=[("ot", g % 2, t, hf) for t in range(4) for hf in range(2)], w=[("out", g)], key="ot%d" % (g % 2))
            P.pop()
        P.emit()
    return nc, P


def _t5_bucket_np(dist):
    n = np.maximum(dist, 0)
    max_exact = 16
    nf = np.maximum(n, 1).astype(np.float32)
    large = max_exact + (np.log(nf / np.float32(max_exact)) / np.float32(math.log(2048 / max_exact))
                         * np.float32(32 - max_exact)).astype(np.int32)
    large = np.minimum(large, 31)
    return np.where(n < max_exact, n, large)


_CONSTS = {}


def _consts():
    if _CONSTS:
        return _CONSTS
    bf = ml_dtypes.bfloat16
    _CONSTS["identb"] = np.eye(128, dtype=np.float32).astype(bf)
    _CONSTS["identf"] = np.eye(128, dtype=np.float32)
    s = np.arange(128)
    _CONSTS["causb"] = (s[:, None] <= s[None, :]).astype(np.float32).astype(bf)
    e = np.zeros((32, 32, 128), np.float32)
    for n in range(32):
        e[n, n, :] = 1.0
    _CONSTS["emat"] = e.reshape(32, 32 * 128).astype(bf)
    own = np.arange(32)[:, None]
    nn = np.arange(32)[None, :]
    ng = np.where(nn < own, 0.0, -1e30).astype(np.float32)
    _CONSTS["negm"] = np.ascontiguousarray(np.broadcast_to(ng[None], (128, 32, 32))).reshape(128, 1024)
    sel = np.zeros((2, 2, 128), np.float32)
    sel[0, 0, :] = 1.0
    sel[1, 1, :] = 1.0
    _CONSTS["sel"] = sel.reshape(2, 256)
    k = np.arange(128)[:, None]
    xx = np.arange(XW)[None, :]
    dist = xx - k - 384
    _CONSTS["bucket"] = _t5_bucket_np(dist)
    _CONSTS["neg"] = dist < 0
    return _CONSTS


def core_inputs(inp, b, hh, S):
    C = _consts()
    f32 = np.float32
    w_in = np.asarray(inp["w_in"][0], f32)
    heads = [4 * hh + i for i in range(4)]
    ch_order = np.concatenate([np.arange(512 * hh, 512 * hh + 512), np.arange(512 * (1 - hh), 512 * (1 - hh) + 512)])

    def hcols(base):
        return np.concatenate([w_in[:, base + 128 * h: base + 128 * h + 128] for h in heads], axis=1)

    w_in_c = np.concatenate([hcols(0), hcols(1024), w_in[:, 4096 + ch_order], w_in[:, 5120 + ch_order[:512]],
                             hcols(2048), hcols(3072)], axis=1)
    d = {"x": np.ascontiguousarray(np.asarray(inp["x"][b, :S], f32)),
         "gpre": np.asarray(inp["g_pre"], f32).reshape(1, D),
         "gpost": np.asarray(inp["g_post"], f32).reshape(1, D),
         "w_in": np.ascontiguousarray(w_in_c)}
    w_out = np.asarray(inp["w_out"][0], f32)
    rows = []
    for r in range(2):
        rows.append(w_out[512 * r:512 * r + 512])
        rows.append(w_out[1024 + 512 * r:1024 + 512 * r + 512])
    d["w_out"] = np.ascontiguousarray(np.concatenate(rows, axis=0))
    rel = np.asarray(inp["rel_bias"], f32)
    b2 = np.empty((4, 128, XW), f32)
    for i, h in enumerate(heads):
        t = rel[:, h][C["bucket"]]
        t = np.where(C["neg"], np.float32(-30000.0), t)
        b2[i] = t
    d["b2"] = b2
    d["bfar"] = np.ascontiguousarray(np.broadcast_to(rel[31, heads][None, :], (128, 4))).astype(f32)
    conv_w = np.asarray(inp["conv_w"][0], f32)[:, ch_order]
    d["cw"] = np.ascontiguousarray(conv_w.reshape(4, 8, 128).transpose(2, 1, 0)).reshape(128, 32)
    d["cb"] = np.ascontiguousarray(np.asarray(inp["conv_b"][0], f32)[ch_order].reshape(8, 128).T)
    blk_order = ch_order.reshape(256, 4)[:, 0] // 4
    bd = np.zeros((3, 8, 128, 128), f32)
    for wi, nm in enumerate(("wq_m", "wk_m", "wv_m")):
        w = np.asarray(inp[nm][0], f32)[blk_order]
        for ti in range(8):
            for nl in range(32):
                bd[wi, ti, 4 * nl:4 * nl + 4, 4 * nl:4 * nl + 4] = w[32 * ti + nl]
    d["bd"] = np.ascontiguousarray(bd[:, 0:4].reshape(12, 128, 128).transpose(1, 0, 2)).reshape(128, 12 * 128)
    d["bdT"] = np.ascontiguousarray(bd.transpose(0, 1, 3, 2).reshape(24, 128, 128).transpose(1, 0, 2)).reshape(128, 24 * 128)
    w_if = np.asarray(inp["w_if"][0], f32)
    gcols = [2 * hh, 2 * hh + 1, 4 + 2 * hh, 4 + 2 * hh + 1]
    wif = np.stack([w_if[1024 * wi + ch_order][:, gcols] for wi in range(3)], axis=0)
    d["wif"] = np.ascontiguousarray(wif.reshape(3, 8, 128, 4).transpose(2, 0, 1, 3)).reshape(128, 96)
    b_if = np.asarray(inp["b_if"][0], f32)
    d["bg"] = np.ascontiguousarray(np.stack([b_if[gcols[0:2]], b_if[gcols[2:4]]], axis=1))
    d["mhn"] = np.ascontiguousarray(np.asarray(inp["mh_norm"][0], f32)[ch_order[:512]].reshape(4, 128).T)
    d["skp"] = np.ascontiguousarray(np.asarray(inp["skip"][0], f32)[ch_order[:512]].reshape(4, 128).T)
    for k in ("identb", "identf", "causb", "emat", "negm", "sel"):
        d[k] = C[k]
    return d


_PROG_CACHE = {}


def run(inputs, S, dbg=()):
    key = (S, tuple(dbg))
    if key not in _PROG_CACHE:
        _PROG_CACHE[key] = build_program(S, dbg)[0]
    nc = _PROG_CACHE[key]
    in_maps = [core_inputs(inputs, c // 2, c % 2, S) for c in range(8)]
    res = run_bass_kernel_spmd(nc, in_maps, core_ids=list(range(8)))
    return res.results


def kernel(**inputs):
    S = inputs["x"].shape[1]
    res = run(inputs, S)
    outp = np.empty((4, S, D), np.float32)
    for b in range(4):
        outp[b, :S // 2] = res[2 * b]["out"][:S // 2]
        outp[b, S // 2:] = res[2 * b + 1]["out"][S // 2:]
    return outp
```

```python
import contextlib
import math
import numpy as np
import ml_dtypes
import concourse.bass as bass
import concourse.mybir as mybir
from concourse.bass_utils import run_bass_kernel_spmd

F32 = mybir.dt.float32
BF16 = mybir.dt.bfloat16
AF = mybir.ActivationFunctionType
ALU = mybir.AluOpType
AX = mybir.AxisListType

D = 1024
NCOL = 3584
XW = 2432
SCALE = 128.0 ** -0.5
ENGS = ["pe", "act", "dve", "pool", "sp"]
STQ = "sp"
import os
CUT = int(os.environ.get("K_CUT", "0"))


class Op:
    __slots__ = ("eng", "fn", "deps", "dma_key", "signal", "event", "inc")

    def __init__(self, eng, fn, deps, dma_key, inc):
        self.eng = eng
        self.fn = fn
        self.deps = deps
        self.dma_key = dma_key
        self.signal = False
        self.event = None
        self.inc = inc


class Prog:
    def __init__(self, nc):
        self.nc = nc
        self.ops = []
        self.last_w = {}
        self.readers = {}
        self.root = contextlib.ExitStack()
        self.scopes = [self.root]
        self.last_eng = {}
        self.last_key = {}
        self.rr = 0
        self.keymap = {}

    def sb(self, name, shape, dt):
        return self.scopes[-1].enter_context(self.nc.sbuf_tensor("s_" + name, list(shape), dt))

    def ps(self, name, shape, dt=F32):
        return self.scopes[-1].enter_context(self.nc.psum_tensor("p_" + name, list(shape), dt))

    def push(self):
        st = contextlib.ExitStack()
        self.scopes.append(st)
        return st

    def pop(self):
        self.barrier()
        self.scopes.pop().close()

    def add(self, eng, fn, r=(), w=(), dma_key=None, inc=16, deps=None):
        if deps is None:
            deps = set()
            for t in r:
                x = self.last_w.get(t)
                if x is not None:
                    deps.add(x)
            for t in w:
                x = self.last_w.get(t)
                if x is not None:
                    deps.add(x)
                for y in self.readers.get(t, ()):
                    deps.add(y)
            if eng == "pe":
                deps = {d for d in deps if not (self.ops[d].eng == "pe" and self.ops[d].dma_key is None)}
            best = {}
            keep = set()
            for d in deps:
                o = self.ops[d]
                if o.dma_key is not None:
                    keep.add(d)
                elif best.get(o.eng, -1) < d:
                    best[o.eng] = d
            deps = keep | set(best.values())
        idx = len(self.ops)
        self.ops.append(Op(eng, fn, sorted(deps), dma_key, inc))
        for t in r:
            lst = self.readers.setdefault(t, [])
            if dma_key is None:
                lst[:] = [y for y in lst if not (self.ops[y].eng == eng and self.ops[y].dma_key is None)]
            lst.append(idx)
        for t in w:
            self.last_w[t] = idx
            self.readers[t] = []
        if dma_key is None:
            self.last_eng[eng] = idx
        else:
            self.last_key[dma_key] = idx
        return idx

    def barrier(self):
        deps = set(self.last_eng.values()) | set(self.last_key.values())
        for e in ENGS:
            self.add(e, lambda h: h.nop(), deps=set(deps))
        self.last_w = {}
        self.readers = {}
        self.keymap = {}

    def dma(self, out, in_, r=(), w=(), key=None, q="sp"):
        return self.add(q, lambda e: e.dma_start(out=out, in_=in_), r, w, dma_key=self.mapkey(key))

    def mapkey(self, key):
        if key not in self.keymap:
            self.keymap[key] = "d%d" % len(self.keymap)
        return self.keymap[key]

    def mm(self, out, lhsT, rhs, start=True, stop=True, r=(), w=()):
        return self.add("pe", lambda e: e.matmul(out, lhsT, rhs, start=start, stop=stop), r, w)

    def tr(self, out, in_, ident, r=(), w=()):
        return self.add("pe", lambda e: e.transpose(out, in_, ident), r, w)

    def act(self, out, in_, func, r=(), w=(), **kw):
        return self.add("act", lambda e: e.activation(out, in_, func, **kw), r, w)

    def v(self, name, r=(), w=(), eng="dve", **kw):
        return self.add(eng, lambda e: getattr(e, name)(**kw), r, w)

    def evac(self, out, in_, r=(), w=(), scale=None, eng=None):
        if eng is None:
            self.rr += 1
            eng = "act" if self.rr % 2 else "dve"
        if eng == "act":
            if scale is None:
                return self.act(out, in_, AF.Copy, r, w)
            return self.act(out, in_, AF.Copy, r, w, scale=float(scale))
        if scale is None:
            return self.v("tensor_copy", r, w, out=out, in_=in_)
        return self.v("tensor_scalar", r, w, out=out, in0=in_, scalar1=float(scale), scalar2=None, op0=ALU.mult)

    def emit(self):
        nc = self.nc
        ops = self.ops
        for o in ops:
            for d in o.deps:
                ops[d].signal = True
        esem = {e: self.root.enter_context(nc.semaphore("es_" + e)) for e in ENGS}
        dsem = {}
        cnt = {e: 0 for e in ENGS}
        dcnt = {}
        for o in ops:
            if o.dma_key is not None:
                if o.dma_key not in dsem:
                    dsem[o.dma_key] = self.root.enter_context(nc.semaphore("ds_%d" % len(dsem)))
                    dcnt[o.dma_key] = 0
                dcnt[o.dma_key] += o.inc
                o.event = (dsem[o.dma_key], dcnt[o.dma_key])
            elif o.signal:
                cnt[o.eng] += 1
                o.event = (esem[o.eng], cnt[o.eng])
        self.nsem = len(dsem) + len(ENGS)
        self.cnt = cnt
        per = {e: [o for o in ops if o.eng == e] for e in ENGS}

        def run(ename, handle):
            seen = {}
            for o in per[ename]:
                need = {}
                for d in o.deps:
                    sem, val = ops[d].event
                    k = id(sem)
                    if need.get(k, (None, 0))[1] < val:
                        need[k] = (sem, val)
                for k, (sem, val) in need.items():
                    if seen.get(k, 0) < val:
                        handle.wait_ge(sem, val)
                        seen[k] = val
                ins = o.fn(handle)
                if o.dma_key is not None:
                    ins.then_inc(o.event[0], o.inc)
                elif o.signal:
                    ins.then_inc(o.event[0], 1)

        with nc.Block() as block:
            @block.tensor
            def _(e):
                run("pe", e)

            @block.scalar
            def _(e):
                run("act", e)

            @block.vector
            def _(e):
                run("dve", e)

            @block.gpsimd
            def _(e):
                run("pool", e)

            @block.sync
            def _(e):
                run("sp", e)


def dram_bcast(ap, nparts, n):
    return bass.AP(ap.tensor, ap.offset, [[0, nparts], [1, n]])


def build_program(S, dbg=(), upto=4):
    NG = S // 512
    NT = S // 128
    NCH = S // 128
    nc = bass.Bass("TRN2", target_bir_lowering=False)

    def din(name, shape, dt=F32):
        return nc.dram_tensor(name, list(shape), dt, kind="ExternalInput").ap()

    def dscr(name, shape, dt=BF16):
        kind = "ExternalOutput" if name in dbg else "Internal"
        return nc.dram_tensor(name, list(shape), dt, kind=kind).ap()

    x = din("x", [S, D])
    gpre = din("gpre", [1, D])
    gpost = din("gpost", [1, D])
    w_in = din("w_in", [D, NCOL])
    w_out = din("w_out", [2 * D, D])
    b2 = din("b2", [4, 128, XW])
    bfar = din("bfar", [128, 4])
    cw = din("cw", [128, 32])
    cb = din("cb", [128, 8])
    bd = din("bd", [128, 3 * 4 * 128])
    bdT = din("bdT", [128, 3 * 8 * 128])
    wif = din("wif", [128, 3 * 8 * 4])
    bg = din("bg", [2, 2])
    mhn = din("mhn", [128, 4])
    skp = din("skp", [128, 4])
    identb_d = din("identb", [128, 128], BF16)
    identf_d = din("identf", [128, 128])
    causb_d = din("causb", [128, 128], BF16)
    emat_d = din("emat", [32, 32 * 128], BF16)
    negm_d = din("negm", [128, 32 * 32])
    sel_d = din("sel", [2, 2 * 128])
    out = nc.dram_tensor("out", [S, D], F32, kind="ExternalOutput").ap()

    QT = dscr("QT", [4, 128, S])
    KT = dscr("KT", [4, 128, S])
    Vd = dscr("Vd", [S, 512])
    SGd = dscr("SGd", [S, 512])
    QmT = dscr("QmT", [4, 128, S])
    KmT = dscr("KmT", [4, 128, S])
    Kmd = dscr("Kmd", [S, 512])
    Vmd = dscr("Vmd", [S, 512])
    XSd = dscr("XSd", [4, 128, S])
    SZd = dscr("SZd", [4, 128, S])
    GId = dscr("GId", [2, S], F32)
    GFd = dscr("GFd", [2, S], F32)
    TB = min(S, 1024)
    NB = S // TB
    YTsrc = nc.dram_tensor("YTsrc", [NB, D, TB], BF16).ap()
    YTall = nc.dram_tensor("YTall", [NB, 2 * D, TB], BF16).ap()

    def yt_slice(row0, g):
        blk, off = (g * 512) // TB, (g * 512) % TB
        return YTsrc[blk, row0:row0 + 128, off:off + 512]
    YTdbg = dscr("YTdbg", [D, S]) if "YTdbg" in dbg else None

    P = Prog(nc)
    with P.root:
        identb = P.sb("identb", [128, 128], BF16)
        identf = P.sb("identf", [128, 128], F32)
        causb = P.sb("causb", [128, 128], BF16)
        gpre_b = P.sb("gpre_b", [128, D], F32)
        gpost_b = P.sb("gpost_b", [128, D], F32)
        kms = P.sb("kms", [128, 4, 32], F32)
        bfar_t = P.sb("bfar_t", [128, 4], F32)
        mhn_t = P.sb("mhn_t", [128, 4], F32)
        skp_t = P.sb("skp_t", [128, 4], F32)
        cb_t = P.sb("cb_t", [128, 8], F32)
        bg_t = P.sb("bg_t", [2, 2], F32)
        P.dma(identb[:], identb_d, w=["identb"], key="c0")
        P.dma(identf[:], identf_d, w=["identf"], key="c1")
        P.dma(causb[:], causb_d, w=["causb"], key="c2")
        P.dma(gpre_b[:], dram_bcast(gpre, 128, D), w=["gpre_b"], key="c3")
        P.dma(gpost_b[:], dram_bcast(gpost, 128, D), w=["gpost_b"], key="c4")
        P.dma(bfar_t[:], bfar, w=["bfar"], key="c5")
        P.dma(mhn_t[:], mhn, w=["mhn"], key="c6")
        P.dma(skp_t[:], skp, w=["skp"], key="c7")
        P.dma(cb_t[:], cb, w=["cb"], key="c8")
        P.dma(bg_t[:], bg, w=["bg"], key="c9")
        P.v("memset", w=["kms"], ap=kms[:], constant=0.0)
        zcol = P.sb("zcol", [128, 2], F32)
        P.v("memset", w=["zcol"], ap=zcol[:, 0:1], constant=0.0)
        P.v("memset", w=["zcol"], ap=zcol[:, 1:2], constant=1.0)
        P.barrier()

        if upto >= 1:
            P.push()
            Wb = P.sb("Wb", [128, 8, NCOL], BF16)
            Dg = P.sb("Dg", [128, 8, 4, 128], BF16)
            bd_b = P.sb("bd_b", [128, 12, 128], BF16)
            Wg = P.sb("Wg", [128, 8, 2, 4], BF16)
            pm = [P.ps("pm%d" % i, [128, 512], F32) for i in range(4)]
            P.push()
            wst = [P.sb("wst%d" % i, [128, NCOL], F32) for i in range(2)]
            cw_t = P.sb("cw_t", [128, 8, 4], F32)
            bd_f = P.sb("bd_f", [128, 12, 128], F32)
            bdT_f = P.sb("bdT_f", [128, 24, 128], F32)
            wif_t = P.sb("wif_t", [128, 24, 4], F32)
            P.dma(cw_t[:], cw.rearrange("p (t j) -> p t j", j=4), w=["cw"], key="c10")
            P.dma(bd_f[:], bd.rearrange("p (k c) -> p k c", c=128), w=["bd_f"], key="c11")
            P.dma(bdT_f[:], bdT.rearrange("p (k c) -> p k c", c=128), w=["bdT_f"], key="c12")
            P.dma(wif_t[:], wif.rearrange("p (k c) -> p k c", c=4), w=["wif"], key="c13")
            for c in range(8):
                P.dma(wst[c % 2][:], w_in[c * 128:(c + 1) * 128, :], w=[("wst", c % 2)], key="wst%d" % (c % 2))
                third = NCOL // 4
                P.v("tensor_copy", r=[("wst", c % 2)], w=[("Wb", c, 0)], out=Wb[:, c, 0:2 * third], in_=wst[c % 2][:, 0:2 * third])
                P.v("tensor_copy", r=[("wst", c % 2)], w=[("Wb", c, 1)], eng="pool", out=Wb[:, c, 2 * third:NCOL],
                    in_=wst[c % 2][:, 2 * third:NCOL])
            wb_tok = [("Wb", c, k) for c in range(8) for k in range(2)]
            P.v("tensor_copy", r=["bd_f"], w=["bd_b"], out=bd_b[:], in_=bd_f[:])
            for ti in range(8):
                for j in range(4):
                    P.v("tensor_scalar", r=["identf", "cw"], w=["Dg"], out=Dg[:, ti, j, :], in0=identf[:],
                        scalar1=cw_t[:, ti, j:j + 1], scalar2=None, op0=ALU.mult)
            for ti in range(8):
                pw = pm[ti % 4]
                P.mm(pw[:, 0:4], bdT_f[:, 0 * 8 + ti, :], wif_t[:, 0 * 8 + ti, :], start=True, stop=False,
                     r=["bdT_f", "wif"], w=[("pm", ti % 4)])
                P.mm(pw[:, 0:4], bdT_f[:, 1 * 8 + ti, :], wif_t[:, 1 * 8 + ti, :], start=False, stop=True,
                     r=["bdT_f", "wif"], w=[("pm", ti % 4)])
                P.mm(pw[:, 8:12], bdT_f[:, 2 * 8 + ti, :], wif_t[:, 2 * 8 + ti, :], start=True, stop=True,
                     r=["bdT_f", "wif"], w=[("pm", ti % 4)])
                P.v("tensor_copy", r=[("pm", ti % 4)], w=["Wg"], out=Wg[:, ti, 0, :], in_=pw[:, 0:4])
                P.v("tensor_copy", r=[("pm", ti % 4)], w=["Wg"], out=Wg[:, ti, 1, :], in_=pw[:, 8:12])
            P.pop()
            xbuf = [P.sb("xbuf%d" % i, [128, 4, D], F32) for i in range(2)]
            junk = P.sb("junk", [128, D], BF16)
            ss = [P.sb("ss%d" % i, [128, 4], F32) for i in range(2)]
            rs = [P.sb("rs%d" % i, [128, 4], F32) for i in range(2)]
            hb = P.sb("hb", [128, 4, D], BF16)
            hT = [P.sb("hT%d" % i, [128, 8, 512], BF16) for i in range(2)]
            XM = [P.sb("XM%d" % i, [128, 8, 516], BF16) for i in range(2)]
            XC = P.sb("XC", [128, 8, 512], BF16)
            NST = 6
            stg = [P.sb("stg%d" % i, [128, 512], BF16) for i in range(NST)]
            tst = [P.sb("tst%d" % i, [128, 4, 512], BF16) for i in range(4)]
            gst = [P.sb("gst%d" % i, [2, 512], F32) for i in range(2)]
            pt = [P.ps("pt%d" % i, [128, 1024], BF16) for i in range(2)]
            pgi = P.ps("pgi", [2, 512], F32)
            pgf = P.ps("pgf", [2, 512], F32)

            P.v("memset", w=[("XM", 0), ("XM", 1)], ap=XM[0][:, :, 0:4], constant=0.0)

            sctr = [0]
            pctr = [0]

            def next_pm():
                pctr[0] += 1
                i = pctr[0] % 4
                return pm[i], ("pm", i)

            def next_stg():
                sctr[0] += 1
                i = sctr[0] % NST
                return stg[i], ("stg", i), "stg%d" % i

            def load_x(g):
                P.dma(xbuf[g % 2][:], x[g * 512:(g + 1) * 512, :].rearrange("(t p) c -> p t c", p=128),
                      w=[("x", g % 2)], key="x%d" % (g % 2))

            load_x(0)
            for g in range(NG if CUT != 1 else 0):
                if g + 1 < NG:
                    load_x(g + 1)
                xb = xbuf[g % 2]
                tsl = slice(g * 512, (g + 1) * 512)
                sq, rq = ss[g % 2], rs[g % 2]
                for t in range(4):
                    P.act(junk[:], xb[:, t, :], AF.Square, r=[("x", g % 2)], w=["junk", ("ss", g % 2)],
                          accum_out=sq[:, t:t + 1])
                P.act(rq[:], sq[:], AF.Sqrt, r=[("ss", g % 2)], w=[("rs", g % 2)], scale=1.0 / D, bias=1e-6)
                P.v("reciprocal", r=[("rs", g % 2)], w=[("rs", g % 2)], out=rq[:], in_=rq[:])
                for t in range(4):
                    P.v("scalar_tensor_tensor", r=[("rs", g % 2), ("x", g % 2), "gpre_b"], w=[("hb", t)],
                        out=hb[:, t, :], in0=xb[:, t, :], scalar=rq[:, t:t + 1], in1=gpre_b[:], op0=ALU.mult, op1=ALU.mult)
                h_T = hT[g % 2]
                for c in range(8):
                    ptt = pt[c % 2]
                    for t in range(4):
                        P.tr(ptt[:, t * 128:(t + 1) * 128], hb[:, t, c * 128:(c + 1) * 128], identb[:],
                             r=[("hb", t), "identb"], w=[("pt", c % 2)])
                    P.evac(h_T[:, c, :], ptt[:, 0:512], r=[("pt", c % 2)], w=[("hT", g % 2, c)])
                hT_tok = [("hT", g % 2, c) for c in range(8)]

                def chan_tile(i):
                    pw, ptok = next_pm()
                    for c in range(8):
                        P.mm(pw[:], Wb[:, c, i * 128:(i + 1) * 128], h_T[:, c, :], start=(c == 0), stop=(c == 7),
                             r=wb_tok + hT_tok if c in (0, 7) else (), w=[ptok])
                    return pw, ptok

                xm = XM[g % 2]
                xmp = XM[(g + 1) % 2]
                for ti in range(8):
                    pw, ptok = chan_tile(8 + ti)
                    P.evac(xm[:, ti, 4:516], pw[:], r=[ptok], w=[("XM", g % 2, ti)])
                    if g > 0:
                        P.v("tensor_copy", r=[("XM", (g + 1) % 2, ti)], w=[("XM", g % 2, ti)], eng="pool",
                            out=xm[:, ti, 0:4], in_=xmp[:, ti, 512:516])
                    pc, ctok = next_pm()
                    for j in range(4):
                        P.mm(pc[:], Dg[:, ti, j, :], xm[:, ti, 1 + j:1 + j + 512], start=(j == 0), stop=(j == 3),
                             r=["Dg", ("XM", g % 2, ti)], w=[ctok])
                    P.act(XC[:, ti, :], pc[:], AF.Silu, r=[ctok, "cb"], w=[("XC", ti)], bias=cb_t[:, ti:ti + 1])
                if CUT == 2:
                    continue
                for (pg, c0, nm) in ((pgi, 0, "pgi"), (pgf, 2, "pgf")):
                    for ti in range(8):
                        P.mm(pg[:], Wg[:, ti, 0, c0:c0 + 2], XC[:, ti, :], start=(ti == 0), stop=False,
                             r=["Wg", ("XC", ti)], w=[nm])
                        P.mm(pg[:], Wg[:, ti, 1, c0:c0 + 2], xm[:, ti, 4:516], start=False, stop=(ti == 7),
                             r=["Wg", ("XM", g % 2, ti)], w=[nm])
                P.act(gst[0][:], pgi[:], AF.Identity, r=["pgi", "bg"], w=["gst0"], bias=bg_t[:, 0:1])
                P.dma(GId[:, tsl], gst[0][:], r=["gst0"], w=[("GI", g)], key="gst0", q=STQ)
                P.act(gst[1][:], pgf[:], AF.Identity, r=["pgf", "bg"], w=["gst1"], bias=bg_t[:, 1:2])
                P.dma(GFd[:, tsl], gst[1][:], r=["gst1"], w=[("GF", g)], key="gst1", q=STQ)
                if CUT == 3:
                    continue
                for ti in range(4):
                    pw, ptok = next_pm()
                    P.mm(pw[:], bd_b[:, 0 * 4 + ti, :], XC[:, ti, :], r=["bd_b", ("XC", ti)], w=[ptok])
                    st, stok, skey = next_stg()
                    P.evac(st[:], pw[:], r=[ptok], w=[stok])
                    P.dma(QmT[ti, :, tsl], st[:], r=[stok], w=[("QmT", ti, g)], key=skey, q=STQ)
                    pw, ptok = next_pm()
                    P.mm(pw[:], bd_b[:, 1 * 4 + ti, :], XC[:, ti, :], r=["bd_b", ("XC", ti)], w=[ptok])
                    st, stok, skey = next_stg()
                    P.evac(st[:], pw[:], r=[ptok], w=[stok], scale=1.0 / 16)
                    P.dma(KmT[ti, :, tsl], st[:], r=[stok], w=[("KmT", ti, g)], key=skey, q=STQ)
                    st, stok, skey = next_stg()
                    P.v("tensor_scalar", r=[("XC", ti), "skp"], w=[stok], eng="pool", out=st[:], in0=XC[:, ti, :],
                        scalar1=skp_t[:, ti:ti + 1], scalar2=None, op0=ALU.mult)
                    P.dma(XSd[ti, :, tsl], st[:], r=[stok], w=[("XS", ti, g)], key=skey, q=STQ)
                for (which, tsi, dst, nm) in ((1, 0, Kmd, "Km"), (2, 1, Vmd, "Vm")):
                    tb = tst[tsi]
                    for t in range(4):
                        pw, ptok = next_pm()
                        for ti in range(4):
                            if which == 1:
                                lhsT = XC[:, ti, t * 128:(t + 1) * 128]
                                rr = [("XC", ti)]
                            else:
                                lhsT = xm[:, ti, 4 + t * 128:4 + (t + 1) * 128]
                                rr = [("XM", g % 2, ti)]
                            P.mm(pw[:, ti * 128:(ti + 1) * 128], lhsT, bd_b[:, which * 4 + ti, :],
                                 r=["bd_b"] + rr, w=[ptok])
                        P.evac(tb[:, t, :], pw[:], r=[ptok], w=[("tst", tsi)], scale=(1.0 / 16 if which == 1 else None))
                    P.dma(dst[tsl, :].rearrange("(t p) c -> p t c", p=128), tb[:], r=[("tst", tsi)], w=[(nm, g)],
                          key="tst%d" % tsi, q=STQ)
                if CUT == 4:
                    continue
                for ti in range(4):
                    pw, ptok = chan_tile(16 + ti)
                    st, stok, skey = next_stg()
                    P.act(st[:], pw[:], AF.Silu, r=[ptok, "zcol"], w=[stok], bias=zcol[:, 0:1])
                    P.dma(SZd[ti, :, tsl], st[:], r=[stok], w=[("SZ", ti, g)], key=skey, q=STQ)
                for h in range(4):
                    pw, ptok = chan_tile(h)
                    st, stok, skey = next_stg()
                    P.evac(st[:], pw[:], r=[ptok], w=[stok])
                    P.dma(QT[h, :, tsl], st[:], r=[stok], w=[("QT", h, g)], key=skey, q=STQ)
                for h in range(4):
                    pw, ptok = chan_tile(4 + h)
                    st, stok, skey = next_stg()
                    for bb in range(2):
                        P.act(st[:, bb * 256:(bb + 1) * 256], pw[:, bb * 256:(bb + 1) * 256], AF.Copy, r=[ptok],
                              w=[stok, "kms"], accum_out=kms[:, h, 2 * g + bb:2 * g + bb + 1])
                    P.dma(KT[h, :, tsl], st[:], r=[stok], w=[("KT", h, g)], key=skey, q=STQ)
                for (c0, tsi, dst, nm, silu) in ((2560, 2, Vd, "V", False), (3072, 3, SGd, "SG", True)):
                    tb = tst[tsi]
                    for t in range(4):
                        pw, ptok = next_pm()
                        for c in range(8):
                            P.mm(pw[:], h_T[:, c, t * 128:(t + 1) * 128], Wb[:, c, c0:c0 + 512], start=(c == 0),
                                 stop=(c == 7), r=wb_tok + hT_tok if c in (0, 7) else (), w=[ptok])
                        if silu:
                            P.act(tb[:, t, :], pw[:], AF.Silu, r=[ptok, "zcol"], w=[("tst", tsi)], bias=zcol[:, 0:1])
                        else:
                            P.evac(tb[:, t, :], pw[:], r=[ptok], w=[("tst", tsi)])
                    P.dma(dst[tsl, :].rearrange("(t p) c -> p t c", p=128), tb[:], r=[("tst", tsi)], w=[(nm, g)],
                          key="tst%d" % tsi, q=STQ)
            P.pop()

        if upto >= 2:
            P.push()
            emat = P.sb("emat", [32, 32, 128], BF16)
            negm = P.sb("negm", [128, 32, 32], F32)
            kmb = P.sb("kmb", [128, 4, 32], BF16)
            KTs = [P.sb("KTs%d" % i, [128, S], BF16) for i in range(2)]
            Va = [P.sb("Va%d" % i, [128, NT, 130], BF16) for i in range(2)]
            b2s = P.sb("b2s", [128, XW], F32)
            EB = [P.sb("EB%d" % i, [128, XW], BF16) for i in range(2)]
            QTg = [P.sb("QTg%d" % i, [128, 512], BF16) for i in range(2)]
            SGg = [P.sb("SGg%d" % i, [128, 4, 128], BF16) for i in range(2)]
            gm = P.sb("gm", [128, 4, 32], F32)
            top8 = P.sb("top8", [128, 4, 8], F32)
            Mb = P.sb("Mb", [128, 4, 32], BF16)
            MT = [P.sb("MT%d" % i, [32, 512], BF16) for i in range(2)]
            NPT = 4
            PT = [P.sb("PT%d" % i, [128, 512], BF16) for i in range(NPT)]
            rc = P.sb("rc", [128, 4], F32)
            ya = P.sb("ya", [128, 4, 128], BF16)
            yst = [P.sb("yst%d" % i, [128, 512], BF16) for i in range(2)]
            NPS = 3
            psc = [P.ps("psc%d" % i, [128, 512], F32) for i in range(NPS)]
            po = [P.ps("po%d" % i, [128, 512], F32) for i in range(4)]
            pmisc = P.ps("pmisc", [128, 1024], BF16)
            P.dma(emat[:], emat_d.rearrange("p (n k) -> p n k", k=128), w=["emat"], key="a0")
            P.dma(negm[:], negm_d.rearrange("p (o n) -> p o n", n=32), w=["negm"], key="a1")
            P.v("tensor_copy", r=["kms"], w=["kmb"], out=kmb[:], in_=kms[:])
            for i in range(2):
                P.v("memset", w=[("Va", i)], ap=Va[i][:, :, 128:130], constant=1.0)

            def load_head(h):
                P.dma(KTs[h % 2][:], KT[h], w=[("KTs", h % 2)], key="KTs%d" % (h % 2))
                for q4 in range(4):
                    t0, t1 = q4 * NT // 4, (q4 + 1) * NT // 4
                    P.dma(Va[h % 2][:, t0:t1, 0:128],
                          Vd[t0 * 128:t1 * 128, h * 128:(h + 1) * 128].rearrange("(t p) c -> p t c", p=128),
                          w=[("Va", h % 2)], key="Va%d" % (h % 2))
                P.dma(b2s[:], b2[h], w=["b2s"], key="b2s")
                P.act(EB[h % 2][:], b2s[:], AF.Exp, r=["b2s", "zcol"], w=[("EB", h % 2)], bias=zcol[:, 0:1])

            def load_qg(h, j, par):
                P.dma(QTg[par][:], QT[h, :, j * 512:(j + 1) * 512], w=[("QTg", par)], key="QTg%d" % par)
                P.dma(SGg[par][:], SGd[j * 512:(j + 1) * 512, h * 128:(h + 1) * 128].rearrange("(t p) c -> p t c", p=128),
                      w=[("SGg", par)], key="SGg%d" % par)

            gsc = pmisc[:, 0:256].bitcast(F32)
            groups = [(h, j) for h in range(4) for j in range(NG)]

            def prologue(gi):
                h, j = groups[gi]
                par = gi % 2
                qg, mt = QTg[par], MT[par]
                for t in range(4):
                    P.mm(gsc[:, t * 32:(t + 1) * 32], qg[:, t * 128:(t + 1) * 128], kmb[:, h, :],
                         r=[("QTg", par), "kmb"], w=["pmisc"])
                for t in range(4):
                    own = 2 * j + t // 2
                    P.v("tensor_tensor", r=["pmisc", "negm"], w=[("gm", t)], out=gm[:, t, :],
                        in0=gsc[:, t * 32:(t + 1) * 32], in1=negm[:, own, :], op=ALU.add)
                for t in range(4):
                    own = 2 * j + t // 2
                    P.v("max", r=[("gm", t)], w=[("top8", t)], out=top8[:, t, :], in_=gm[:, t, :])
                    P.v("tensor_scalar", r=[("gm", t), ("top8", t)], w=[("Mb", t)], out=Mb[:, t, :], in0=gm[:, t, :],
                        scalar1=top8[:, t, 2:3], scalar2=-30000.0, op0=ALU.is_lt, op1=ALU.mult)
                    P.v("memset", r=[], w=[("Mb", t)], ap=Mb[:, t, own:own + 1], constant=0.0)
                for half in range(2):
                    for t in (2 * half, 2 * half + 1):
                        P.tr(pmisc[0:32, 256 + (t % 2) * 128:256 + (t % 2 + 1) * 128], Mb[:, t, :], identb[:],
                             r=[("Mb", t), "identb"], w=["pmisc"])
                    P.evac(mt[:, half * 256:(half + 1) * 256], pmisc[0:32, 256:512], r=["pmisc"], w=[("MT", par)])

            items = []
            for gi, (h, j) in enumerate(groups):
                for kt in range(4 * j + 4):
                    items.append((gi, h, j, kt))

            def stage_a(n):
                gi, h, j, kt = items[n]
                par = gi % 2
                qg, mt, kts = QTg[par], MT[par], KTs[h % 2]
                nb = kt // 2
                qlo = max(0, 128 * (kt - 4 * j))
                psi = n % NPS
                ps_ = psc[psi]
                P.mm(ps_[:, qlo:512], kts[:, kt * 128:(kt + 1) * 128], qg[:, qlo:512], start=True, stop=False,
                     r=[("KTs", h % 2), ("QTg", par)], w=[("psc", psi)])
                P.mm(ps_[:, qlo:512], emat[:, nb, :], mt[:, qlo:512], start=False, stop=True,
                     r=["emat", ("MT", par)], w=[("psc", psi)])

            def stage_b(n):
                gi, h, j, kt = items[n]
                eb = EB[h % 2]
                qlo = max(0, 128 * (kt - 4 * j))
                rel = 512 * j - 128 * kt
                psi, pti = n % NPS, n % NPT
                ps_, pt_ = psc[psi], PT[pti]
                if rel <= 1536:
                    P.act(pt_[:, qlo:512], ps_[:, qlo:512], AF.Exp, r=[("psc", psi), "zcol"], w=[("PT", pti)], scale=SCALE,
                          bias=zcol[:, 0:1])
                    o0 = rel + 384 + qlo
                    P.v("tensor_tensor", r=[("PT", pti), ("EB", h % 2)], w=[("PT", pti)], out=pt_[:, qlo:512],
                        in0=pt_[:, qlo:512], in1=eb[:, o0:o0 + 512 - qlo], op=ALU.mult)
                else:
                    P.act(pt_[:, qlo:512], ps_[:, qlo:512], AF.Exp, r=[("psc", psi), "bfar"], w=[("PT", pti)],
                          scale=SCALE, bias=bfar_t[:, h:h + 1])

            def stage_c(n):
                gi, h, j, kt = items[n]
                par = gi % 2
                sg, va = SGg[par], Va[h % 2]
                qlo = max(0, 128 * (kt - 4 * j))
                pti = n % NPT
                pt_ = PT[pti]
                for t in range(qlo // 128, 4):
                    P.mm(po[t][:, 0:129], pt_[:, t * 128:(t + 1) * 128], va[:, kt, 0:129], start=(kt == 0),
                         stop=(kt == 4 * j + t), r=[("PT", pti), ("Va", h % 2)], w=[("po", t)])
                t = kt - 4 * j
                if t >= 0:
                    P.v("reciprocal", r=[("po", t)], w=[("rc", t)], out=rc[:, t:t + 1], in_=po[t][:, 128:129])
                    P.v("scalar_tensor_tensor", r=[("po", t), ("rc", t), ("SGg", par)], w=[("ya", t)], out=ya[:, t, :],
                        in0=po[t][:, 0:128], scalar=rc[:, t:t + 1], in1=sg[:, t, :], op0=ALU.mult, op1=ALU.mult)
                    P.tr(pmisc[:, 512 + t * 128:512 + (t + 1) * 128], ya[:, t, :], identb[:], r=[("ya", t), "identb"],
                         w=["pmisc"])
                if kt == 4 * j + 3:
                    ys = yst[par]
                    P.evac(ys[:], pmisc[:, 512:1024], r=["pmisc"], w=[("yst", par)])
                    P.dma(yt_slice(h * 128, j), ys[:], r=[("yst", par)], w=[("YT", h, j)], key="yst%d" % par, q=STQ)
                    if gi + 2 < len(groups):
                        h2, j2 = groups[gi + 2]
                        load_qg(h2, j2, par)
                    if j == NG - 1 and h + 2 < 4:
                        load_head(h + 2)

            load_head(0)
            load_head(1)
            load_qg(groups[0][0], groups[0][1], 0)
            load_qg(groups[1][0], groups[1][1], 1)
            NI = len(items)
            for n in range(NI + 2):
                if n < NI:
                    if items[n][3] == 0:
                        prologue(items[n][0])
                    stage_a(n)
                if 0 <= n - 1 < NI:
                    stage_b(n - 1)
                if 0 <= n - 2 < NI:
                    stage_c(n - 2)
            P.pop()

        if upto >= 3:
            P.push()
            PIECE = min(S, 2048)
            NPC = S // PIECE
            CPP = PIECE // 128
            sel = P.sb("sel", [2, 2, 128], F32)
            gi = P.sb("gi", [2, PIECE], F32)
            gf = P.sb("gf", [2, PIECE], F32)
            cs = P.sb("cs", [2, PIECE], F32)
            aa = P.sb("aa", [2, PIECE], F32)
            gg = P.sb("gg", [2, PIECE], F32)
            zer = P.sb("zer", [2, PIECE], F32)
            uu = P.sb("uu", [2, PIECE], F32)
            ph = P.sb("ph", [2, PIECE], F32)
            carry = P.sb("carry", [2, 2], F32)
            gends = P.sb("gends", [2, NCH + 1], F32)
            brow = P.sb("brow", [2, NCH], F32)
            UT = P.sb("UT", [128, NCH, 2, 2], F32)
            betab = P.sb("betab", [128, 2, NCH], F32)
            pu = P.ps("pu", [128, 512], F32)
            P.dma(sel[:], sel_d.rearrange("p (h k) -> p h k", k=128), w=["sel"], key="m0")
            P.v("memset", w=["zer"], ap=zer[:], constant=0.0)
            P.v("memset", w=["carry"], ap=carry[:], constant=0.0)
            P.v("memset", w=["gends"], ap=gends[:], constant=0.0)
            for pc in range(NPC):
                psl = slice(pc * PIECE, (pc + 1) * PIECE)
                P.dma(gi[:], GId[:, psl], w=["gi"], key="m1")
                P.dma(gf[:], GFd[:, psl], w=["gf"], key="m2")
                P.act(gf[:], gf[:], AF.Exp, r=["gf", "zcol"], w=["gf"], scale=-1.0, bias=zcol[0:2, 0:1])
                P.act(gf[:], gf[:], AF.Ln, r=["gf", "zcol"], w=["gf"], bias=zcol[0:2, 1:2])
                P.v("tensor_tensor_scan", r=["gf", "zer", "carry"], w=["cs"], out=cs[:], data0=gf[:], data1=zer[:],
                    initial=carry[:, 0:1], op0=ALU.add, op1=ALU.add)
                P.v("tensor_tensor", r=["gi", "cs"], w=["aa"], out=aa[:], in0=gi[:], in1=cs[:], op=ALU.add)
                P.v("tensor_tensor_scan", r=["aa", "zer", "carry"], w=["gg"], out=gg[:], data0=aa[:], data1=zer[:],
                    initial=carry[:, 1:2], op0=ALU.max, op1=ALU.max)
                P.v("tensor_copy", r=["cs"], w=["carry"], out=carry[:, 0:1], in_=cs[:, PIECE - 1:PIECE])
                P.v("tensor_copy", r=["gg"], w=["carry"], out=carry[:, 1:2], in_=gg[:, PIECE - 1:PIECE])
                P.v("tensor_copy", r=["gg"], w=["gends"], out=gends[:, 1 + pc * CPP:1 + (pc + 1) * CPP],
                    in_=gg[:].rearrange("p (c k) -> p c k", k=128)[:, :, 127])
                for c in range(CPP):
                    cg = pc * CPP + c
                    csl = slice(c * 128, (c + 1) * 128)
                    P.v("tensor_scalar", r=["aa", "gends"], w=["uu"], out=uu[:, csl], in0=aa[:, csl],
                        scalar1=gends[:, cg + 1:cg + 2], scalar2=None, op0=ALU.subtract)
                    P.v("tensor_scalar", r=["cs", "gends"], w=["ph"], out=ph[:, csl], in0=cs[:, csl],
                        scalar1=gends[:, cg + 1:cg + 2], scalar2=None, op0=ALU.subtract)
                P.act(uu[:], uu[:], AF.Exp, r=["uu", "zcol"], w=["uu"], bias=zcol[0:2, 0:1])
                P.act(ph[:], ph[:], AF.Exp, r=["ph", "zcol"], w=["ph"], bias=zcol[0:2, 0:1])
                for c in range(CPP):
                    csl = slice(c * 128, (c + 1) * 128)
                    P.tr(pu[:, c * 4:c * 4 + 2], uu[:, csl], identf[0:2, 0:2], r=["uu", "identf"], w=["pu"])
                    P.tr(pu[:, c * 4 + 2:c * 4 + 4], ph[:, csl], identf[0:2, 0:2], r=["ph", "identf"], w=["pu"])
                P.v("tensor_copy", r=["pu"], w=["UT"], out=UT[:, pc * CPP:(pc + 1) * CPP, :, :],
                    in_=pu[:, 0:CPP * 4].rearrange("p (c a b) -> p c a b", a=2, b=2))
            P.v("tensor_tensor", r=["gends"], w=["brow"], out=brow[:], in0=gends[:, 0:NCH], in1=gends[:, 1:NCH + 1],
                op=ALU.subtract)
            P.act(brow[:], brow[:], AF.Exp, r=["brow", "zcol"], w=["brow"], bias=zcol[0:2, 0:1])
            for hl in range(2):
                P.mm(pu[:, 0:NCH], sel[:, hl, :], brow[:], r=["sel", "brow", "UT"], w=["pu"])
                P.v("tensor_copy", r=["pu"], w=["betab"], out=betab[:, hl, :], in_=pu[:, 0:NCH])

            qk = [P.sb("qk%d" % i, [128, 4, 512], BF16) for i in range(2)]
            xz = [P.sb("xz%d" % i, [128, 4, 512], BF16) for i in range(2)]
            kv = [P.sb("kv%d" % i, [128, 4, 514], BF16) for i in range(2)]
            SM = [P.sb("SM%d" % i, [128, 128], BF16) for i in range(2)]
            Vp = [P.sb("Vp%d" % i, [128, 258], BF16) for i in range(2)]
            Dst = P.sb("Dst", [128, 2, 257], F32)
            Cb = [P.sb("Cb%d" % i, [128, 2, 258], BF16) for i in range(2)]
            den = P.sb("den", [128, 2], F32)
            hsc = P.sb("hsc", [128, 256], F32)
            bst = P.sb("bst", [128, 6], F32)
            mv = P.sb("mv", [128, 2], F32)
            rstd = P.sb("rstd", [128, 2], F32)
            hn = P.sb("hn", [128, 256], BF16)
            tmp = P.sb("tmp", [128, 2, 128], F32)
            ymst = [P.sb("ymst%d" % i, [128, 2, 512], BF16) for i in range(2)]
            pst = P.ps("pst", [128, 512], F32)
            pn = [P.ps("pn%d" % i, [128, 512], F32) for i in range(2)]
            pdc = [P.ps("pdc%d" % i, [128, 512], F32) for i in range(2)]
            pht = P.ps("pht", [128, 1024], BF16)
            for i in range(2):
                P.v("memset", w=[("kv", i)], ap=kv[i][:, :, 512:514], constant=1.0)

            def load_m(hl, g, par):
                tsl = slice(g * 512, (g + 1) * 512)
                for i in range(2):
                    ti = 2 * hl + i
                    P.dma(qk[par][:, i, :], QmT[ti, :, tsl], w=[("qk", par)], key="qk%d" % par)
                    P.dma(qk[par][:, 2 + i, :], KmT[ti, :, tsl], w=[("qk", par)], key="qk%d" % par)
                    P.dma(xz[par][:, i, :], XSd[ti, :, tsl], w=[("xz", par)], key="xz%d" % par)
                    P.dma(xz[par][:, 2 + i, :], SZd[ti, :, tsl], w=[("xz", par)], key="xz%d" % par)
                P.dma(kv[par][:, :, 0:256], Kmd[tsl, hl * 256:(hl + 1) * 256].rearrange("(c p) d -> p c d", p=128),
                      w=[("kv", par)], key="kv%d" % par)
                P.dma(kv[par][:, :, 256:512], Vmd[tsl, hl * 256:(hl + 1) * 256].rearrange("(c p) d -> p c d", p=128),
                      w=[("kv", par)], key="kv%d" % par)

            it = 0
            load_m(0, 0, 0)
            for hl in range(2):
                for g in range(NG):
                    par = it % 2
                    it += 1
                    if g + 1 < NG:
                        load_m(hl, g + 1, it % 2)
                    elif hl + 1 < 2:
                        load_m(hl + 1, 0, it % 2)
                    q_k, x_z, k_v, ym = qk[par], xz[par], kv[par], ymst[par]
                    for cc in range(4):
                        c = g * 4 + cc
                        csl = slice(cc * 128, (cc + 1) * 128)
                        sm, vp, cbf = SM[c % 2], Vp[c % 2], Cb[c % 2]
                        pnn, pnt = pn[c % 2], ("pn", c % 2)
                        if c > 0:
                            for dc in range(2):
                                P.act(cbf[:, dc, 0:257], Dst[:, dc, :], AF.Copy, r=[("Dst", dc), "betab"], w=[("Cb", c % 2, dc)],
                                      scale=betab[:, hl, c:c + 1])
                        for i in range(2):
                            P.mm(pst[:, 0:128], q_k[:, 2 + i, csl], q_k[:, i, csl], start=(i == 0), stop=(i == 1),
                                 r=[("qk", par)], w=["pst"])
                        P.v("tensor_tensor", r=["pst", "causb"], w=[("SM", c % 2)], out=sm[:], in0=pst[:, 0:128], in1=causb[:],
                            op=ALU.mult)
                        P.act(vp[:, 0:257], k_v[:, cc, 256:513], AF.Copy, r=[("kv", par), "UT"], w=[("Vp", c % 2)],
                              scale=UT[:, c, 0, hl:hl + 1])
                        P.mm(pnn[:, 0:257], sm[:], vp[:, 0:257], start=True, stop=(c == 0), r=[("SM", c % 2), ("Vp", c % 2)],
                             w=[pnt])
                        if c > 0:
                            for dc in range(2):
                                P.mm(pnn[:, 0:257], q_k[:, dc, csl], cbf[:, dc, 0:257], start=False, stop=(dc == 1),
                                     r=[("qk", par), ("Cb", c % 2, dc)], w=[pnt])
                        for dc in range(2):
                            P.mm(pdc[dc][:, 0:257], k_v[:, cc, dc * 128:(dc + 1) * 128], vp[:, 0:257],
                                 r=[("kv", par), ("Vp", c % 2)], w=[("pdc", dc)])
                            if c == 0:
                                P.v("tensor_copy", r=[("pdc", dc)], w=[("Dst", dc)], out=Dst[:, dc, :], in_=pdc[dc][:, 0:257])
                            else:
                                P.v("scalar_tensor_tensor", r=[("pdc", dc), ("Dst", dc), "betab"], w=[("Dst", dc)],
                                    out=Dst[:, dc, :], in0=Dst[:, dc, :], scalar=betab[:, hl, c:c + 1], in1=pdc[dc][:, 0:257],
                                    op0=ALU.mult, op1=ALU.add)
                        P.act(den[:, 0:1], pnn[:, 256:257], AF.Abs, r=[pnt, "zcol"], w=["den"], bias=zcol[:, 0:1])
                        P.v("tensor_tensor", r=["den", "UT"], w=["den"], out=den[:, 0:1], in0=den[:, 0:1],
                            in1=UT[:, c, 1, hl:hl + 1], op=ALU.max)
                        P.v("reciprocal", r=["den"], w=["den"], out=den[:, 1:2], in_=den[:, 0:1])
                        P.act(hsc[:], pnn[:, 0:256], AF.Copy, r=[pnt, "den"], w=["hsc"], scale=den[:, 1:2])
                        P.v("bn_stats", r=["hsc"], w=["bst"], out=bst[:], in_=hsc[:])
                        P.v("bn_aggr", r=["bst"], w=["mv"], out=mv[:], in_=bst[:])
                        P.act(rstd[:, 0:1], mv[:, 1:2], AF.Sqrt, r=["mv"], w=["rstd"], bias=1e-5)
                        P.v("reciprocal", r=["rstd"], w=["rstd"], out=rstd[:, 1:2], in_=rstd[:, 0:1])
                        P.v("tensor_scalar", r=["hsc", "mv", "rstd"], w=["hn"], out=hn[:], in0=hsc[:], scalar1=mv[:, 0:1],
                            scalar2=rstd[:, 1:2], op0=ALU.subtract, op1=ALU.mult)
                        for i in range(2):
                            P.tr(pht[:, i * 128:(i + 1) * 128], hn[:, i * 128:(i + 1) * 128], identb[:], r=["hn", "identb"],
                                 w=["pht"])
                        for i in range(2):
                            ti = 2 * hl + i
                            P.v("scalar_tensor_tensor", r=["pht", "mhn", ("xz", par)], w=[("tmp", i)], out=tmp[:, i, :],
                                in0=pht[:, i * 128:(i + 1) * 128], scalar=mhn_t[:, ti:ti + 1], in1=x_z[:, i, csl],
                                op0=ALU.mult, op1=ALU.add)
                            P.v("tensor_tensor", r=[("tmp", i), ("xz", par)], w=[("ymst", par)], eng="pool", out=ym[:, i, csl],
                                in0=tmp[:, i, :], in1=x_z[:, 2 + i, csl], op=ALU.mult)
                    for i in range(2):
                        ti = 2 * hl + i
                        P.dma(yt_slice(512 + ti * 128, g), ym[:, i, :],
                              r=[("ymst", par)], w=[("YTm", ti, g)], key="ymst%d" % par, q=STQ)
            P.pop()

        if upto >= 4:
            P.push()
            rg = [[0, 1], [2, 3], [4, 5], [6, 7]]
            for blk in range(NB):
                P.add("pool", (lambda e, blk=blk: e.collective_compute("AllGather", ALU.bypass, replica_groups=rg,
                                                                       ins=[YTsrc[blk].opt()], outs=[YTall[blk].opt()])),
                      w=[("YTall", blk)], dma_key=P.mapkey("cc"), inc=1)
            Wo = P.sb("Wo", [128, 16, D], BF16)
            wos = [P.sb("wos%d" % i, [128, D], F32) for i in range(2)]
            ytl = [P.sb("ytl%d" % i, [128, 16, 512], BF16) for i in range(2)]
            xr = [P.sb("xr%d" % i, [128, 4, D], F32) for i in range(2)]
            ot = [P.sb("ot%d" % i, [128, 4, D], F32) for i in range(2)]
            junk3 = P.sb("junk3", [128, 512], BF16)
            s2 = P.sb("s2", [128, 4], F32)
            py = [[P.ps("py%d_%d" % (i, k), [128, 512], F32) for k in range(2)] for i in range(2)]
            for ck in range(16):
                P.dma(wos[ck % 2][:], w_out[ck * 128:(ck + 1) * 128, :], w=[("wos", ck % 2)], key="wos%d" % (ck % 2))
                P.v("tensor_copy", r=[("wos", ck % 2)], w=[("Wo", ck)], eng=("dve" if ck % 2 else "pool"),
                    out=Wo[:, ck, :], in_=wos[ck % 2][:])
            wo_tok = [("Wo", ck) for ck in range(16)]

            def load3(g):
                blk, off = (g * 512) // TB, (g * 512) % TB
                P.dma(ytl[g % 2][:], YTall[blk, :, off:off + 512].rearrange("(k p) t -> p k t", p=128),
                      r=[("YTall", blk)], w=[("ytl", g % 2)], key="ytl%d" % (g % 2))
                P.dma(xr[g % 2][:], x[g * 512:(g + 1) * 512, :].rearrange("(t p) c -> p t c", p=128),
                      w=[("xr", g % 2)], key="xr%d" % (g % 2))

            load3(0)
            n3 = 0
            for g in range(NG):
                if g + 1 < NG:
                    load3(g + 1)
                yt_, xr_, ot_ = ytl[g % 2], xr[g % 2], ot[g % 2]
                for t in range(4):
                    pp = py[n3 % 2]
                    ptk = [("py", n3 % 2, 0), ("py", n3 % 2, 1)]
                    n3 += 1
                    for hf in range(2):
                        for ck in range(16):
                            P.mm(pp[hf][:], yt_[:, ck, t * 128:(t + 1) * 128], Wo[:, ck, hf * 512:(hf + 1) * 512],
                                 start=(ck == 0), stop=(ck == 15), r=wo_tok + [("ytl", g % 2)] if ck in (0, 15) else (), w=[ptk[hf]])
                        P.act(junk3[:], pp[hf][:], AF.Square, r=[ptk[hf]], w=["junk3", ("s2", hf)], accum_out=s2[:, hf:hf + 1])
                    P.v("tensor_tensor", r=[("s2", 0), ("s2", 1)], w=[("s2", 2)], out=s2[:, 2:3], in0=s2[:, 0:1], in1=s2[:, 1:2],
                        op=ALU.add)
                    P.act(s2[:, 3:4], s2[:, 2:3], AF.Sqrt, r=[("s2", 2)], w=[("s2", 3)], scale=1.0 / D, bias=1e-6)
                    P.v("reciprocal", r=[("s2", 3)], w=[("s2", 3)], out=s2[:, 3:4], in_=s2[:, 3:4])
                    for hf in range(2):
                        hs = slice(hf * 512, (hf + 1) * 512)
                        P.v("scalar_tensor_tensor", r=[ptk[hf], ("s2", 3), "gpost_b"], w=[("ot", g % 2, t, hf)],
                            out=ot_[:, t, hs], in0=pp[hf][:], scalar=s2[:, 3:4], in1=gpost_b[:, hs], op0=ALU.mult, op1=ALU.mult)
                        P.v("tensor_tensor", r=[("ot", g % 2, t, hf), ("xr", g % 2)], w=[("ot", g % 2, t, hf)], eng="pool",
                            out=ot_[:, t, hs], in0=ot_[:, t, hs], in1=xr_[:, t, hs], op=ALU.add)
                P.dma(out[g * 512:(g + 1) * 512, :].rearrange("(t p) c -> p t c", p=128), ot_[:],
                      r=[("ot", g % 2, t, hf) for t in range(4) for hf in range(2)], w=[("out", g)], key="ot%d" % (g % 2))
            P.pop()
        P.emit()
    return nc, P


def _t5_bucket_np(dist):
    n = np.maximum(dist, 0)
    max_exact = 16
    nf = np.maximum(n, 1).astype(np.float32)
    large = max_exact + (np.log(nf / np.float32(max_exact)) / np.float32(math.log(2048 / max_exact))
                         * np.float32(32 - max_exact)).astype(np.int32)
    large = np.minimum(large, 31)
    return np.where(n < max_exact, n, large)


_CONSTS = {}


def _consts():
    if _CONSTS:
        return _CONSTS
    bf = ml_dtypes.bfloat16
    _CONSTS["identb"] = np.eye(128, dtype=np.float32).astype(bf)
    _CONSTS["identf"] = np.eye(128, dtype=np.float32)
    s = np.arange(128)
    _CONSTS["causb"] = (s[:, None] <= s[None, :]).astype(np.float32).astype(bf)
    e = np.zeros((32, 32, 128), np.float32)
    for n in range(32):
        e[n, n, :] = 1.0
    _CONSTS["emat"] = e.reshape(32, 32 * 128).astype(bf)
    own = np.arange(32)[:, None]
    nn = np.arange(32)[None, :]
    ng = np.where(nn < own, 0.0, -1e30).astype(np.float32)
    _CONSTS["negm"] = np.ascontiguousarray(np.broadcast_to(ng[None], (128, 32, 32))).reshape(128, 1024)
    sel = np.zeros((2, 2, 128), np.float32)
    sel[0, 0, :] = 1.0
    sel[1, 1, :] = 1.0
    _CONSTS["sel"] = sel.reshape(2, 256)
    k = np.arange(128)[:, None]
    xx = np.arange(XW)[None, :]
    dist = xx - k - 384
    _CONSTS["bucket"] = _t5_bucket_np(dist)
    _CONSTS["neg"] = dist < 0
    return _CONSTS


def core_inputs(inp, b, hh, S):
    C = _consts()
    f32 = np.float32
    w_in = np.asarray(inp["w_in"][0], f32)
    heads = [4 * hh + i for i in range(4)]
    ch_order = np.concatenate([np.arange(512 * hh, 512 * hh + 512), np.arange(512 * (1 - hh), 512 * (1 - hh) + 512)])

    def hcols(base):
        return np.concatenate([w_in[:, base + 128 * h: base + 128 * h + 128] for h in heads], axis=1)

    w_in_c = np.concatenate([hcols(0), hcols(1024), w_in[:, 4096 + ch_order], w_in[:, 5120 + ch_order[:512]],
                             hcols(2048), hcols(3072)], axis=1)
    d = {"x": np.ascontiguousarray(np.asarray(inp["x"][b, :S], f32)),
         "gpre": np.asarray(inp["g_pre"], f32).reshape(1, D),
         "gpost": np.asarray(inp["g_post"], f32).reshape(1, D),
         "w_in": np.ascontiguousarray(w_in_c)}
    w_out = np.asarray(inp["w_out"][0], f32)
    rows = []
    for r in range(2):
        rows.append(w_out[512 * r:512 * r + 512])
        rows.append(w_out[1024 + 512 * r:1024 + 512 * r + 512])
    d["w_out"] = np.ascontiguousarray(np.concatenate(rows, axis=0))
    rel = np.asarray(inp["rel_bias"], f32)
    b2 = np.empty((4, 128, XW), f32)
    for i, h in enumerate(heads):
        t = rel[:, h][C["bucket"]]
        t = np.where(C["neg"], np.float32(-30000.0), t)
        b2[i] = t
    d["b2"] = b2
    d["bfar"] = np.ascontiguousarray(np.broadcast_to(rel[31, heads][None, :], (128, 4))).astype(f32)
    conv_w = np.asarray(inp["conv_w"][0], f32)[:, ch_order]
    d["cw"] = np.ascontiguousarray(conv_w.reshape(4, 8, 128).transpose(2, 1, 0)).reshape(128, 32)
    d["cb"] = np.ascontiguousarray(np.asarray(inp["conv_b"][0], f32)[ch_order].reshape(8, 128).T)
    blk_order = ch_order.reshape(256, 4)[:, 0] // 4
    bd = np.zeros((3, 8, 128, 128), f32)
    for wi, nm in enumerate(("wq_m", "wk_m", "wv_m")):
        w = np.asarray(inp[nm][0], f32)[blk_order]
        for ti in range(8):
            for nl in range(32):
                bd[wi, ti, 4 * nl:4 * nl + 4, 4 * nl:4 * nl + 4] = w[32 * ti + nl]
    d["bd"] = np.ascontiguousarray(bd[:, 0:4].reshape(12, 128, 128).transpose(1, 0, 2)).reshape(128, 12 * 128)
    d["bdT"] = np.ascontiguousarray(bd.transpose(0, 1, 3, 2).reshape(24, 128, 128).transpose(1, 0, 2)).reshape(128, 24 * 128)
    w_if = np.asarray(inp["w_if"][0], f32)
    gcols = [2 * hh, 2 * hh + 1, 4 + 2 * hh, 4 + 2 * hh + 1]
    wif = np.stack([w_if[1024 * wi + ch_order][:, gcols] for wi in range(3)], axis=0)
    d["wif"] = np.ascontiguousarray(wif.reshape(3, 8, 128, 4).transpose(2, 0, 1, 3)).reshape(128, 96)
    b_if = np.asarray(inp["b_if"][0], f32)
    d["bg"] = np.ascontiguousarray(np.stack([b_if[gcols[0:2]], b_if[gcols[2:4]]], axis=1))
    d["mhn"] = np.ascontiguousarray(np.asarray(inp["mh_norm"][0], f32)[ch_order[:512]].reshape(4, 128).T)
    d["skp"] = np.ascontiguousarray(np.asarray(inp["skip"][0], f32)[ch_order[:512]].reshape(4, 128).T)
    for k in ("identb", "identf", "causb", "emat", "negm", "sel"):
        d[k] = C[k]
    return d


_PROG_CACHE = {}


def run(inputs, S, dbg=()):
    key = (S, tuple(dbg))
    if key not in _PROG_CACHE:
        _PROG_CACHE[key] = build_program(S, dbg)[0]
    nc = _PROG_CACHE[key]
    in_maps = [core_inputs(inputs, c // 2, c % 2, S) for c in range(8)]
    res = run_bass_kernel_spmd(nc, in_maps, core_ids=list(range(8)))
    return res.results


def kernel(**inputs):
    S = inputs["x"].shape[1]
    res = run(inputs, S)
    outp = np.empty((4, S, D), np.float32)
    for b in range(4):
        outp[b, :S // 2] = res[2 * b]["out"][:S // 2]
        outp[b, S // 2:] = res[2 * b + 1]["out"][S // 2:]
    return outp
```

```python
import contextlib
import math
import numpy as np
import ml_dtypes
import concourse.bass as bass
import concourse.mybir as mybir
from concourse.bass_utils import run_bass_kernel_spmd

F32 = mybir.dt.float32
BF16 = mybir.dt.bfloat16
AF = mybir.ActivationFunctionType
ALU = mybir.AluOpType
AX = mybir.AxisListType

D = 1024
NCOL = 3584
XW = 2432
SCALE = 128.0 ** -0.5
ENGS = ["pe", "act", "dve", "pool", "sp"]
STQ = "sp"
import os
CUT = int(os.environ.get("K_CUT", "0"))


class Op:
    __slots__ = ("eng", "fn", "deps", "dma_key", "signal", "event", "inc")

    def __init__(self, eng, fn, deps, dma_key, inc):
        self.eng = eng
        self.fn = fn
        self.deps = deps
        self.dma_key = dma_key
        self.signal = False
        self.event = None
        self.inc = inc


class Prog:
    def __init__(self, nc):
        self.nc = nc
        self.ops = []
        self.last_w = {}
        self.readers = {}
        self.root = contextlib.ExitStack()
        self.scopes = [self.root]
        self.last_eng = {}
        self.last_key = {}
        self.rr = 0
        self.keymap = {}

    def sb(self, name, shape, dt):
        return self.scopes[-1].enter_context(self.nc.sbuf_tensor("s_" + name, list(shape), dt))

    def ps(self, name, shape, dt=F32):
        return self.scopes[-1].enter_context(self.nc.psum_tensor("p_" + name, list(shape), dt))

    def push(self):
        st = contextlib.ExitStack()
        self.scopes.append(st)
        return st

    def pop(self):
        self.barrier()
        self.scopes.pop().close()

    def add(self, eng, fn, r=(), w=(), dma_key=None, inc=16, deps=None):
        if deps is None:
            deps = set()
            for t in r:
                x = self.last_w.get(t)
                if x is not None:
                    deps.add(x)
            for t in w:
                x = self.last_w.get(t)
                if x is not None:
                    deps.add(x)
                for y in self.readers.get(t, ()):
                    deps.add(y)
            if eng == "pe":
                deps = {d for d in deps if not (self.ops[d].eng == "pe" and self.ops[d].dma_key is None)}
            best = {}
            keep = set()
            for d in deps:
                o = self.ops[d]
                if o.dma_key is not None:
                    keep.add(d)
                elif best.get(o.eng, -1) < d:
                    best[o.eng] = d
            deps = keep | set(best.values())
        idx = len(self.ops)
        self.ops.append(Op(eng, fn, sorted(deps), dma_key, inc))
        for t in r:
            lst = self.readers.setdefault(t, [])
            if dma_key is None:
                lst[:] = [y for y in lst if not (self.ops[y].eng == eng and self.ops[y].dma_key is None)]
            lst.append(idx)
        for t in w:
            self.last_w[t] = idx
            self.readers[t] = []
        if dma_key is None:
            self.last_eng[eng] = idx
        else:
            self.last_key[dma_key] = idx
        return idx

    def barrier(self):
        deps = set(self.last_eng.values()) | set(self.last_key.values())
        for e in ENGS:
            self.add(e, lambda h: h.nop(), deps=set(deps))
        self.last_w = {}
        self.readers = {}
        self.keymap = {}

    def dma(self, out, in_, r=(), w=(), key=None, q="sp"):
        return self.add(q, lambda e: e.dma_start(out=out, in_=in_), r, w, dma_key=self.mapkey(key))

    def mapkey(self, key):
        if key not in self.keymap:
            self.keymap[key] = "d%d" % len(self.keymap)
        return self.keymap[key]

    def mm(self, out, lhsT, rhs, start=True, stop=True, r=(), w=()):
        return self.add("pe", lambda e: e.matmul(out, lhsT, rhs, start=start, stop=stop), r, w)

    def tr(self, out, in_, ident, r=(), w=()):
        return self.add("pe", lambda e: e.transpose(out, in_, ident), r, w)

    def act(self, out, in_, func, r=(), w=(), **kw):
        return self.add("act", lambda e: e.activation(out, in_, func, **kw), r, w)

    def v(self, name, r=(), w=(), eng="dve", **kw):
        return self.add(eng, lambda e: getattr(e, name)(**kw), r, w)

    def evac(self, out, in_, r=(), w=(), scale=None, eng=None):
        if eng is None:
            self.rr += 1
            eng = "act" if self.rr % 2 else "dve"
        if eng == "act":
            if scale is None:
                return self.act(out, in_, AF.Copy, r, w)
            return self.act(out, in_, AF.Copy, r, w, scale=float(scale))
        if scale is None:
            return self.v("tensor_copy", r, w, out=out, in_=in_)
        return self.v("tensor_scalar", r, w, out=out, in0=in_, scalar1=float(scale), scalar2=None, op0=ALU.mult)

    def emit(self):
        nc = self.nc
        ops = self.ops
        for o in ops:
            for d in o.deps:
                ops[d].signal = True
        esem = {e: self.root.enter_context(nc.semaphore("es_" + e)) for e in ENGS}
        dsem = {}
        cnt = {e: 0 for e in ENGS}
        dcnt = {}
        for o in ops:
            if o.dma_key is not None:
                if o.dma_key not in dsem:
                    dsem[o.dma_key] = self.root.enter_context(nc.semaphore("ds_%d" % len(dsem)))
                    dcnt[o.dma_key] = 0
                dcnt[o.dma_key] += o.inc
                o.event = (dsem[o.dma_key], dcnt[o.dma_key])
            elif o.signal:
                cnt[o.eng] += 1
                o.event = (esem[o.eng], cnt[o.eng])
        self.nsem = len(dsem) + len(ENGS)
        self.cnt = cnt
        per = {e: [o for o in ops if o.eng == e] for e in ENGS}

        def run(ename, handle):
            seen = {}
            for o in per[ename]:
                need = {}
                for d in o.deps:
                    sem, val = ops[d].event
                    k = id(sem)
                    if need.get(k, (None, 0))[1] < val:
                        need[k] = (sem, val)
                for k, (sem, val) in need.items():
                    if seen.get(k, 0) < val:
                        handle.wait_ge(sem, val)
                        seen[k] = val
                ins = o.fn(handle)
                if o.dma_key is not None:
                    ins.then_inc(o.event[0], o.inc)
                elif o.signal:
                    ins.then_inc(o.event[0], 1)

        with nc.Block() as block:
            @block.tensor
            def _(e):
                run("pe", e)

            @block.scalar
            def _(e):
                run("act", e)

            @block.vector
            def _(e):
                run("dve", e)

            @block.gpsimd
            def _(e):
                run("pool", e)

            @block.sync
            def _(e):
                run("sp", e)


def dram_bcast(ap, nparts, n):
    return bass.AP(ap.tensor, ap.offset, [[0, nparts], [1, n]])


def build_program(S, dbg=(), upto=4):
    NG = S // 512
    NT = S // 128
    NCH = S // 128
    nc = bass.Bass("TRN2", target_bir_lowering=False)

    def din(name, shape, dt=F32):
        return nc.dram_tensor(name, list(shape), dt, kind="ExternalInput").ap()

    def dscr(name, shape, dt=BF16):
        kind = "ExternalOutput" if name in dbg else "Internal"
        return nc.dram_tensor(name, list(shape), dt, kind=kind).ap()

    x = din("x", [S, D])
    gpre = din("gpre", [1, D])
    gpost = din("gpost", [1, D])
    w_in = din("w_in", [D, NCOL])
    w_out = din("w_out", [2 * D, D])
    b2 = din("b2", [4, 128, XW])
    bfar = din("bfar", [128, 4])
    cw = din("cw", [128, 32])
    cb = din("cb", [128, 8])
    bd = din("bd", [128, 3 * 4 * 128])
    bdT = din("bdT", [128, 3 * 8 * 128])
    wif = din("wif", [128, 3 * 8 * 4])
    bg = din("bg", [2, 2])
    mhn = din("mhn", [128, 4])
    skp = din("skp", [128, 4])
    identb_d = din("identb", [128, 128], BF16)
    identf_d = din("identf", [128, 128])
    causb_d = din("causb", [128, 128], BF16)
    emat_d = din("emat", [32, 32 * 128], BF16)
    negm_d = din("negm", [128, 32 * 32])
    sel_d = din("sel", [2, 2 * 128])
    out = nc.dram_tensor("out", [S, D], F32, kind="ExternalOutput").ap()

    QT = dscr("QT", [4, 128, S])
    KT = dscr("KT", [4, 128, S])
    Vd = dscr("Vd", [S, 512])
    SGd = dscr("SGd", [S, 512])
    QmT = dscr("QmT", [4, 128, S])
    KmT = dscr("KmT", [4, 128, S])
    Kmd = dscr("Kmd", [S, 512])
    Vmd = dscr("Vmd", [S, 512])
    XSd = dscr("XSd", [4, 128, S])
    SZd = dscr("SZd", [4, 128, S])
    GId = dscr("GId", [2, S], F32)
    GFd = dscr("GFd", [2, S], F32)
    TB = min(S, 1024)
    NB = S // TB
    YTsrc = nc.dram_tensor("YTsrc", [NB, D, TB], BF16).ap()
    YTall = nc.dram_tensor("YTall", [NB, 2 * D, TB], BF16).ap()

    def yt_slice(row0, g):
        blk, off = (g * 512) // TB, (g * 512) % TB
        return YTsrc[blk, row0:row0 + 128, off:off + 512]
    YTdbg = dscr("YTdbg", [D, S]) if "YTdbg" in dbg else None

    P = Prog(nc)
    with P.root:
        identb = P.sb("identb", [128, 128], BF16)
        identf = P.sb("identf", [128, 128], F32)
        causb = P.sb("causb", [128, 128], BF16)
        gpre_b = P.sb("gpre_b", [128, D], F32)
        gpost_b = P.sb("gpost_b", [128, D], F32)
        kms = P.sb("kms", [128, 4, 32], F32)
        bfar_t = P.sb("bfar_t", [128, 4], F32)
        mhn_t = P.sb("mhn_t", [128, 4], F32)
        skp_t = P.sb("skp_t", [128, 4], F32)
        cb_t = P.sb("cb_t", [128, 8], F32)
        bg_t = P.sb("bg_t", [2, 2], F32)
        P.dma(identb[:], identb_d, w=["identb"], key="c0")
        P.dma(identf[:], identf_d, w=["identf"], key="c1")
        P.dma(causb[:], causb_d, w=["causb"], key="c2")
        P.dma(gpre_b[:], dram_bcast(gpre, 128, D), w=["gpre_b"], key="c3")
        P.dma(gpost_b[:], dram_bcast(gpost, 128, D), w=["gpost_b"], key="c4")
        P.dma(bfar_t[:], bfar, w=["bfar"], key="c5")
        P.dma(mhn_t[:], mhn, w=["mhn"], key="c6")
        P.dma(skp_t[:], skp, w=["skp"], key="c7")
        P.dma(cb_t[:], cb, w=["cb"], key="c8")
        P.dma(bg_t[:], bg, w=["bg"], key="c9")
        P.v("memset", w=["kms"], ap=kms[:], constant=0.0)
        zcol = P.sb("zcol", [128, 2], F32)
        P.v("memset", w=["zcol"], ap=zcol[:, 0:1], constant=0.0)
        P.v("memset", w=["zcol"], ap=zcol[:, 1:2], constant=1.0)
        P.barrier()

        if upto >= 1:
            P.push()
            Wb = P.sb("Wb", [128, 8, NCOL], BF16)
            Dg = P.sb("Dg", [128, 8, 4, 128], BF16)
            bd_b = P.sb("bd_b", [128, 12, 128], BF16)
            Wg = P.sb("Wg", [128, 8, 2, 4], BF16)
            pm = [P.ps("pm%d" % i, [128, 512], F32) for i in range(4)]
            P.push()
            wst = [P.sb("wst%d" % i, [128, NCOL], F32) for i in range(2)]
            cw_t = P.sb("cw_t", [128, 8, 4], F32)
            bd_f = P.sb("bd_f", [128, 12, 128], F32)
            bdT_f = P.sb("bdT_f", [128, 24, 128], F32)
            wif_t = P.sb("wif_t", [128, 24, 4], F32)
            P.dma(cw_t[:], cw.rearrange("p (t j) -> p t j", j=4), w=["cw"], key="c10")
            P.dma(bd_f[:], bd.rearrange("p (k c) -> p k c", c=128), w=["bd_f"], key="c11")
            P.dma(bdT_f[:], bdT.rearrange("p (k c) -> p k c", c=128), w=["bdT_f"], key="c12")
            P.dma(wif_t[:], wif.rearrange("p (k c) -> p k c", c=4), w=["wif"], key="c13")
            for c in range(8):
                P.dma(wst[c % 2][:], w_in[c * 128:(c + 1) * 128, :], w=[("wst", c % 2)], key="wst%d" % (c % 2))
                third = NCOL // 4
                P.v("tensor_copy", r=[("wst", c % 2)], w=[("Wb", c, 0)], out=Wb[:, c, 0:2 * third], in_=wst[c % 2][:, 0:2 * third])
                P.v("tensor_copy", r=[("wst", c % 2)], w=[("Wb", c, 1)], eng="pool", out=Wb[:, c, 2 * third:NCOL],
                    in_=wst[c % 2][:, 2 * third:NCOL])
            wb_tok = [("Wb", c, k) for c in range(8) for k in range(2)]
            P.v("tensor_copy", r=["bd_f"], w=["bd_b"], out=bd_b[:], in_=bd_f[:])
            for ti in range(8):
                for j in range(4):
                    P.v("tensor_scalar", r=["identf", "cw"], w=["Dg"], out=Dg[:, ti, j, :], in0=identf[:],
                        scalar1=cw_t[:, ti, j:j + 1], scalar2=None, op0=ALU.mult)
            for ti in range(8):
                pw = pm[ti % 4]
                P.mm(pw[:, 0:4], bdT_f[:, 0 * 8 + ti, :], wif_t[:, 0 * 8 + ti, :], start=True, stop=False,
                     r=["bdT_f", "wif"], w=[("pm", ti % 4)])
                P.mm(pw[:, 0:4], bdT_f[:, 1 * 8 + ti, :], wif_t[:, 1 * 8 + ti, :], start=False, stop=True,
                     r=["bdT_f", "wif"], w=[("pm", ti % 4)])
                P.mm(pw[:, 8:12], bdT_f[:, 2 * 8 + ti, :], wif_t[:, 2 * 8 + ti, :], start=True, stop=True,
                     r=["bdT_f", "wif"], w=[("pm", ti % 4)])
                P.v("tensor_copy", r=[("pm", ti % 4)], w=["Wg"], out=Wg[:, ti, 0, :], in_=pw[:, 0:4])
                P.v("tensor_copy", r=[("pm", ti % 4)], w=["Wg"], out=Wg[:, ti, 1, :], in_=pw[:, 8:12])
            P.pop()
            xbuf = [P.sb("xbuf%d" % i, [128, 4, D], F32) for i in range(2)]
            junk = P.sb("junk", [128, D], BF16)
            ss = [P.sb("ss%d" % i, [128, 4], F32) for i in range(2)]
            rs = [P.sb("rs%d" % i, [128, 4], F32) for i in range(2)]
            hb = P.sb("hb", [128, 4, D], BF16)
            hT = [P.sb("hT%d" % i, [128, 8, 512], BF16) for i in range(2)]
            XM = [P.sb("XM%d" % i, [128, 8, 516], BF16) for i in range(2)]
            XC = P.sb("XC", [128, 8, 512], BF16)
            NST = 6
            stg = [P.sb("stg%d" % i, [128, 512], BF16) for i in range(NST)]
            tst = [P.sb("tst%d" % i, [128, 4, 512], BF16) for i in range(4)]
            gst = [P.sb("gst%d" % i, [2, 512], F32) for i in range(2)]
            pt = [P.ps("pt%d" % i, [128, 1024], BF16) for i in range(2)]
            pgi = P.ps("pgi", [2, 512], F32)
            pgf = P.ps("pgf", [2, 512], F32)

            P.v("memset", w=[("XM", 0), ("XM", 1)], ap=XM[0][:, :, 0:4], constant=0.0)

            sctr = [0]
            pctr = [0]

            def next_pm():
                pctr[0] += 1
                i = pctr[0] % 4
                return pm[i], ("pm", i)

            def next_stg():
                sctr[0] += 1
                i = sctr[0] % NST
                return stg[i], ("stg", i), "stg%d" % i

            def load_x(g):
                P.dma(xbuf[g % 2][:], x[g * 512:(g + 1) * 512, :].rearrange("(t p) c -> p t c", p=128),
                      w=[("x", g % 2)], key="x%d" % (g % 2))

            load_x(0)
            for g in range(NG if CUT != 1 else 0):
                if g + 1 < NG:
                    load_x(g + 1)
                xb = xbuf[g % 2]
                tsl = slice(g * 512, (g + 1) * 512)
                sq, rq = ss[g % 2], rs[g % 2]
                for t in range(4):
                    P.act(junk[:], xb[:, t, :], AF.Square, r=[("x", g % 2)], w=["junk", ("ss", g % 2)],
                          accum_out=sq[:, t:t + 1])
                P.act(rq[:], sq[:], AF.Sqrt, r=[("ss", g % 2)], w=[("rs", g % 2)], scale=1.0 / D, bias=1e-6)
                P.v("reciprocal", r=[("rs", g % 2)], w=[("rs", g % 2)], out=rq[:], in_=rq[:])
                for t in range(4):
                    P.v("scalar_tensor_tensor", r=[("rs", g % 2), ("x", g % 2), "gpre_b"], w=[("hb", t)],
                        out=hb[:, t, :], in0=xb[:, t, :], scalar=rq[:, t:t + 1], in1=gpre_b[:], op0=ALU.mult, op1=ALU.mult)
                h_T = hT[g % 2]
                for c in range(8):
                    ptt = pt[c % 2]
                    for t in range(4):
                        P.tr(ptt[:, t * 128:(t + 1) * 128], hb[:, t, c * 128:(c + 1) * 128], identb[:],
                             r=[("hb", t), "identb"], w=[("pt", c % 2)])
                    P.evac(h_T[:, c, :], ptt[:, 0:512], r=[("pt", c % 2)], w=[("hT", g % 2, c)])
                hT_tok = [("hT", g % 2, c) for c in range(8)]

                def chan_tile(i):
                    pw, ptok = next_pm()
                    for c in range(8):
                        P.mm(pw[:], Wb[:, c, i * 128:(i + 1) * 128], h_T[:, c, :], start=(c == 0), stop=(c == 7),
                             r=wb_tok + hT_tok if c in (0, 7) else (), w=[ptok])
                    return pw, ptok

                xm = XM[g % 2]
                xmp = XM[(g + 1) % 2]
                for ti in range(8):
                    pw, ptok = chan_tile(8 + ti)
                    P.evac(xm[:, ti, 4:516], pw[:], r=[ptok], w=[("XM", g % 2, ti)])
                    if g > 0:
                        P.v("tensor_copy", r=[("XM", (g + 1) % 2, ti)], w=[("XM", g % 2, ti)], eng="pool",
                            out=xm[:, ti, 0:4], in_=xmp[:, ti, 512:516])
                    pc, ctok = next_pm()
                    for j in range(4):
                        P.mm(pc[:], Dg[:, ti, j, :], xm[:, ti, 1 + j:1 + j + 512], start=(j == 0), stop=(j == 3),
                             r=["Dg", ("XM", g % 2, ti)], w=[ctok])
                    P.act(XC[:, ti, :], pc[:], AF.Silu, r=[ctok, "cb"], w=[("XC", ti)], bias=cb_t[:, ti:ti + 1])
                if CUT == 2:
                    continue
                for (pg, c0, nm) in ((pgi, 0, "pgi"), (pgf, 2, "pgf")):
                    for ti in range(8):
                        P.mm(pg[:], Wg[:, ti, 0, c0:c0 + 2], XC[:, ti, :], start=(ti == 0), stop=False,
                             r=["Wg", ("XC", ti)], w=[nm])
                        P.mm(pg[:], Wg[:, ti, 1, c0:c0 + 2], xm[:, ti, 4:516], start=False, stop=(ti == 7),
                             r=["Wg", ("XM", g % 2, ti)], w=[nm])
                P.act(gst[0][:], pgi[:], AF.Identity, r=["pgi", "bg"], w=["gst0"], bias=bg_t[:, 0:1])
                P.dma(GId[:, tsl], gst[0][:], r=["gst0"], w=[("GI", g)], key="gst0", q=STQ)
                P.act(gst[1][:], pgf[:], AF.Identity, r=["pgf", "bg"], w=["gst1"], bias=bg_t[:, 1:2])
                P.dma(GFd[:, tsl], gst[1][:], r=["gst1"], w=[("GF", g)], key="gst1", q=STQ)
                if CUT == 3:
                    continue
                for ti in range(4):
                    pw, ptok = next_pm()
                    P.mm(pw[:], bd_b[:, 0 * 4 + ti, :], XC[:, ti, :], r=["bd_b", ("XC", ti)], w=[ptok])
                    st, stok, skey = next_stg()
                    P.evac(st[:], pw[:], r=[ptok], w=[stok])
                    P.dma(QmT[ti, :, tsl], st[:], r=[stok], w=[("QmT", ti, g)], key=skey, q=STQ)
                    pw, ptok = next_pm()
                    P.mm(pw[:], bd_b[:, 1 * 4 + ti, :], XC[:, ti, :], r=["bd_b", ("XC", ti)], w=[ptok])
                    st, stok, skey = next_stg()
                    P.evac(st[:], pw[:], r=[ptok], w=[stok], scale=1.0 / 16)
                    P.dma(KmT[ti, :, tsl], st[:], r=[stok], w=[("KmT", ti, g)], key=skey, q=STQ)
                    st, stok, skey = next_stg()
                    P.v("tensor_scalar", r=[("XC", ti), "skp"], w=[stok], eng="pool", out=st[:], in0=XC[:, ti, :],
                        scalar1=skp_t[:, ti:ti + 1], scalar2=None, op0=ALU.mult)
                    P.dma(XSd[ti, :, tsl], st[:], r=[stok], w=[("XS", ti, g)], key=skey, q=STQ)
                for (which, tsi, dst, nm) in ((1, 0, Kmd, "Km"), (2, 1, Vmd, "Vm")):
                    tb = tst[tsi]
                    for t in range(4):
                        pw, ptok = next_pm()
                        for ti in range(4):
                            if which == 1:
                                lhsT = XC[:, ti, t * 128:(t + 1) * 128]
                                rr = [("XC", ti)]
                            else:
                                lhsT = xm[:, ti, 4 + t * 128:4 + (t + 1) * 128]
                                rr = [("XM", g % 2, ti)]
                            P.mm(pw[:, ti * 128:(ti + 1) * 128], lhsT, bd_b[:, which * 4 + ti, :],
                                 r=["bd_b"] + rr, w=[ptok])
                        P.evac(tb[:, t, :], pw[:], r=[ptok], w=[("tst", tsi)], scale=(1.0 / 16 if which == 1 else None))
                    P.dma(dst[tsl, :].rearrange("(t p) c -> p t c", p=128), tb[:], r=[("tst", tsi)], w=[(nm, g)],
                          key="tst%d" % tsi, q=STQ)
                if CUT == 4:
                    continue
                for ti in range(4):
                    pw, ptok = chan_tile(16 + ti)
                    st, stok, skey = next_stg()
                    P.act(st[:], pw[:], AF.Silu, r=[ptok, "zcol"], w=[stok], bias=zcol[:, 0:1])
                    P.dma(SZd[ti, :, tsl], st[:], r=[stok], w=[("SZ", ti, g)], key=skey, q=STQ)
                for h in range(4):
                    pw, ptok = chan_tile(h)
                    st, stok, skey = next_stg()
                    P.evac(st[:], pw[:], r=[ptok], w=[stok])
                    P.dma(QT[h, :, tsl], st[:], r=[stok], w=[("QT", h, g)], key=skey, q=STQ)
                for h in range(4):
                    pw, ptok = chan_tile(4 + h)
                    st, stok, skey = next_stg()
                    for bb in range(2):
                        P.act(st[:, bb * 256:(bb + 1) * 256], pw[:, bb * 256:(bb + 1) * 256], AF.Copy, r=[ptok],
                              w=[stok, "kms"], accum_out=kms[:, h, 2 * g + bb:2 * g + bb + 1])
                    P.dma(KT[h, :, tsl], st[:], r=[stok], w=[("KT", h, g)], key=skey, q=STQ)
                for (c0, tsi, dst, nm, silu) in ((2560, 2, Vd, "V", False), (3072, 3, SGd, "SG", True)):
                    tb = tst[tsi]
                    for t in range(4):
                        pw, ptok = next_pm()
                        for c in range(8):
                            P.mm(pw[:], h_T[:, c, t * 128:(t + 1) * 128], Wb[:, c, c0:c0 + 512], start=(c == 0),
                                 stop=(c == 7), r=wb_tok + hT_tok if c in (0, 7) else (), w=[ptok])
                        if silu:
                            P.act(tb[:, t, :], pw[:], AF.Silu, r=[ptok, "zcol"], w=[("tst", tsi)], bias=zcol[:, 0:1])
                        else:
                            P.evac(tb[:, t, :], pw[:], r=[ptok], w=[("tst", tsi)])
                    P.dma(dst[tsl, :].rearrange("(t p) c -> p t c", p=128), tb[:], r=[("tst", tsi)], w=[(nm, g)],
                          key="tst%d" % tsi, q=STQ)
            P.pop()

        if upto >= 2:
            P.push()
            emat = P.sb("emat", [32, 32, 128], BF16)
            negm = P.sb("negm", [128, 32, 32], F32)
            kmb = P.sb("kmb", [128, 4, 32], BF16)
            KTs = [P.sb("KTs%d" % i, [128, S], BF16) for i in range(2)]
            Va = [P.sb("Va%d" % i, [128, NT, 130], BF16) for i in range(2)]
            b2s = P.sb("b2s", [128, XW], F32)
            EB = [P.sb("EB%d" % i, [128, XW], BF16) for i in range(2)]
            QTg = [P.sb("QTg%d" % i, [128, 512], BF16) for i in range(2)]
            SGg = [P.sb("SGg%d" % i, [128, 4, 128], BF16) for i in range(2)]
            gm = P.sb("gm", [128, 4, 32], F32)
            top8 = P.sb("top8", [128, 4, 8], F32)
            Mb = P.sb("Mb", [128, 4, 32], BF16)
            MT = [P.sb("MT%d" % i, [32, 512], BF16) for i in range(2)]
            NPT = 4
            PT = [P.sb("PT%d" % i, [128, 512], BF16) for i in range(NPT)]
            rc = P.sb("rc", [128, 4], F32)
            ya = P.sb("ya", [128, 4, 128], BF16)
            yst = [P.sb("yst%d" % i, [128, 512], BF16) for i in range(2)]
            NPS = 3
            psc = [P.ps("psc%d" % i, [128, 512], F32) for i in range(NPS)]
            po = [P.ps("po%d" % i, [128, 512], F32) for i in range(4)]
            pmisc = P.ps("pmisc", [128, 1024], BF16)
            P.dma(emat[:], emat_d.rearrange("p (n k) -> p n k", k=128), w=["emat"], key="a0")
            P.dma(negm[:], negm_d.rearrange("p (o n) -> p o n", n=32), w=["negm"], key="a1")
            P.v("tensor_copy", r=["kms"], w=["kmb"], out=kmb[:], in_=kms[:])
            for i in range(2):
                P.v("memset", w=[("Va", i)], ap=Va[i][:, :, 128:130], constant=1.0)

            def load_head(h):
                P.dma(KTs[h % 2][:], KT[h], w=[("KTs", h % 2)], key="KTs%d" % (h % 2))
                for q4 in range(4):
                    t0, t1 = q4 * NT // 4, (q4 + 1) * NT // 4
                    P.dma(Va[h % 2][:, t0:t1, 0:128],
                          Vd[t0 * 128:t1 * 128, h * 128:(h + 1) * 128].rearrange("(t p) c -> p t c", p=128),
                          w=[("Va", h % 2)], key="Va%d" % (h % 2))
                P.dma(b2s[:], b2[h], w=["b2s"], key="b2s")
                P.act(EB[h % 2][:], b2s[:], AF.Copy, r=["b2s"], w=[("EB", h % 2)], scale=1.0 / SCALE)

            def load_qg(h, j, par):
                P.dma(QTg[par][:], QT[h, :, j * 512:(j + 1) * 512], w=[("QTg", par)], key="QTg%d" % par)
                P.dma(SGg[par][:], SGd[j * 512:(j + 1) * 512, h * 128:(h + 1) * 128].rearrange("(t p) c -> p t c", p=128),
                      w=[("SGg", par)], key="SGg%d" % par)

            gsc = pmisc[:, 0:256].bitcast(F32)
            groups = [(h, j) for h in range(4) for j in range(NG)]

            def prologue(gi):
                h, j = groups[gi]
                par = gi % 2
                qg, mt = QTg[par], MT[par]
                for t in range(4):
                    P.mm(gsc[:, t * 32:(t + 1) * 32], qg[:, t * 128:(t + 1) * 128], kmb[:, h, :],
                         r=[("QTg", par), "kmb"], w=["pmisc"])
                for t in range(4):
                    own = 2 * j + t // 2
                    P.v("tensor_tensor", r=["pmisc", "negm"], w=[("gm", t)], out=gm[:, t, :],
                        in0=gsc[:, t * 32:(t + 1) * 32], in1=negm[:, own, :], op=ALU.add)
                for t in range(4):
                    own = 2 * j + t // 2
                    P.v("max", r=[("gm", t)], w=[("top8", t)], out=top8[:, t, :], in_=gm[:, t, :])
                    P.v("tensor_scalar", r=[("gm", t), ("top8", t)], w=[("Mb", t)], out=Mb[:, t, :], in0=gm[:, t, :],
                        scalar1=top8[:, t, 2:3], scalar2=-30000.0, op0=ALU.is_lt, op1=ALU.mult)
                    P.v("memset", r=[], w=[("Mb", t)], ap=Mb[:, t, own:own + 1], constant=0.0)
                for half in range(2):
                    for t in (2 * half, 2 * half + 1):
                        P.tr(pmisc[0:32, 256 + (t % 2) * 128:256 + (t % 2 + 1) * 128], Mb[:, t, :], identb[:],
                             r=[("Mb", t), "identb"], w=["pmisc"])
                    P.evac(mt[:, half * 256:(half + 1) * 256], pmisc[0:32, 256:512], r=["pmisc"], w=[("MT", par)])

            items = []
            for gi, (h, j) in enumerate(groups):
                for kt in range(4 * j + 4):
                    items.append((gi, h, j, kt))

            def stage_a(n):
                gi, h, j, kt = items[n]
                par = gi % 2
                qg, mt, kts = QTg[par], MT[par], KTs[h % 2]
                nb = kt // 2
                qlo = max(0, 128 * (kt - 4 * j))
                psi = n % NPS
                ps_ = psc[psi]
                P.mm(ps_[:, qlo:512], kts[:, kt * 128:(kt + 1) * 128], qg[:, qlo:512], start=True, stop=False,
                     r=[("KTs", h % 2), ("QTg", par)], w=[("psc", psi)])
                rel = 512 * j - 128 * kt
                near = rel <= 1536
                P.mm(ps_[:, qlo:512], emat[:, nb, :], mt[:, qlo:512], start=False, stop=not near,
                     r=["emat", ("MT", par)], w=[("psc", psi)])
                if near:
                    o0 = rel + 384 + qlo
                    P.mm(ps_[:, qlo:512], identb[:], EB[h % 2][:, o0:o0 + 512 - qlo], start=False, stop=True,
                         r=["identb", ("EB", h % 2)], w=[("psc", psi)])

            def stage_b(n):
                gi, h, j, kt = items[n]
                eb = EB[h % 2]
                qlo = max(0, 128 * (kt - 4 * j))
                rel = 512 * j - 128 * kt
                psi, pti = n % NPS, n % NPT
                ps_, pt_ = psc[psi], PT[pti]
                if rel <= 1536:
                    P.act(pt_[:, qlo:512], ps_[:, qlo:512], AF.Exp, r=[("psc", psi), "zcol"], w=[("PT", pti)], scale=SCALE,
                          bias=zcol[:, 0:1])
                else:
                    P.act(pt_[:, qlo:512], ps_[:, qlo:512], AF.Exp, r=[("psc", psi), "bfar"], w=[("PT", pti)],
                          scale=SCALE, bias=bfar_t[:, h:h + 1])

            def stage_c(n):
                gi, h, j, kt = items[n]
                par = gi % 2
                sg, va = SGg[par], Va[h % 2]
                qlo = max(0, 128 * (kt - 4 * j))
                pti = n % NPT
                pt_ = PT[pti]
                for t in range(qlo // 128, 4):
                    P.mm(po[t][:, 0:129], pt_[:, t * 128:(t + 1) * 128], va[:, kt, 0:129], start=(kt == 0),
                         stop=(kt == 4 * j + t), r=[("PT", pti), ("Va", h % 2)], w=[("po", t)])
                t = kt - 4 * j
                if t >= 0:
                    P.v("reciprocal", r=[("po", t)], w=[("rc", t)], out=rc[:, t:t + 1], in_=po[t][:, 128:129])
                    P.v("scalar_tensor_tensor", r=[("po", t), ("rc", t), ("SGg", par)], w=[("ya", t)], out=ya[:, t, :],
                        in0=po[t][:, 0:128], scalar=rc[:, t:t + 1], in1=sg[:, t, :], op0=ALU.mult, op1=ALU.mult)
                    P.tr(pmisc[:, 512 + t * 128:512 + (t + 1) * 128], ya[:, t, :], identb[:], r=[("ya", t), "identb"],
                         w=["pmisc"])
                if kt == 4 * j + 3:
                    ys = yst[par]
                    P.evac(ys[:], pmisc[:, 512:1024], r=["pmisc"], w=[("yst", par)])
                    P.dma(yt_slice(h * 128, j), ys[:], r=[("yst", par)], w=[("YT", h, j)], key="yst%d" % par, q=STQ)
                    if gi + 2 < len(groups):
                        h2, j2 = groups[gi + 2]
                        load_qg(h2, j2, par)
                    if j == NG - 1 and h + 2 < 4:
                        load_head(h + 2)

            load_head(0)
            load_head(1)
            load_qg(groups[0][0], groups[0][1], 0)
            load_qg(groups[1][0], groups[1][1], 1)
            NI = len(items)
            for n in range(NI + 2):
                if n < NI:
                    if items[n][3] == 0:
                        prologue(items[n][0])
                    stage_a(n)
                if 0 <= n - 1 < NI:
                    stage_b(n - 1)
                if 0 <= n - 2 < NI:
                    stage_c(n - 2)
            P.pop()

        if upto >= 3:
            P.push()
            PIECE = min(S, 2048)
            NPC = S // PIECE
            CPP = PIECE // 128
            sel = P.sb("sel", [2, 2, 128], F32)
            gi = P.sb("gi", [2, PIECE], F32)
            gf = P.sb("gf", [2, PIECE], F32)
            cs = P.sb("cs", [2, PIECE], F32)
            aa = P.sb("aa", [2, PIECE], F32)
            gg = P.sb("gg", [2, PIECE], F32)
            zer = P.sb("zer", [2, PIECE], F32)
            uu = P.sb("uu", [2, PIECE], F32)
            ph = P.sb("ph", [2, PIECE], F32)
            carry = P.sb("carry", [2, 2], F32)
            gends = P.sb("gends", [2, NCH + 1], F32)
            brow = P.sb("brow", [2, NCH], F32)
            UT = P.sb("UT", [128, NCH, 2, 2], F32)
            betab = P.sb("betab", [128, 2, NCH], F32)
            pu = P.ps("pu", [128, 512], F32)
            P.dma(sel[:], sel_d.rearrange("p (h k) -> p h k", k=128), w=["sel"], key="m0")
            P.v("memset", w=["zer"], ap=zer[:], constant=0.0)
            P.v("memset", w=["carry"], ap=carry[:], constant=0.0)
            P.v("memset", w=["gends"], ap=gends[:], constant=0.0)
            for pc in range(NPC):
                psl = slice(pc * PIECE, (pc + 1) * PIECE)
                P.dma(gi[:], GId[:, psl], w=["gi"], key="m1")
                P.dma(gf[:], GFd[:, psl], w=["gf"], key="m2")
                P.act(gf[:], gf[:], AF.Exp, r=["gf", "zcol"], w=["gf"], scale=-1.0, bias=zcol[0:2, 0:1])
                P.act(gf[:], gf[:], AF.Ln, r=["gf", "zcol"], w=["gf"], bias=zcol[0:2, 1:2])
                P.v("tensor_tensor_scan", r=["gf", "zer", "carry"], w=["cs"], out=cs[:], data0=gf[:], data1=zer[:],
                    initial=carry[:, 0:1], op0=ALU.add, op1=ALU.add)
                P.v("tensor_tensor", r=["gi", "cs"], w=["aa"], out=aa[:], in0=gi[:], in1=cs[:], op=ALU.add)
                P.v("tensor_tensor_scan", r=["aa", "zer", "carry"], w=["gg"], out=gg[:], data0=aa[:], data1=zer[:],
                    initial=carry[:, 1:2], op0=ALU.max, op1=ALU.max)
                P.v("tensor_copy", r=["cs"], w=["carry"], out=carry[:, 0:1], in_=cs[:, PIECE - 1:PIECE])
                P.v("tensor_copy", r=["gg"], w=["carry"], out=carry[:, 1:2], in_=gg[:, PIECE - 1:PIECE])
                P.v("tensor_copy", r=["gg"], w=["gends"], out=gends[:, 1 + pc * CPP:1 + (pc + 1) * CPP],
                    in_=gg[:].rearrange("p (c k) -> p c k", k=128)[:, :, 127])
                for c in range(CPP):
                    cg = pc * CPP + c
                    csl = slice(c * 128, (c + 1) * 128)
                    P.v("tensor_scalar", r=["aa", "gends"], w=["uu"], out=uu[:, csl], in0=aa[:, csl],
                        scalar1=gends[:, cg + 1:cg + 2], scalar2=None, op0=ALU.subtract)
                    P.v("tensor_scalar", r=["cs", "gends"], w=["ph"], out=ph[:, csl], in0=cs[:, csl],
                        scalar1=gends[:, cg + 1:cg + 2], scalar2=None, op0=ALU.subtract)
                P.act(uu[:], uu[:], AF.Exp, r=["uu", "zcol"], w=["uu"], bias=zcol[0:2, 0:1])
                P.act(ph[:], ph[:], AF.Exp, r=["ph", "zcol"], w=["ph"], bias=zcol[0:2, 0:1])
                for c in range(CPP):
                    csl = slice(c * 128, (c + 1) * 128)
                    P.tr(pu[:, c * 4:c * 4 + 2], uu[:, csl], identf[0:2, 0:2], r=["uu", "identf"], w=["pu"])
                    P.tr(pu[:, c * 4 + 2:c * 4 + 4], ph[:, csl], identf[0:2, 0:2], r=["ph", "identf"], w=["pu"])
                P.v("tensor_copy", r=["pu"], w=["UT"], out=UT[:, pc * CPP:(pc + 1) * CPP, :, :],
                    in_=pu[:, 0:CPP * 4].rearrange("p (c a b) -> p c a b", a=2, b=2))
            P.v("tensor_tensor", r=["gends"], w=["brow"], out=brow[:], in0=gends[:, 0:NCH], in1=gends[:, 1:NCH + 1],
                op=ALU.subtract)
            P.act(brow[:], brow[:], AF.Exp, r=["brow", "zcol"], w=["brow"], bias=zcol[0:2, 0:1])
            for hl in range(2):
                P.mm(pu[:, 0:NCH], sel[:, hl, :], brow[:], r=["sel", "brow", "UT"], w=["pu"])
                P.v("tensor_copy", r=["pu"], w=["betab"], out=betab[:, hl, :], in_=pu[:, 0:NCH])

            qk = [P.sb("qk%d" % i, [128, 4, 512], BF16) for i in range(2)]
            xz = [P.sb("xz%d" % i, [128, 4, 512], BF16) for i in range(2)]
            kv = [P.sb("kv%d" % i, [128, 4, 514], BF16) for i in range(2)]
            SM = [P.sb("SM%d" % i, [128, 128], BF16) for i in range(2)]
            Vp = [P.sb("Vp%d" % i, [128, 258], BF16) for i in range(2)]
            Dst = P.sb("Dst", [128, 2, 257], F32)
            Cb = [P.sb("Cb%d" % i, [128, 2, 258], BF16) for i in range(2)]
            den = P.sb("den", [128, 2], F32)
            hsc = P.sb("hsc", [128, 256], F32)
            bst = P.sb("bst", [128, 6], F32)
            mv = P.sb("mv", [128, 2], F32)
            rstd = P.sb("rstd", [128, 2], F32)
            hn = P.sb("hn", [128, 256], BF16)
            tmp = P.sb("tmp", [128, 2, 128], F32)
            ymst = [P.sb("ymst%d" % i, [128, 2, 512], BF16) for i in range(2)]
            pst = P.ps("pst", [128, 512], F32)
            pn = [P.ps("pn%d" % i, [128, 512], F32) for i in range(2)]
            pdc = [P.ps("pdc%d" % i, [128, 512], F32) for i in range(2)]
            pht = P.ps("pht", [128, 1024], BF16)
            for i in range(2):
                P.v("memset", w=[("kv", i)], ap=kv[i][:, :, 512:514], constant=1.0)

            def load_m(hl, g, par):
                tsl = slice(g * 512, (g + 1) * 512)
                for i in range(2):
                    ti = 2 * hl + i
                    P.dma(qk[par][:, i, :], QmT[ti, :, tsl], w=[("qk", par)], key="qk%d" % par)
                    P.dma(qk[par][:, 2 + i, :], KmT[ti, :, tsl], w=[("qk", par)], key="qk%d" % par)
                    P.dma(xz[par][:, i, :], XSd[ti, :, tsl], w=[("xz", par)], key="xz%d" % par)
                    P.dma(xz[par][:, 2 + i, :], SZd[ti, :, tsl], w=[("xz", par)], key="xz%d" % par)
                P.dma(kv[par][:, :, 0:256], Kmd[tsl, hl * 256:(hl + 1) * 256].rearrange("(c p) d -> p c d", p=128),
                      w=[("kv", par)], key="kv%d" % par)
                P.dma(kv[par][:, :, 256:512], Vmd[tsl, hl * 256:(hl + 1) * 256].rearrange("(c p) d -> p c d", p=128),
                      w=[("kv", par)], key="kv%d" % par)

            it = 0
            load_m(0, 0, 0)
            for hl in range(2):
                for g in range(NG):
                    par = it % 2
                    it += 1
                    if g + 1 < NG:
                        load_m(hl, g + 1, it % 2)
                    elif hl + 1 < 2:
                        load_m(hl + 1, 0, it % 2)
                    q_k, x_z, k_v, ym = qk[par], xz[par], kv[par], ymst[par]
                    for cc in range(4):
                        c = g * 4 + cc
                        csl = slice(cc * 128, (cc + 1) * 128)
                        sm, vp, cbf = SM[c % 2], Vp[c % 2], Cb[c % 2]
                        pnn, pnt = pn[c % 2], ("pn", c % 2)
                        if c > 0:
                            for dc in range(2):
                                P.act(cbf[:, dc, 0:257], Dst[:, dc, :], AF.Copy, r=[("Dst", dc), "betab"], w=[("Cb", c % 2, dc)],
                                      scale=betab[:, hl, c:c + 1])
                        for i in range(2):
                            P.mm(pst[:, 0:128], q_k[:, 2 + i, csl], q_k[:, i, csl], start=(i == 0), stop=(i == 1),
                                 r=[("qk", par)], w=["pst"])
                        P.v("tensor_tensor", r=["pst", "causb"], w=[("SM", c % 2)], out=sm[:], in0=pst[:, 0:128], in1=causb[:],
                            op=ALU.mult)
                        P.act(vp[:, 0:257], k_v[:, cc, 256:513], AF.Copy, r=[("kv", par), "UT"], w=[("Vp", c % 2)],
                              scale=UT[:, c, 0, hl:hl + 1])
                        P.mm(pnn[:, 0:257], sm[:], vp[:, 0:257], start=True, stop=(c == 0), r=[("SM", c % 2), ("Vp", c % 2)],
                             w=[pnt])
                        if c > 0:
                            for dc in range(2):
                                P.mm(pnn[:, 0:257], q_k[:, dc, csl], cbf[:, dc, 0:257], start=False, stop=(dc == 1),
                                     r=[("qk", par), ("Cb", c % 2, dc)], w=[pnt])
                        for dc in range(2):
                            P.mm(pdc[dc][:, 0:257], k_v[:, cc, dc * 128:(dc + 1) * 128], vp[:, 0:257],
                                 r=[("kv", par), ("Vp", c % 2)], w=[("pdc", dc)])
                            if c == 0:
                                P.v("tensor_copy", r=[("pdc", dc)], w=[("Dst", dc)], out=Dst[:, dc, :], in_=pdc[dc][:, 0:257])
                            else:
                                P.v("scalar_tensor_tensor", r=[("pdc", dc), ("Dst", dc), "betab"], w=[("Dst", dc)],
                                    out=Dst[:, dc, :], in0=Dst[:, dc, :], scalar=betab[:, hl, c:c + 1], in1=pdc[dc][:, 0:257],
                                    op0=ALU.mult, op1=ALU.add)
                        P.act(den[:, 0:1], pnn[:, 256:257], AF.Abs, r=[pnt, "zcol"], w=["den"], bias=zcol[:, 0:1])
                        P.v("tensor_tensor", r=["den", "UT"], w=["den"], out=den[:, 0:1], in0=den[:, 0:1],
                            in1=UT[:, c, 1, hl:hl + 1], op=ALU.max)
                        P.v("reciprocal", r=["den"], w=["den"], out=den[:, 1:2], in_=den[:, 0:1])
                        P.act(hsc[:], pnn[:, 0:256], AF.Copy, r=[pnt, "den"], w=["hsc"], scale=den[:, 1:2])
                        P.v("bn_stats", r=["hsc"], w=["bst"], out=bst[:], in_=hsc[:])
                        P.v("bn_aggr", r=["bst"], w=["mv"], out=mv[:], in_=bst[:])
                        P.act(rstd[:, 0:1], mv[:, 1:2], AF.Sqrt, r=["mv"], w=["rstd"], bias=1e-5)
                        P.v("reciprocal", r=["rstd"], w=["rstd"], out=rstd[:, 1:2], in_=rstd[:, 0:1])
                        P.v("tensor_scalar", r=["hsc", "mv", "rstd"], w=["hn"], out=hn[:], in0=hsc[:], scalar1=mv[:, 0:1],
                            scalar2=rstd[:, 1:2], op0=ALU.subtract, op1=ALU.mult)
                        for i in range(2):
                            P.tr(pht[:, i * 128:(i + 1) * 128], hn[:, i * 128:(i + 1) * 128], identb[:], r=["hn", "identb"],
                                 w=["pht"])
                        for i in range(2):
                            ti = 2 * hl + i
                            P.v("scalar_tensor_tensor", r=["pht", "mhn", ("xz", par)], w=[("tmp", i)], out=tmp[:, i, :],
                                in0=pht[:, i * 128:(i + 1) * 128], scalar=mhn_t[:, ti:ti + 1], in1=x_z[:, i, csl],
                                op0=ALU.mult, op1=ALU.add)
                            P.v("tensor_tensor", r=[("tmp", i), ("xz", par)], w=[("ymst", par)], eng="pool", out=ym[:, i, csl],
                                in0=tmp[:, i, :], in1=x_z[:, 2 + i, csl], op=ALU.mult)
                    for i in range(2):
                        ti = 2 * hl + i
                        P.dma(yt_slice(512 + ti * 128, g), ym[:, i, :],
                              r=[("ymst", par)], w=[("YTm", ti, g)], key="ymst%d" % par, q=STQ)
            P.pop()

        if upto >= 4:
            P.push()
            rg = [[0, 1], [2, 3], [4, 5], [6, 7]]
            for blk in range(NB):
                P.add("pool", (lambda e, blk=blk: e.collective_compute("AllGather", ALU.bypass, replica_groups=rg,
                                                                       ins=[YTsrc[blk].opt()], outs=[YTall[blk].opt()])),
                      w=[("YTall", blk)], dma_key=P.mapkey("cc"), inc=1)
            Wo = P.sb("Wo", [128, 16, D], BF16)
            wos = [P.sb("wos%d" % i, [128, D], F32) for i in range(2)]
            ytl = [P.sb("ytl%d" % i, [128, 16, 512], BF16) for i in range(2)]
            xr = [P.sb("xr%d" % i, [128, 4, D], F32) for i in range(2)]
            ot = [P.sb("ot%d" % i, [128, 4, D], F32) for i in range(2)]
            junk3 = P.sb("junk3", [128, 512], BF16)
            s2 = P.sb("s2", [128, 4], F32)
            py = [[P.ps("py%d_%d" % (i, k), [128, 512], F32) for k in range(2)] for i in range(2)]
            for ck in range(16):
                P.dma(wos[ck % 2][:], w_out[ck * 128:(ck + 1) * 128, :], w=[("wos", ck % 2)], key="wos%d" % (ck % 2))
                P.v("tensor_copy", r=[("wos", ck % 2)], w=[("Wo", ck)], eng=("dve" if ck % 2 else "pool"),
                    out=Wo[:, ck, :], in_=wos[ck % 2][:])
            wo_tok = [("Wo", ck) for ck in range(16)]

            def load3(g):
                blk, off = (g * 512) // TB, (g * 512) % TB
                P.dma(ytl[g % 2][:], YTall[blk, :, off:off + 512].rearrange("(k p) t -> p k t", p=128),
                      r=[("YTall", blk)], w=[("ytl", g % 2)], key="ytl%d" % (g % 2))
                P.dma(xr[g % 2][:], x[g * 512:(g + 1) * 512, :].rearrange("(t p) c -> p t c", p=128),
                      w=[("xr", g % 2)], key="xr%d" % (g % 2))

            load3(0)
            n3 = 0
            for g in range(NG):
                if g + 1 < NG:
                    load3(g + 1)
                yt_, xr_, ot_ = ytl[g % 2], xr[g % 2], ot[g % 2]
                for t in range(4):
                    pp = py[n3 % 2]
                    ptk = [("py", n3 % 2, 0), ("py", n3 % 2, 1)]
                    n3 += 1
                    for hf in range(2):
                        for ck in range(16):
                            P.mm(pp[hf][:], yt_[:, ck, t * 128:(t + 1) * 128], Wo[:, ck, hf * 512:(hf + 1) * 512],
                                 start=(ck == 0), stop=(ck == 15), r=wo_tok + [("ytl", g % 2)] if ck in (0, 15) else (), w=[ptk[hf]])
                        P.act(junk3[:], pp[hf][:], AF.Square, r=[ptk[hf]], w=["junk3", ("s2", hf)], accum_out=s2[:, hf:hf + 1])
                    P.v("tensor_tensor", r=[("s2", 0), ("s2", 1)], w=[("s2", 2)], out=s2[:, 2:3], in0=s2[:, 0:1], in1=s2[:, 1:2],
                        op=ALU.add)
                    P.act(s2[:, 3:4], s2[:, 2:3], AF.Sqrt, r=[("s2", 2)], w=[("s2", 3)], scale=1.0 / D, bias=1e-6)
                    P.v("reciprocal", r=[("s2", 3)], w=[("s2", 3)], out=s2[:, 3:4], in_=s2[:, 3:4])
                    for hf in range(2):
                        hs = slice(hf * 512, (hf + 1) * 512)
                        P.v("scalar_tensor_tensor", r=[ptk[hf], ("s2", 3), "gpost_b"], w=[("ot", g % 2, t, hf)],
                            out=ot_[:, t, hs], in0=pp[hf][:], scalar=s2[:, 3:4], in1=gpost_b[:, hs], op0=ALU.mult, op1=ALU.mult)
                        P.v("tensor_tensor", r=[("ot", g % 2, t, hf), ("xr", g % 2)], w=[("ot", g % 2, t, hf)], eng="pool",
                            out=ot_[:, t, hs], in0=ot_[:, t, hs], in1=xr_[:, t, hs], op=ALU.add)
                P.dma(out[g * 512:(g + 1) * 512, :].rearrange("(t p) c -> p t c", p=128), ot_[:],
                      r=[("ot", g % 2, t, hf) for t in range(4) for hf in range(2)], w=[("out", g)], key="ot%d" % (g % 2))
            P.pop()
        P.emit()
    return nc, P


def _t5_bucket_np(dist):
    n = np.maximum(dist, 0)
    max_exact = 16
    nf = np.maximum(n, 1).astype(np.float32)
    large = max_exact + (np.log(nf / np.float32(max_exact)) / np.float32(math.log(2048 / max_exact))
                         * np.float32(32 - max_exact)).astype(np.int32)
    large = np.minimum(large, 31)
    return np.where(n < max_exact, n, large)


_CONSTS = {}


def _consts():
    if _CONSTS:
        return _CONSTS
    bf = ml_dtypes.bfloat16
    _CONSTS["identb"] = np.eye(128, dtype=np.float32).astype(bf)
    _CONSTS["identf"] = np.eye(128, dtype=np.float32)
    s = np.arange(128)
    _CONSTS["causb"] = (s[:, None] <= s[None, :]).astype(np.float32).astype(bf)
    e = np.zeros((32, 32, 128), np.float32)
    for n in range(32):
        e[n, n, :] = 1.0
    _CONSTS["emat"] = e.reshape(32, 32 * 128).astype(bf)
    own = np.arange(32)[:, None]
    nn = np.arange(32)[None, :]
    ng = np.where(nn < own, 0.0, -1e30).astype(np.float32)
    _CONSTS["negm"] = np.ascontiguousarray(np.broadcast_to(ng[None], (128, 32, 32))).reshape(128, 1024)
    sel = np.zeros((2, 2, 128), np.float32)
    sel[0, 0, :] = 1.0
    sel[1, 1, :] = 1.0
    _CONSTS["sel"] = sel.reshape(2, 256)
    k = np.arange(128)[:, None]
    xx = np.arange(XW)[None, :]
    dist = xx - k - 384
    _CONSTS["bucket"] = _t5_bucket_np(dist)
    _CONSTS["neg"] = dist < 0
    return _CONSTS


def core_inputs(inp, b, hh, S):
    C = _consts()
    f32 = np.float32
    w_in = np.asarray(inp["w_in"][0], f32)
    heads = [4 * hh + i for i in range(4)]
    ch_order = np.concatenate([np.arange(512 * hh, 512 * hh + 512), np.arange(512 * (1 - hh), 512 * (1 - hh) + 512)])

    def hcols(base):
        return np.concatenate([w_in[:, base + 128 * h: base + 128 * h + 128] for h in heads], axis=1)

    w_in_c = np.concatenate([hcols(0), hcols(1024), w_in[:, 4096 + ch_order], w_in[:, 5120 + ch_order[:512]],
                             hcols(2048), hcols(3072)], axis=1)
    d = {"x": np.ascontiguousarray(np.asarray(inp["x"][b, :S], f32)),
         "gpre": np.asarray(inp["g_pre"], f32).reshape(1, D),
         "gpost": np.asarray(inp["g_post"], f32).reshape(1, D),
         "w_in": np.ascontiguousarray(w_in_c)}
    w_out = np.asarray(inp["w_out"][0], f32)
    rows = []
    for r in range(2):
        rows.append(w_out[512 * r:512 * r + 512])
        rows.append(w_out[1024 + 512 * r:1024 + 512 * r + 512])
    d["w_out"] = np.ascontiguousarray(np.concatenate(rows, axis=0))
    rel = np.asarray(inp["rel_bias"], f32)
    b2 = np.empty((4, 128, XW), f32)
    for i, h in enumerate(heads):
        t = rel[:, h][C["bucket"]]
        t = np.where(C["neg"], np.float32(-30000.0), t)
        b2[i] = t
    d["b2"] = b2
    d["bfar"] = np.ascontiguousarray(np.broadcast_to(rel[31, heads][None, :], (128, 4))).astype(f32)
    conv_w = np.asarray(inp["conv_w"][0], f32)[:, ch_order]
    d["cw"] = np.ascontiguousarray(conv_w.reshape(4, 8, 128).transpose(2, 1, 0)).reshape(128, 32)
    d["cb"] = np.ascontiguousarray(np.asarray(inp["conv_b"][0], f32)[ch_order].reshape(8, 128).T)
    blk_order = ch_order.reshape(256, 4)[:, 0] // 4
    bd = np.zeros((3, 8, 128, 128), f32)
    for wi, nm in enumerate(("wq_m", "wk_m", "wv_m")):
        w = np.asarray(inp[nm][0], f32)[blk_order]
        for ti in range(8):
            for nl in range(32):
                bd[wi, ti, 4 * nl:4 * nl + 4, 4 * nl:4 * nl + 4] = w[32 * ti + nl]
    d["bd"] = np.ascontiguousarray(bd[:, 0:4].reshape(12, 128, 128).transpose(1, 0, 2)).reshape(128, 12 * 128)
    d["bdT"] = np.ascontiguousarray(bd.transpose(0, 1, 3, 2).reshape(24, 128, 128).transpose(1, 0, 2)).reshape(128, 24 * 128)
    w_if = np.asarray(inp["w_if"][0], f32)
    gcols = [2 * hh, 2 * hh + 1, 4 + 2 * hh, 4 + 2 * hh + 1]
    wif = np.stack([w_if[1024 * wi + ch_order][:, gcols] for wi in range(3)], axis=0)
    d["wif"] = np.ascontiguousarray(wif.reshape(3, 8, 128, 4).transpose(2, 0, 1, 3)).reshape(128, 96)
    b_if = np.asarray(inp["b_if"][0], f32)
    d["bg"] = np.ascontiguousarray(np.stack([b_if[gcols[0:2]], b_if[gcols[2:4]]], axis=1))
    d["mhn"] = np.ascontiguousarray(np.asarray(inp["mh_norm"][0], f32)[ch_order[:512]].reshape(4, 128).T)
    d["skp"] = np.ascontiguousarray(np.asarray(inp["skip"][0], f32)[ch_order[:512]].reshape(4, 128).T)
    for k in ("identb", "identf", "causb", "emat", "negm", "sel"):
        d[k] = C[k]
    return d


_PROG_CACHE = {}


def run(inputs, S, dbg=()):
    key = (S, tuple(dbg))
    if key not in _PROG_CACHE:
        _PROG_CACHE[key] = build_program(S, dbg)[0]
    nc = _PROG_CACHE[key]
    in_maps = [core_inputs(inputs, c // 2, c % 2, S) for c in range(8)]
    res = run_bass_kernel_spmd(nc, in_maps, core_ids=list(range(8)))
    return res.results


def kernel(**inputs):
    S = inputs["x"].shape[1]
    res = run(inputs, S)
    outp = np.empty((4, S, D), np.float32)
    for b in range(4):
        outp[b, :S // 2] = res[2 * b]["out"][:S // 2]
        outp[b, S // 2:] = res[2 * b + 1]["out"][S // 2:]
    return outp
```

```python
import contextlib
import math
import numpy as np
import ml_dtypes
import concourse.bass as bass
import concourse.mybir as mybir
from concourse.bass_utils import run_bass_kernel_spmd

F32 = mybir.dt.float32
BF16 = mybir.dt.bfloat16
AF = mybir.ActivationFunctionType
ALU = mybir.AluOpType
AX = mybir.AxisListType

D = 1024
NCOL = 3584
XW = 2432
SCALE = 128.0 ** -0.5
ENGS = ["pe", "act", "dve", "pool", "sp"]
STQ = "sp"
import os
CUT = int(os.environ.get("K_CUT", "0"))


class Op:
    __slots__ = ("eng", "fn", "deps", "dma_key", "signal", "event", "inc")

    def __init__(self, eng, fn, deps, dma_key, inc):
        self.eng = eng
        self.fn = fn
        self.deps = deps
        self.dma_key = dma_key
        self.signal = False
        self.event = None
        self.inc = inc


class Prog:
    def __init__(self, nc):
        self.nc = nc
        self.ops = []
        self.last_w = {}
        self.readers = {}
        self.root = contextlib.ExitStack()
        self.scopes = [self.root]
        self.last_eng = {}
        self.last_key = {}
        self.rr = 0
        self.keymap = {}

    def sb(self, name, shape, dt):
        return self.scopes[-1].enter_context(self.nc.sbuf_tensor("s_" + name, list(shape), dt))

    def ps(self, name, shape, dt=F32):
        return self.scopes[-1].enter_context(self.nc.psum_tensor("p_" + name, list(shape), dt))

    def push(self):
        st = contextlib.ExitStack()
        self.scopes.append(st)
        return st

    def pop(self):
        self.barrier()
        self.scopes.pop().close()

    def add(self, eng, fn, r=(), w=(), dma_key=None, inc=16, deps=None):
        if deps is None:
            deps = set()
            for t in r:
                x = self.last_w.get(t)
                if x is not None:
                    deps.add(x)
            for t in w:
                x = self.last_w.get(t)
                if x is not None:
                    deps.add(x)
                for y in self.readers.get(t, ()):
                    deps.add(y)
            if eng == "pe":
                deps = {d for d in deps if not (self.ops[d].eng == "pe" and self.ops[d].dma_key is None)}
            best = {}
            keep = set()
            for d in deps:
                o = self.ops[d]
                if o.dma_key is not None:
                    keep.add(d)
                elif best.get(o.eng, -1) < d:
                    best[o.eng] = d
            deps = keep | set(best.values())
        idx = len(self.ops)
        self.ops.append(Op(eng, fn, sorted(deps), dma_key, inc))
        for t in r:
            lst = self.readers.setdefault(t, [])
            if dma_key is None:
                lst[:] = [y for y in lst if not (self.ops[y].eng == eng and self.ops[y].dma_key is None)]
            lst.append(idx)
        for t in w:
            self.last_w[t] = idx
            self.readers[t] = []
        if dma_key is None:
            self.last_eng[eng] = idx
        else:
            self.last_key[dma_key] = idx
        return idx

    def barrier(self):
        deps = set(self.last_eng.values()) | set(self.last_key.values())
        for e in ENGS:
            self.add(e, lambda h: h.nop(), deps=set(deps))
        self.last_w = {}
        self.readers = {}
        self.keymap = {}

    def dma(self, out, in_, r=(), w=(), key=None, q="sp"):
        return self.add(q, lambda e: e.dma_start(out=out, in_=in_), r, w, dma_key=self.mapkey(key))

    def mapkey(self, key):
        if key not in self.keymap:
            self.keymap[key] = "d%d" % len(self.keymap)
        return self.keymap[key]

    def mm(self, out, lhsT, rhs, start=True, stop=True, r=(), w=()):
        return self.add("pe", lambda e: e.matmul(out, lhsT, rhs, start=start, stop=stop), r, w)

    def tr(self, out, in_, ident, r=(), w=()):
        return self.add("pe", lambda e: e.transpose(out, in_, ident), r, w)

    def act(self, out, in_, func, r=(), w=(), **kw):
        return self.add("act", lambda e: e.activation(out, in_, func, **kw), r, w)

    def v(self, name, r=(), w=(), eng="dve", **kw):
        return self.add(eng, lambda e: getattr(e, name)(**kw), r, w)

    def evac(self, out, in_, r=(), w=(), scale=None, eng=None):
        if eng is None:
            self.rr += 1
            eng = "act" if self.rr % 2 else "dve"
        if eng == "act":
            if scale is None:
                return self.act(out, in_, AF.Copy, r, w)
            return self.act(out, in_, AF.Copy, r, w, scale=float(scale))
        if scale is None:
            return self.v("tensor_copy", r, w, out=out, in_=in_)
        return self.v("tensor_scalar", r, w, out=out, in0=in_, scalar1=float(scale), scalar2=None, op0=ALU.mult)

    def emit(self):
        nc = self.nc
        ops = self.ops
        for o in ops:
            for d in o.deps:
                ops[d].signal = True
        esem = {e: self.root.enter_context(nc.semaphore("es_" + e)) for e in ENGS}
        dsem = {}
        cnt = {e: 0 for e in ENGS}
        dcnt = {}
        for o in ops:
            if o.dma_key is not None:
                if o.dma_key not in dsem:
                    dsem[o.dma_key] = self.root.enter_context(nc.semaphore("ds_%d" % len(dsem)))
                    dcnt[o.dma_key] = 0
                dcnt[o.dma_key] += o.inc
                o.event = (dsem[o.dma_key], dcnt[o.dma_key])
            elif o.signal:
                cnt[o.eng] += 1
                o.event = (esem[o.eng], cnt[o.eng])
        self.nsem = len(dsem) + len(ENGS)
        self.cnt = cnt
        per = {e: [o for o in ops if o.eng == e] for e in ENGS}

        def run(ename, handle):
            seen = {}
            for o in per[ename]:
                need = {}
                for d in o.deps:
                    sem, val = ops[d].event
                    k = id(sem)
                    if need.get(k, (None, 0))[1] < val:
                        need[k] = (sem, val)
                for k, (sem, val) in need.items():
                    if seen.get(k, 0) < val:
                        handle.wait_ge(sem, val)
                        seen[k] = val
                ins = o.fn(handle)
                if o.dma_key is not None:
                    ins.then_inc(o.event[0], o.inc)
                elif o.signal:
                    ins.then_inc(o.event[0], 1)

        with nc.Block() as block:
            @block.tensor
            def _(e):
                run("pe", e)

            @block.scalar
            def _(e):
                run("act", e)

            @block.vector
            def _(e):
                run("dve", e)

            @block.gpsimd
            def _(e):
                run("pool", e)

            @block.sync
            def _(e):
                run("sp", e)


def dram_bcast(ap, nparts, n):
    return bass.AP(ap.tensor, ap.offset, [[0, nparts], [1, n]])


def build_program(S, dbg=(), upto=4):
    NG = S // 512
    NT = S // 128
    NCH = S // 128
    nc = bass.Bass("TRN2", target_bir_lowering=False)

    def din(name, shape, dt=F32):
        return nc.dram_tensor(name, list(shape), dt, kind="ExternalInput").ap()

    def dscr(name, shape, dt=BF16):
        kind = "ExternalOutput" if name in dbg else "Internal"
        return nc.dram_tensor(name, list(shape), dt, kind=kind).ap()

    x = din("x", [S, D])
    gpre = din("gpre", [1, D])
    gpost = din("gpost", [1, D])
    w_in = din("w_in", [D, NCOL])
    w_out = din("w_out", [2 * D, D])
    b2 = din("b2", [4, 128, XW])
    bfar = din("bfar", [128, 4])
    cw = din("cw", [128, 32])
    cb = din("cb", [128, 8])
    bd = din("bd", [128, 3 * 4 * 128])
    bdT = din("bdT", [128, 3 * 8 * 128])
    wif = din("wif", [128, 3 * 8 * 4])
    bg = din("bg", [2, 2])
    mhn = din("mhn", [128, 4])
    skp = din("skp", [128, 4])
    identb_d = din("identb", [128, 128], BF16)
    identf_d = din("identf", [128, 128])
    causb_d = din("causb", [128, 128], BF16)
    emat_d = din("emat", [32, 32 * 128], BF16)
    negm_d = din("negm", [128, 32 * 32])
    sel_d = din("sel", [2, 2 * 128])
    out = nc.dram_tensor("out", [S, D], F32, kind="ExternalOutput").ap()

    QT = dscr("QT", [4, 128, S])
    KT = dscr("KT", [4, 128, S])
    Vd = dscr("Vd", [S, 512])
    SGd = dscr("SGd", [S, 512])
    QmT = dscr("QmT", [4, 128, S])
    KmT = dscr("KmT", [4, 128, S])
    Kmd = dscr("Kmd", [S, 512])
    Vmd = dscr("Vmd", [S, 512])
    XSd = dscr("XSd", [4, 128, S])
    SZd = dscr("SZd", [4, 128, S])
    GId = dscr("GId", [2, S], F32)
    GFd = dscr("GFd", [2, S], F32)
    TB = min(S, 1024)
    NB = S // TB
    YTsrc = nc.dram_tensor("YTsrc", [NB, D, TB], BF16).ap()
    YTall = nc.dram_tensor("YTall", [NB, 2 * D, TB], BF16).ap()

    def yt_slice(row0, g):
        blk, off = (g * 512) // TB, (g * 512) % TB
        return YTsrc[blk, row0:row0 + 128, off:off + 512]
    YTdbg = dscr("YTdbg", [D, S]) if "YTdbg" in dbg else None

    P = Prog(nc)
    with P.root:
        identb = P.sb("identb", [128, 128], BF16)
        identf = P.sb("identf", [128, 128], F32)
        causb = P.sb("causb", [128, 128], BF16)
        gpre_b = P.sb("gpre_b", [128, D], F32)
        gpost_b = P.sb("gpost_b", [128, D], F32)
        kms = P.sb("kms", [128, 4, 32], F32)
        bfar_t = P.sb("bfar_t", [128, 4], F32)
        mhn_t = P.sb("mhn_t", [128, 4], F32)
        skp_t = P.sb("skp_t", [128, 4], F32)
        cb_t = P.sb("cb_t", [128, 8], F32)
        bg_t = P.sb("bg_t", [2, 2], F32)
        P.dma(identb[:], identb_d, w=["identb"], key="c0")
        P.dma(identf[:], identf_d, w=["identf"], key="c1")
        P.dma(causb[:], causb_d, w=["causb"], key="c2")
        P.dma(gpre_b[:], dram_bcast(gpre, 128, D), w=["gpre_b"], key="c3")
        P.dma(gpost_b[:], dram_bcast(gpost, 128, D), w=["gpost_b"], key="c4")
        P.dma(bfar_t[:], bfar, w=["bfar"], key="c5")
        P.dma(mhn_t[:], mhn, w=["mhn"], key="c6")
        P.dma(skp_t[:], skp, w=["skp"], key="c7")
        P.dma(cb_t[:], cb, w=["cb"], key="c8")
        P.dma(bg_t[:], bg, w=["bg"], key="c9")
        P.v("memset", w=["kms"], ap=kms[:], constant=0.0)
        zcol = P.sb("zcol", [128, 2], F32)
        P.v("memset", w=["zcol"], ap=zcol[:, 0:1], constant=0.0)
        P.v("memset", w=["zcol"], ap=zcol[:, 1:2], constant=1.0)
        P.barrier()

        if upto >= 1:
            P.push()
            Wb = P.sb("Wb", [128, 8, NCOL], BF16)
            Dg = P.sb("Dg", [128, 8, 4, 128], BF16)
            bd_b = P.sb("bd_b", [128, 12, 128], BF16)
            Wg = P.sb("Wg", [128, 8, 2, 2, 128], BF16)
            pm = [P.ps("pm%d" % i, [128, 512], F32) for i in range(4)]
            P.push()
            wst = [P.sb("wst%d" % i, [128, NCOL], F32) for i in range(2)]
            cw_t = P.sb("cw_t", [128, 8, 4], F32)
            bd_f = P.sb("bd_f", [128, 12, 128], F32)
            bdT_f = P.sb("bdT_f", [128, 24, 128], F32)
            wif_t = P.sb("wif_t", [128, 24, 4], F32)
            P.dma(cw_t[:], cw.rearrange("p (t j) -> p t j", j=4), w=["cw"], key="c10")
            P.dma(bd_f[:], bd.rearrange("p (k c) -> p k c", c=128), w=["bd_f"], key="c11")
            P.dma(bdT_f[:], bdT.rearrange("p (k c) -> p k c", c=128), w=["bdT_f"], key="c12")
            P.dma(wif_t[:], wif.rearrange("p (k c) -> p k c", c=4), w=["wif"], key="c13")
            for c in range(8):
                P.dma(wst[c % 2][:], w_in[c * 128:(c + 1) * 128, :], w=[("wst", c % 2)], key="wst%d" % (c % 2))
                third = NCOL // 4
                P.v("tensor_copy", r=[("wst", c % 2)], w=[("Wb", c, 0)], out=Wb[:, c, 0:2 * third], in_=wst[c % 2][:, 0:2 * third])
                P.v("tensor_copy", r=[("wst", c % 2)], w=[("Wb", c, 1)], eng="pool", out=Wb[:, c, 2 * third:NCOL],
                    in_=wst[c % 2][:, 2 * third:NCOL])
            wb_tok = [("Wb", c, k) for c in range(8) for k in range(2)]
            P.v("tensor_copy", r=["bd_f"], w=["bd_b"], out=bd_b[:], in_=bd_f[:])
            for ti in range(8):
                for j in range(4):
                    P.v("tensor_scalar", r=["identf", "cw"], w=["Dg"], out=Dg[:, ti, j, :], in0=identf[:],
                        scalar1=cw_t[:, ti, j:j + 1], scalar2=None, op0=ALU.mult)
            P.v("memset", w=["Wg"], eng="pool", ap=Wg[:], constant=0.0)
            for ti in range(8):
                pw = pm[ti % 4]
                P.mm(pw[:, 0:4], bdT_f[:, 0 * 8 + ti, :], wif_t[:, 0 * 8 + ti, :], start=True, stop=False,
                     r=["bdT_f", "wif"], w=[("pm", ti % 4)])
                P.mm(pw[:, 0:4], bdT_f[:, 1 * 8 + ti, :], wif_t[:, 1 * 8 + ti, :], start=False, stop=True,
                     r=["bdT_f", "wif"], w=[("pm", ti % 4)])
                P.mm(pw[:, 8:12], bdT_f[:, 2 * 8 + ti, :], wif_t[:, 2 * 8 + ti, :], start=True, stop=True,
                     r=["bdT_f", "wif"], w=[("pm", ti % 4)])
                for gq in range(2):
                    P.v("tensor_copy", r=[("pm", ti % 4)], w=["Wg"], out=Wg[:, ti, 0, gq, 0:2], in_=pw[:, 2 * gq:2 * gq + 2])
                    P.v("tensor_copy", r=[("pm", ti % 4)], w=["Wg"], out=Wg[:, ti, 1, gq, 0:2], in_=pw[:, 8 + 2 * gq:10 + 2 * gq])
            P.pop()
            xbuf = [P.sb("xbuf%d" % i, [128, 4, D], F32) for i in range(2)]
            junk = P.sb("junk", [128, D], BF16)
            ss = [P.sb("ss%d" % i, [128, 4], F32) for i in range(2)]
            rs = [P.sb("rs%d" % i, [128, 4], F32) for i in range(2)]
            hb = P.sb("hb", [128, 4, D], BF16)
            hT = [P.sb("hT%d" % i, [128, 8, 512], BF16) for i in range(2)]
            XM = [P.sb("XM%d" % i, [128, 8, 516], BF16) for i in range(2)]
            XC = P.sb("XC", [128, 8, 512], BF16)
            NST = 6
            stg = [P.sb("stg%d" % i, [128, 512], BF16) for i in range(NST)]
            tst = [P.sb("tst%d" % i, [128, 4, 512], BF16) for i in range(4)]
            gst = [P.sb("gst%d" % i, [2, 512], F32) for i in range(2)]
            pt = [P.ps("pt%d" % i, [128, 1024], BF16) for i in range(2)]
            pgi = P.ps("pgi", [128, 512], F32)
            pgf = P.ps("pgf", [128, 512], F32)

            P.v("memset", w=[("XM", 0), ("XM", 1)], ap=XM[0][:, :, 0:4], constant=0.0)

            sctr = [0]
            pctr = [0]

            def next_pm():
                pctr[0] += 1
                i = pctr[0] % 4
                return pm[i], ("pm", i)

            def next_stg():
                sctr[0] += 1
                i = sctr[0] % NST
                return stg[i], ("stg", i), "stg%d" % i

            def load_x(g):
                P.dma(xbuf[g % 2][:], x[g * 512:(g + 1) * 512, :].rearrange("(t p) c -> p t c", p=128),
                      w=[("x", g % 2)], key="x%d" % (g % 2))

            load_x(0)
            for g in range(NG if CUT != 1 else 0):
                if g + 1 < NG:
                    load_x(g + 1)
                xb = xbuf[g % 2]
                tsl = slice(g * 512, (g + 1) * 512)
                sq, rq = ss[g % 2], rs[g % 2]
                for t in range(4):
                    P.act(junk[:], xb[:, t, :], AF.Square, r=[("x", g % 2)], w=["junk", ("ss", g % 2)],
                          accum_out=sq[:, t:t + 1])
                P.act(rq[:], sq[:], AF.Sqrt, r=[("ss", g % 2)], w=[("rs", g % 2)], scale=1.0 / D, bias=1e-6)
                P.v("reciprocal", r=[("rs", g % 2)], w=[("rs", g % 2)], out=rq[:], in_=rq[:])
                for t in range(4):
                    P.v("scalar_tensor_tensor", r=[("rs", g % 2), ("x", g % 2), "gpre_b"], w=[("hb", t)],
                        out=hb[:, t, :], in0=xb[:, t, :], scalar=rq[:, t:t + 1], in1=gpre_b[:], op0=ALU.mult, op1=ALU.mult)
                h_T = hT[g % 2]
                for c in range(8):
                    ptt = pt[c % 2]
                    for t in range(4):
                        P.tr(ptt[:, t * 128:(t + 1) * 128], hb[:, t, c * 128:(c + 1) * 128], identb[:],
                             r=[("hb", t), "identb"], w=[("pt", c % 2)])
                    P.evac(h_T[:, c, :], ptt[:, 0:512], r=[("pt", c % 2)], w=[("hT", g % 2, c)])
                hT_tok = [("hT", g % 2, c) for c in range(8)]

                def chan_tile(i):
                    pw, ptok = next_pm()
                    for c in range(8):
                        P.mm(pw[:], Wb[:, c, i * 128:(i + 1) * 128], h_T[:, c, :], start=(c == 0), stop=(c == 7),
                             r=wb_tok + hT_tok if c in (0, 7) else (), w=[ptok])
                    return pw, ptok

                xm = XM[g % 2]
                xmp = XM[(g + 1) % 2]
                for ti in range(8):
                    pw, ptok = chan_tile(8 + ti)
                    P.evac(xm[:, ti, 4:516], pw[:], r=[ptok], w=[("XM", g % 2, ti)])
                    if g > 0:
                        P.v("tensor_copy", r=[("XM", (g + 1) % 2, ti)], w=[("XM", g % 2, ti)], eng="pool",
                            out=xm[:, ti, 0:4], in_=xmp[:, ti, 512:516])
                    pc, ctok = next_pm()
                    for j in range(4):
                        P.mm(pc[:], Dg[:, ti, j, :], xm[:, ti, 1 + j:1 + j + 512], start=(j == 0), stop=(j == 3),
                             r=["Dg", ("XM", g % 2, ti)], w=[ctok])
                    P.act(XC[:, ti, :], pc[:], AF.Silu, r=[ctok, "cb"], w=[("XC", ti)], bias=cb_t[:, ti:ti + 1])
                if CUT == 2:
                    continue
                for (pg, c0, nm) in ((pgi, 0, "pgi"), (pgf, 2, "pgf")):
                    for ti in range(8):
                        P.mm(pg[:], Wg[:, ti, 0, c0 // 2, :], XC[:, ti, :], start=(ti == 0), stop=False,
                             r=["Wg", ("XC", ti)], w=[nm])
                        P.mm(pg[:], Wg[:, ti, 1, c0 // 2, :], xm[:, ti, 4:516], start=False, stop=(ti == 7),
                             r=["Wg", ("XM", g % 2, ti)], w=[nm])
                P.act(gst[0][:], pgi[0:2, :], AF.Identity, r=["pgi", "bg"], w=["gst0"], bias=bg_t[:, 0:1])
                P.dma(GId[:, tsl], gst[0][:], r=["gst0"], w=[("GI", g)], key="gst0", q=STQ)
                P.act(gst[1][:], pgf[0:2, :], AF.Identity, r=["pgf", "bg"], w=["gst1"], bias=bg_t[:, 1:2])
                P.dma(GFd[:, tsl], gst[1][:], r=["gst1"], w=[("GF", g)], key="gst1", q=STQ)
                if CUT == 3:
                    continue
                for ti in range(4):
                    pw, ptok = next_pm()
                    P.mm(pw[:], bd_b[:, 0 * 4 + ti, :], XC[:, ti, :], r=["bd_b", ("XC", ti)], w=[ptok])
                    st, stok, skey = next_stg()
                    P.evac(st[:], pw[:], r=[ptok], w=[stok])
                    P.dma(QmT[ti, :, tsl], st[:], r=[stok], w=[("QmT", ti, g)], key=skey, q=STQ)
                    pw, ptok = next_pm()
                    P.mm(pw[:], bd_b[:, 1 * 4 + ti, :], XC[:, ti, :], r=["bd_b", ("XC", ti)], w=[ptok])
                    st, stok, skey = next_stg()
                    P.evac(st[:], pw[:], r=[ptok], w=[stok], scale=1.0 / 16)
                    P.dma(KmT[ti, :, tsl], st[:], r=[stok], w=[("KmT", ti, g)], key=skey, q=STQ)
                    st, stok, skey = next_stg()
                    P.v("tensor_scalar", r=[("XC", ti), "skp"], w=[stok], eng="pool", out=st[:], in0=XC[:, ti, :],
                        scalar1=skp_t[:, ti:ti + 1], scalar2=None, op0=ALU.mult)
                    P.dma(XSd[ti, :, tsl], st[:], r=[stok], w=[("XS", ti, g)], key=skey, q=STQ)
                for (which, tsi, dst, nm) in ((1, 0, Kmd, "Km"), (2, 1, Vmd, "Vm")):
                    tb = tst[tsi]
                    for t in range(4):
                        pw, ptok = next_pm()
                        for ti in range(4):
                            if which == 1:
                                lhsT = XC[:, ti, t * 128:(t + 1) * 128]
                                rr = [("XC", ti)]
                            else:
                                lhsT = xm[:, ti, 4 + t * 128:4 + (t + 1) * 128]
                                rr = [("XM", g % 2, ti)]
                            P.mm(pw[:, ti * 128:(ti + 1) * 128], lhsT, bd_b[:, which * 4 + ti, :],
                                 r=["bd_b"] + rr, w=[ptok])
                        P.evac(tb[:, t, :], pw[:], r=[ptok], w=[("tst", tsi)], scale=(1.0 / 16 if which == 1 else None))
                    P.dma(dst[tsl, :].rearrange("(t p) c -> p t c", p=128), tb[:], r=[("tst", tsi)], w=[(nm, g)],
                          key="tst%d" % tsi, q=STQ)
                if CUT == 4:
                    continue
                for ti in range(4):
                    pw, ptok = chan_tile(16 + ti)
                    st, stok, skey = next_stg()
                    P.act(st[:], pw[:], AF.Silu, r=[ptok, "zcol"], w=[stok], bias=zcol[:, 0:1])
                    P.dma(SZd[ti, :, tsl], st[:], r=[stok], w=[("SZ", ti, g)], key=skey, q=STQ)
                for h in range(4):
                    pw, ptok = chan_tile(h)
                    st, stok, skey = next_stg()
                    P.evac(st[:], pw[:], r=[ptok], w=[stok])
                    P.dma(QT[h, :, tsl], st[:], r=[stok], w=[("QT", h, g)], key=skey, q=STQ)
                for h in range(4):
                    pw, ptok = chan_tile(4 + h)
                    st, stok, skey = next_stg()
                    for bb in range(2):
                        P.act(st[:, bb * 256:(bb + 1) * 256], pw[:, bb * 256:(bb + 1) * 256], AF.Copy, r=[ptok],
                              w=[stok, "kms"], accum_out=kms[:, h, 2 * g + bb:2 * g + bb + 1])
                    P.dma(KT[h, :, tsl], st[:], r=[stok], w=[("KT", h, g)], key=skey, q=STQ)
                for (c0, tsi, dst, nm, silu) in ((2560, 2, Vd, "V", False), (3072, 3, SGd, "SG", True)):
                    tb = tst[tsi]
                    for t in range(4):
                        pw, ptok = next_pm()
                        for c in range(8):
                            P.mm(pw[:], h_T[:, c, t * 128:(t + 1) * 128], Wb[:, c, c0:c0 + 512], start=(c == 0),
                                 stop=(c == 7), r=wb_tok + hT_tok if c in (0, 7) else (), w=[ptok])
                        if silu:
                            P.act(tb[:, t, :], pw[:], AF.Silu, r=[ptok, "zcol"], w=[("tst", tsi)], bias=zcol[:, 0:1])
                        else:
                            P.evac(tb[:, t, :], pw[:], r=[ptok], w=[("tst", tsi)])
                    P.dma(dst[tsl, :].rearrange("(t p) c -> p t c", p=128), tb[:], r=[("tst", tsi)], w=[(nm, g)],
                          key="tst%d" % tsi, q=STQ)
            P.pop()

        if upto >= 2:
            P.push()
            emat = P.sb("emat", [128, 32, 128], BF16)
            negm = P.sb("negm", [128, 32, 32], F32)
            kmb = P.sb("kmb", [128, 4, 32], BF16)
            KTs = [P.sb("KTs%d" % i, [128, S], BF16) for i in range(2)]
            Va = [P.sb("Va%d" % i, [128, NT, 130], BF16) for i in range(2)]
            b2s = P.sb("b2s", [128, XW], F32)
            EB = [P.sb("EB%d" % i, [128, XW], BF16) for i in range(2)]
            QTg = [P.sb("QTg%d" % i, [128, 512], BF16) for i in range(2)]
            SGg = [P.sb("SGg%d" % i, [128, 4, 128], BF16) for i in range(2)]
            gm = P.sb("gm", [128, 4, 32], F32)
            top8 = P.sb("top8", [128, 4, 8], F32)
            Mb = P.sb("Mb", [128, 4, 32], BF16)
            MT = [P.sb("MT%d" % i, [128, 512], BF16) for i in range(2)]
            NPT = 4
            PT = [P.sb("PT%d" % i, [128, 512], BF16) for i in range(NPT)]
            rc = P.sb("rc", [128, 4], F32)
            ya = P.sb("ya", [128, 4, 128], BF16)
            yst = [P.sb("yst%d" % i, [128, 512], BF16) for i in range(2)]
            NPS = 3
            psc = [P.ps("psc%d" % i, [128, 512], F32) for i in range(NPS)]
            po = [P.ps("po%d" % i, [128, 512], F32) for i in range(4)]
            pmisc = P.ps("pmisc", [128, 1024], BF16)
            P.v("memset", w=["emat"], eng="pool", ap=emat[:], constant=0.0)
            P.dma(emat[0:32, :, :], emat_d.rearrange("p (n k) -> p n k", k=128), r=["emat"], w=["emat"], key="a0")
            for i in range(2):
                P.v("memset", w=[("MT", i)], ap=MT[i][:], constant=0.0)
            P.dma(negm[:], negm_d.rearrange("p (o n) -> p o n", n=32), w=["negm"], key="a1")
            P.v("tensor_copy", r=["kms"], w=["kmb"], out=kmb[:], in_=kms[:])
            for i in range(2):
                P.v("memset", w=[("Va", i)], ap=Va[i][:, :, 128:130], constant=1.0)

            def load_head(h):
                P.dma(KTs[h % 2][:], KT[h], w=[("KTs", h % 2)], key="KTs%d" % (h % 2))
                for q4 in range(4):
                    t0, t1 = q4 * NT // 4, (q4 + 1) * NT // 4
                    P.dma(Va[h % 2][:, t0:t1, 0:128],
                          Vd[t0 * 128:t1 * 128, h * 128:(h + 1) * 128].rearrange("(t p) c -> p t c", p=128),
                          w=[("Va", h % 2)], key="Va%d" % (h % 2))
                P.dma(b2s[:], b2[h], w=["b2s"], key="b2s")
                P.act(EB[h % 2][:], b2s[:], AF.Copy, r=["b2s"], w=[("EB", h % 2)], scale=1.0 / SCALE)

            def load_qg(h, j, par):
                P.dma(QTg[par][:], QT[h, :, j * 512:(j + 1) * 512], w=[("QTg", par)], key="QTg%d" % par)
                P.dma(SGg[par][:], SGd[j * 512:(j + 1) * 512, h * 128:(h + 1) * 128].rearrange("(t p) c -> p t c", p=128),
                      w=[("SGg", par)], key="SGg%d" % par)

            gsc = pmisc[:, 0:256].bitcast(F32)
            groups = [(h, j) for h in range(4) for j in range(NG)]

            def prologue(gi):
                h, j = groups[gi]
                par = gi % 2
                qg, mt = QTg[par], MT[par]
                for t in range(4):
                    P.mm(gsc[:, t * 32:(t + 1) * 32], qg[:, t * 128:(t + 1) * 128], kmb[:, h, :],
                         r=[("QTg", par), "kmb"], w=["pmisc"])
                for t in range(4):
                    own = 2 * j + t // 2
                    P.v("tensor_tensor", r=["pmisc", "negm"], w=[("gm", t)], out=gm[:, t, :],
                        in0=gsc[:, t * 32:(t + 1) * 32], in1=negm[:, own, :], op=ALU.add)
                for t in range(4):
                    own = 2 * j + t // 2
                    P.v("max", r=[("gm", t)], w=[("top8", t)], out=top8[:, t, :], in_=gm[:, t, :])
                    P.v("tensor_scalar", r=[("gm", t), ("top8", t)], w=[("Mb", t)], out=Mb[:, t, :], in0=gm[:, t, :],
                        scalar1=top8[:, t, 2:3], scalar2=-30000.0, op0=ALU.is_lt, op1=ALU.mult)
                    P.v("memset", r=[], w=[("Mb", t)], ap=Mb[:, t, own:own + 1], constant=0.0)
                for half in range(2):
                    for t in (2 * half, 2 * half + 1):
                        P.tr(pmisc[0:32, 256 + (t % 2) * 128:256 + (t % 2 + 1) * 128], Mb[:, t, :], identb[:],
                             r=[("Mb", t), "identb"], w=["pmisc"])
                    P.evac(mt[0:32, half * 256:(half + 1) * 256], pmisc[0:32, 256:512], r=["pmisc"], w=[("MT", par)], eng="dve")

            items = []
            for gi, (h, j) in enumerate(groups):
                for kt in range(4 * j + 4):
                    items.append((gi, h, j, kt))

            def stage_a(n):
                gi, h, j, kt = items[n]
                par = gi % 2
                qg, mt, kts = QTg[par], MT[par], KTs[h % 2]
                nb = kt // 2
                qlo = max(0, 128 * (kt - 4 * j))
                psi = n % NPS
                ps_ = psc[psi]
                P.mm(ps_[:, qlo:512], kts[:, kt * 128:(kt + 1) * 128], qg[:, qlo:512], start=True, stop=False,
                     r=[("KTs", h % 2), ("QTg", par)], w=[("psc", psi)])
                rel = 512 * j - 128 * kt
                near = rel <= 1536
                if kt < 4 * j + 2:
                    P.mm(ps_[:, qlo:512], emat[:, nb, :], mt[:, qlo:512], start=False, stop=not near,
                         r=["emat", ("MT", par)], w=[("psc", psi)])
                if near:
                    o0 = rel + 384 + qlo
                    P.mm(ps_[:, qlo:512], identb[:], EB[h % 2][:, o0:o0 + 512 - qlo], start=False, stop=True,
                         r=["identb", ("EB", h % 2)], w=[("psc", psi)])

            def stage_b(n):
                gi, h, j, kt = items[n]
                eb = EB[h % 2]
                qlo = max(0, 128 * (kt - 4 * j))
                rel = 512 * j - 128 * kt
                psi, pti = n % NPS, n % NPT
                ps_, pt_ = psc[psi], PT[pti]
                if rel <= 1536:
                    P.act(pt_[:, qlo:512], ps_[:, qlo:512], AF.Exp, r=[("psc", psi), "zcol"], w=[("PT", pti)], scale=SCALE,
                          bias=zcol[:, 0:1])
                else:
                    P.act(pt_[:, qlo:512], ps_[:, qlo:512], AF.Exp, r=[("psc", psi), "bfar"], w=[("PT", pti)],
                          scale=SCALE, bias=bfar_t[:, h:h + 1])

            def stage_c(n):
                gi, h, j, kt = items[n]
                par = gi % 2
                sg, va = SGg[par], Va[h % 2]
                qlo = max(0, 128 * (kt - 4 * j))
                pti = n % NPT
                pt_ = PT[pti]
                for t in range(qlo // 128, 4):
                    P.mm(po[t][:, 0:129], pt_[:, t * 128:(t + 1) * 128], va[:, kt, 0:129], start=(kt == 0),
                         stop=(kt == 4 * j + t), r=[("PT", pti), ("Va", h % 2)], w=[("po", t)])
                t = kt - 4 * j
                if t >= 0:
                    P.v("reciprocal", r=[("po", t)], w=[("rc", t)], out=rc[:, t:t + 1], in_=po[t][:, 128:129])
                    P.v("scalar_tensor_tensor", r=[("po", t), ("rc", t), ("SGg", par)], w=[("ya", t)], out=ya[:, t, :],
                        in0=po[t][:, 0:128], scalar=rc[:, t:t + 1], in1=sg[:, t, :], op0=ALU.mult, op1=ALU.mult)
                    P.tr(pmisc[:, 512 + t * 128:512 + (t + 1) * 128], ya[:, t, :], identb[:], r=[("ya", t), "identb"],
                         w=["pmisc"])
                if kt == 4 * j + 3:
                    ys = yst[par]
                    P.evac(ys[:], pmisc[:, 512:1024], r=["pmisc"], w=[("yst", par)], eng="dve")
                    P.dma(yt_slice(h * 128, j), ys[:], r=[("yst", par)], w=[("YT", h, j)], key="yst%d" % par, q=STQ)
                    if gi + 2 < len(groups):
                        h2, j2 = groups[gi + 2]
                        load_qg(h2, j2, par)
                    if j == NG - 1 and h + 2 < 4:
                        load_head(h + 2)

            load_head(0)
            load_head(1)
            load_qg(groups[0][0], groups[0][1], 0)
            load_qg(groups[1][0], groups[1][1], 1)
            NI = len(items)
            for n in range(NI + 2):
                if n < NI:
                    if items[n][3] == 0:
                        prologue(items[n][0])
                    stage_a(n)
                if 0 <= n - 1 < NI:
                    stage_b(n - 1)
                if 0 <= n - 2 < NI:
                    stage_c(n - 2)
            P.pop()

        if upto >= 3:
            P.push()
            PIECE = min(S, 2048)
            NPC = S // PIECE
            CPP = PIECE // 128
            sel = P.sb("sel", [2, 2, 128], F32)
            gi = P.sb("gi", [2, PIECE], F32)
            gf = P.sb("gf", [2, PIECE], F32)
            cs = P.sb("cs", [2, PIECE], F32)
            aa = P.sb("aa", [2, PIECE], F32)
            gg = P.sb("gg", [2, PIECE], F32)
            zer = P.sb("zer", [2, PIECE], F32)
            uu = P.sb("uu", [2, PIECE], F32)
            ph = P.sb("ph", [2, PIECE], F32)
            carry = P.sb("carry", [2, 2], F32)
            gends = P.sb("gends", [2, NCH + 1], F32)
            brow = P.sb("brow", [2, NCH], F32)
            UT = P.sb("UT", [128, NCH, 2, 2], F32)
            betab = P.sb("betab", [128, 2, NCH], F32)
            pu = P.ps("pu", [128, 512], F32)
            P.dma(sel[:], sel_d.rearrange("p (h k) -> p h k", k=128), w=["sel"], key="m0")
            P.v("memset", w=["zer"], ap=zer[:], constant=0.0)
            P.v("memset", w=["carry"], ap=carry[:], constant=0.0)
            P.v("memset", w=["gends"], ap=gends[:], constant=0.0)
            for pc in range(NPC):
                psl = slice(pc * PIECE, (pc + 1) * PIECE)
                P.dma(gi[:], GId[:, psl], w=["gi"], key="m1")
                P.dma(gf[:], GFd[:, psl], w=["gf"], key="m2")
                P.act(gf[:], gf[:], AF.Exp, r=["gf", "zcol"], w=["gf"], scale=-1.0, bias=zcol[0:2, 0:1])
                P.act(gf[:], gf[:], AF.Ln, r=["gf", "zcol"], w=["gf"], bias=zcol[0:2, 1:2])
                P.v("tensor_tensor_scan", r=["gf", "zer", "carry"], w=["cs"], out=cs[:], data0=gf[:], data1=zer[:],
                    initial=carry[:, 0:1], op0=ALU.add, op1=ALU.add)
                P.v("tensor_tensor", r=["gi", "cs"], w=["aa"], out=aa[:], in0=gi[:], in1=cs[:], op=ALU.add)
                P.v("tensor_tensor_scan", r=["aa", "zer", "carry"], w=["gg"], out=gg[:], data0=aa[:], data1=zer[:],
                    initial=carry[:, 1:2], op0=ALU.max, op1=ALU.max)
                P.v("tensor_copy", r=["cs"], w=["carry"], out=carry[:, 0:1], in_=cs[:, PIECE - 1:PIECE])
                P.v("tensor_copy", r=["gg"], w=["carry"], out=carry[:, 1:2], in_=gg[:, PIECE - 1:PIECE])
                P.v("tensor_copy", r=["gg"], w=["gends"], out=gends[:, 1 + pc * CPP:1 + (pc + 1) * CPP],
                    in_=gg[:].rearrange("p (c k) -> p c k", k=128)[:, :, 127])
                for c in range(CPP):
                    cg = pc * CPP + c
                    csl = slice(c * 128, (c + 1) * 128)
                    P.v("tensor_scalar", r=["aa", "gends"], w=["uu"], out=uu[:, csl], in0=aa[:, csl],
                        scalar1=gends[:, cg + 1:cg + 2], scalar2=None, op0=ALU.subtract)
                    P.v("tensor_scalar", r=["cs", "gends"], w=["ph"], out=ph[:, csl], in0=cs[:, csl],
                        scalar1=gends[:, cg + 1:cg + 2], scalar2=None, op0=ALU.subtract)
                P.act(uu[:], uu[:], AF.Exp, r=["uu", "zcol"], w=["uu"], bias=zcol[0:2, 0:1])
                P.act(ph[:], ph[:], AF.Exp, r=["ph", "zcol"], w=["ph"], bias=zcol[0:2, 0:1])
                for c in range(CPP):
                    csl = slice(c * 128, (c + 1) * 128)
                    P.tr(pu[:, c * 4:c * 4 + 2], uu[:, csl], identf[0:2, 0:2], r=["uu", "identf"], w=["pu"])
                    P.tr(pu[:, c * 4 + 2:c * 4 + 4], ph[:, csl], identf[0:2, 0:2], r=["ph", "identf"], w=["pu"])
                P.v("tensor_copy", r=["pu"], w=["UT"], out=UT[:, pc * CPP:(pc + 1) * CPP, :, :],
                    in_=pu[:, 0:CPP * 4].rearrange("p (c a b) -> p c a b", a=2, b=2))
            P.v("tensor_tensor", r=["gends"], w=["brow"], out=brow[:], in0=gends[:, 0:NCH], in1=gends[:, 1:NCH + 1],
                op=ALU.subtract)
            P.act(brow[:], brow[:], AF.Exp, r=["brow", "zcol"], w=["brow"], bias=zcol[0:2, 0:1])
            for hl in range(2):
                P.mm(pu[:, 0:NCH], sel[:, hl, :], brow[:], r=["sel", "brow", "UT"], w=["pu"])
                P.v("tensor_copy", r=["pu"], w=["betab"], out=betab[:, hl, :], in_=pu[:, 0:NCH])

            qk = [P.sb("qk%d" % i, [128, 4, 512], BF16) for i in range(2)]
            xz = [P.sb("xz%d" % i, [128, 4, 512], BF16) for i in range(2)]
            kv = [P.sb("kv%d" % i, [128, 4, 514], BF16) for i in range(2)]
            SM = [P.sb("SM%d" % i, [128, 128], BF16) for i in range(2)]
            Vp = [P.sb("Vp%d" % i, [128, 258], BF16) for i in range(2)]
            Dst = P.sb("Dst", [128, 2, 257], F32)
            Cb = [P.sb("Cb%d" % i, [128, 2, 258], BF16) for i in range(2)]
            den = P.sb("den", [128, 2], F32)
            hsc = P.sb("hsc", [128, 256], F32)
            bst = P.sb("bst", [128, 6], F32)
            mv = P.sb("mv", [128, 2], F32)
            rstd = P.sb("rstd", [128, 2], F32)
            hn = P.sb("hn", [128, 256], BF16)
            tmp = P.sb("tmp", [128, 2, 128], F32)
            ymst = [P.sb("ymst%d" % i, [128, 2, 512], BF16) for i in range(2)]
            pst = P.ps("pst", [128, 512], F32)
            pn = [P.ps("pn%d" % i, [128, 512], F32) for i in range(2)]
            pdc = [P.ps("pdc%d" % i, [128, 512], F32) for i in range(2)]
            pht = P.ps("pht", [128, 1024], BF16)
            for i in range(2):
                P.v("memset", w=[("kv", i)], ap=kv[i][:, :, 512:514], constant=1.0)

            def load_m(hl, g, par):
                tsl = slice(g * 512, (g + 1) * 512)
                for i in range(2):
                    ti = 2 * hl + i
                    P.dma(qk[par][:, i, :], QmT[ti, :, tsl], w=[("qk", par)], key="qk%d" % par)
                    P.dma(qk[par][:, 2 + i, :], KmT[ti, :, tsl], w=[("qk", par)], key="qk%d" % par)
                    P.dma(xz[par][:, i, :], XSd[ti, :, tsl], w=[("xz", par)], key="xz%d" % par)
                    P.dma(xz[par][:, 2 + i, :], SZd[ti, :, tsl], w=[("xz", par)], key="xz%d" % par)
                P.dma(kv[par][:, :, 0:256], Kmd[tsl, hl * 256:(hl + 1) * 256].rearrange("(c p) d -> p c d", p=128),
                      w=[("kv", par)], key="kv%d" % par)
                P.dma(kv[par][:, :, 256:512], Vmd[tsl, hl * 256:(hl + 1) * 256].rearrange("(c p) d -> p c d", p=128),
                      w=[("kv", par)], key="kv%d" % par)

            it = 0
            load_m(0, 0, 0)
            for hl in range(2):
                for g in range(NG):
                    par = it % 2
                    it += 1
                    if g + 1 < NG:
                        load_m(hl, g + 1, it % 2)
                    elif hl + 1 < 2:
                        load_m(hl + 1, 0, it % 2)
                    q_k, x_z, k_v, ym = qk[par], xz[par], kv[par], ymst[par]
                    for cc in range(4):
                        c = g * 4 + cc
                        csl = slice(cc * 128, (cc + 1) * 128)
                        sm, vp, cbf = SM[c % 2], Vp[c % 2], Cb[c % 2]
                        pnn, pnt = pn[c % 2], ("pn", c % 2)
                        if c > 0:
                            for dc in range(2):
                                P.act(cbf[:, dc, 0:257], Dst[:, dc, :], AF.Copy, r=[("Dst", dc), "betab"], w=[("Cb", c % 2, dc)],
                                      scale=betab[:, hl, c:c + 1])
                        for i in range(2):
                            P.mm(pst[:, 0:128], q_k[:, 2 + i, csl], q_k[:, i, csl], start=(i == 0), stop=(i == 1),
                                 r=[("qk", par)], w=["pst"])
                        P.v("tensor_tensor", r=["pst", "causb"], w=[("SM", c % 2)], out=sm[:], in0=pst[:, 0:128], in1=causb[:],
                            op=ALU.mult)
                        P.act(vp[:, 0:257], k_v[:, cc, 256:513], AF.Copy, r=[("kv", par), "UT"], w=[("Vp", c % 2)],
                              scale=UT[:, c, 0, hl:hl + 1])
                        P.mm(pnn[:, 0:257], sm[:], vp[:, 0:257], start=True, stop=(c == 0), r=[("SM", c % 2), ("Vp", c % 2)],
                             w=[pnt])
                        if c > 0:
                            for dc in range(2):
                                P.mm(pnn[:, 0:257], q_k[:, dc, csl], cbf[:, dc, 0:257], start=False, stop=(dc == 1),
                                     r=[("qk", par), ("Cb", c % 2, dc)], w=[pnt])
                        for dc in range(2):
                            P.mm(pdc[dc][:, 0:257], k_v[:, cc, dc * 128:(dc + 1) * 128], vp[:, 0:257],
                                 r=[("kv", par), ("Vp", c % 2)], w=[("pdc", dc)])
                            if c == 0:
                                P.v("tensor_copy", r=[("pdc", dc)], w=[("Dst", dc)], out=Dst[:, dc, :], in_=pdc[dc][:, 0:257])
                            else:
                                P.v("scalar_tensor_tensor", r=[("pdc", dc), ("Dst", dc), "betab"], w=[("Dst", dc)],
                                    out=Dst[:, dc, :], in0=Dst[:, dc, :], scalar=betab[:, hl, c:c + 1], in1=pdc[dc][:, 0:257],
                                    op0=ALU.mult, op1=ALU.add)
                        P.act(den[:, 0:1], pnn[:, 256:257], AF.Abs, r=[pnt, "zcol"], w=["den"], bias=zcol[:, 0:1])
                        P.v("tensor_tensor", r=["den", "UT"], w=["den"], out=den[:, 0:1], in0=den[:, 0:1],
                            in1=UT[:, c, 1, hl:hl + 1], op=ALU.max)
                        P.v("reciprocal", r=["den"], w=["den"], out=den[:, 1:2], in_=den[:, 0:1])
                        P.act(hsc[:], pnn[:, 0:256], AF.Copy, r=[pnt, "den"], w=["hsc"], scale=den[:, 1:2])
                        P.v("bn_stats", r=["hsc"], w=["bst"], out=bst[:], in_=hsc[:])
                        P.v("bn_aggr", r=["bst"], w=["mv"], out=mv[:], in_=bst[:])
                        P.act(rstd[:, 0:1], mv[:, 1:2], AF.Sqrt, r=["mv"], w=["rstd"], bias=1e-5)
                        P.v("reciprocal", r=["rstd"], w=["rstd"], out=rstd[:, 1:2], in_=rstd[:, 0:1])
                        P.v("tensor_scalar", r=["hsc", "mv", "rstd"], w=["hn"], out=hn[:], in0=hsc[:], scalar1=mv[:, 0:1],
                            scalar2=rstd[:, 1:2], op0=ALU.subtract, op1=ALU.mult)
                        for i in range(2):
                            P.tr(pht[:, i * 128:(i + 1) * 128], hn[:, i * 128:(i + 1) * 128], identb[:], r=["hn", "identb"],
                                 w=["pht"])
                        for i in range(2):
                            ti = 2 * hl + i
                            P.v("scalar_tensor_tensor", r=["pht", "mhn", ("xz", par)], w=[("tmp", i)], out=tmp[:, i, :],
                                in0=pht[:, i * 128:(i + 1) * 128], scalar=mhn_t[:, ti:ti + 1], in1=x_z[:, i, csl],
                                op0=ALU.mult, op1=ALU.add)
                            P.v("tensor_tensor", r=[("tmp", i), ("xz", par)], w=[("ymst", par)], eng="pool", out=ym[:, i, csl],
                                in0=tmp[:, i, :], in1=x_z[:, 2 + i, csl], op=ALU.mult)
                    for i in range(2):
                        ti = 2 * hl + i
                        P.dma(yt_slice(512 + ti * 128, g), ym[:, i, :],
                              r=[("ymst", par)], w=[("YTm", ti, g)], key="ymst%d" % par, q=STQ)
            P.pop()

        if upto >= 4:
            P.push()
            rg = [[0, 1], [2, 3], [4, 5], [6, 7]]
            for blk in range(NB):
                P.add("pool", (lambda e, blk=blk: e.collective_compute("AllGather", ALU.bypass, replica_groups=rg,
                                                                       ins=[YTsrc[blk].opt()], outs=[YTall[blk].opt()])),
                      w=[("YTall", blk)], dma_key=P.mapkey("cc"), inc=1)
            Wo = P.sb("Wo", [128, 16, D], BF16)
            wos = [P.sb("wos%d" % i, [128, D], F32) for i in range(2)]
            ytl = [P.sb("ytl%d" % i, [128, 16, 512], BF16) for i in range(2)]
            xr = [P.sb("xr%d" % i, [128, 4, D], F32) for i in range(2)]
            ot = [P.sb("ot%d" % i, [128, 4, D], F32) for i in range(2)]
            junk3 = P.sb("junk3", [128, 512], BF16)
            s2 = P.sb("s2", [128, 4], F32)
            py = [[P.ps("py%d_%d" % (i, k), [128, 512], F32) for k in range(2)] for i in range(2)]
            for ck in range(16):
                P.dma(wos[ck % 2][:], w_out[ck * 128:(ck + 1) * 128, :], w=[("wos", ck % 2)], key="wos%d" % (ck % 2))
                P.v("tensor_copy", r=[("wos", ck % 2)], w=[("Wo", ck)], eng=("dve" if ck % 2 else "pool"),
                    out=Wo[:, ck, :], in_=wos[ck % 2][:])
            wo_tok = [("Wo", ck) for ck in range(16)]

            def load3(g):
                blk, off = (g * 512) // TB, (g * 512) % TB
                P.dma(ytl[g % 2][:], YTall[blk, :, off:off + 512].rearrange("(k p) t -> p k t", p=128),
                      r=[("YTall", blk)], w=[("ytl", g % 2)], key="ytl%d" % (g % 2))
                P.dma(xr[g % 2][:], x[g * 512:(g + 1) * 512, :].rearrange("(t p) c -> p t c", p=128),
                      w=[("xr", g % 2)], key="xr%d" % (g % 2))

            load3(0)
            n3 = 0
            for g in range(NG):
                if g + 1 < NG:
                    load3(g + 1)
                yt_, xr_, ot_ = ytl[g % 2], xr[g % 2], ot[g % 2]
                for t in range(4):
                    pp = py[n3 % 2]
                    ptk = [("py", n3 % 2, 0), ("py", n3 % 2, 1)]
                    n3 += 1
                    for hf in range(2):
                        for ck in range(16):
                            P.mm(pp[hf][:], yt_[:, ck, t * 128:(t + 1) * 128], Wo[:, ck, hf * 512:(hf + 1) * 512],
                                 start=(ck == 0), stop=(ck == 15), r=wo_tok + [("ytl", g % 2)] if ck in (0, 15) else (), w=[ptk[hf]])
                        P.act(junk3[:], pp[hf][:], AF.Square, r=[ptk[hf]], w=["junk3", ("s2", hf)], accum_out=s2[:, hf:hf + 1])
                    P.v("tensor_tensor", r=[("s2", 0), ("s2", 1)], w=[("s2", 2)], out=s2[:, 2:3], in0=s2[:, 0:1], in1=s2[:, 1:2],
                        op=ALU.add)
                    P.act(s2[:, 3:4], s2[:, 2:3], AF.Sqrt, r=[("s2", 2)], w=[("s2", 3)], scale=1.0 / D, bias=1e-6)
                    P.v("reciprocal", r=[("s2", 3)], w=[("s2", 3)], out=s2[:, 3:4], in_=s2[:, 3:4])
                    for hf in range(2):
                        hs = slice(hf * 512, (hf + 1) * 512)
                        P.v("scalar_tensor_tensor", r=[ptk[hf], ("s2", 3), "gpost_b"], w=[("ot", g % 2, t, hf)],
                            out=ot_[:, t, hs], in0=pp[hf][:], scalar=s2[:, 3:4], in1=gpost_b[:, hs], op0=ALU.mult, op1=ALU.mult)
                        P.v("tensor_tensor", r=[("ot", g % 2, t, hf), ("xr", g % 2)], w=[("ot", g % 2, t, hf)], eng="pool",
                            out=ot_[:, t, hs], in0=ot_[:, t, hs], in1=xr_[:, t, hs], op=ALU.add)
                P.dma(out[g * 512:(g + 1) * 512, :].rearrange("(t p) c -> p t c", p=128), ot_[:],
                      r=[("ot", g % 2, t, hf) for t in range(4) for hf in range(2)], w=[("out", g)], key="ot%d" % (g % 2))
            P.pop()
        P.emit()
    return nc, P


def _t5_bucket_np(dist):
    n = np.maximum(dist, 0)
    max_exact = 16
    nf = np.maximum(n, 1).astype(np.float32)
    large = max_exact + (np.log(nf / np.float32(max_exact)) / np.float32(math.log(2048 / max_exact))
                         * np.float32(32 - max_exact)).astype(np.int32)
    large = np.minimum(large, 31)
    return np.where(n < max_exact, n, large)


_CONSTS = {}


def _consts():
    if _CONSTS:
        return _CONSTS
    bf = ml_dtypes.bfloat16
    _CONSTS["identb"] = np.eye(128, dtype=np.float32).astype(bf)
    _CONSTS["identf"] = np.eye(128, dtype=np.float32)
    s = np.arange(128)
    _CONSTS["causb"] = (s[:, None] <= s[None, :]).astype(np.float32).astype(bf)
    e = np.zeros((32, 32, 128), np.float32)
    for n in range(32):
        e[n, n, :] = 1.0
    _CONSTS["emat"] = e.reshape(32, 32 * 128).astype(bf)
    own = np.arange(32)[:, None]
    nn = np.arange(32)[None, :]
    ng = np.where(nn < own, 0.0, -1e30).astype(np.float32)
    _CONSTS["negm"] = np.ascontiguousarray(np.broadcast_to(ng[None], (128, 32, 32))).reshape(128, 1024)
    sel = np.zeros((2, 2, 128), np.float32)
    sel[0, 0, :] = 1.0
    sel[1, 1, :] = 1.0
    _CONSTS["sel"] = sel.reshape(2, 256)
    k = np.arange(128)[:, None]
    xx = np.arange(XW)[None, :]
    dist = xx - k - 384
    _CONSTS["bucket"] = _t5_bucket_np(dist)
    _CONSTS["neg"] = dist < 0
    return _CONSTS


def core_inputs(inp, b, hh, S):
    C = _consts()
    f32 = np.float32
    w_in = np.asarray(inp["w_in"][0], f32)
    heads = [4 * hh + i for i in range(4)]
    ch_order = np.concatenate([np.arange(512 * hh, 512 * hh + 512), np.arange(512 * (1 - hh), 512 * (1 - hh) + 512)])

    def hcols(base):
        return np.concatenate([w_in[:, base + 128 * h: base + 128 * h + 128] for h in heads], axis=1)

    w_in_c = np.concatenate([hcols(0), hcols(1024), w_in[:, 4096 + ch_order], w_in[:, 5120 + ch_order[:512]],
                             hcols(2048), hcols(3072)], axis=1)
    d = {"x": np.ascontiguousarray(np.asarray(inp["x"][b, :S], f32)),
         "gpre": np.asarray(inp["g_pre"], f32).reshape(1, D),
         "gpost": np.asarray(inp["g_post"], f32).reshape(1, D),
         "w_in": np.ascontiguousarray(w_in_c)}
    w_out = np.asarray(inp["w_out"][0], f32)
    rows = []
    for r in range(2):
        rows.append(w_out[512 * r:512 * r + 512])
        rows.append(w_out[1024 + 512 * r:1024 + 512 * r + 512])
    d["w_out"] = np.ascontiguousarray(np.concatenate(rows, axis=0))
    rel = np.asarray(inp["rel_bias"], f32)
    b2 = np.empty((4, 128, XW), f32)
    for i, h in enumerate(heads):
        t = rel[:, h][C["bucket"]]
        t = np.where(C["neg"], np.float32(-30000.0), t)
        b2[i] = t
    d["b2"] = b2
    d["bfar"] = np.ascontiguousarray(np.broadcast_to(rel[31, heads][None, :], (128, 4))).astype(f32)
    conv_w = np.asarray(inp["conv_w"][0], f32)[:, ch_order]
    d["cw"] = np.ascontiguousarray(conv_w.reshape(4, 8, 128).transpose(2, 1, 0)).reshape(128, 32)
    d["cb"] = np.ascontiguousarray(np.asarray(inp["conv_b"][0], f32)[ch_order].reshape(8, 128).T)
    blk_order = ch_order.reshape(256, 4)[:, 0] // 4
    bd = np.zeros((3, 8, 128, 128), f32)
    for wi, nm in enumerate(("wq_m", "wk_m", "wv_m")):
        w = np.asarray(inp[nm][0], f32)[blk_order]
        for ti in range(8):
            for nl in range(32):
                bd[wi, ti, 4 * nl:4 * nl + 4, 4 * nl:4 * nl + 4] = w[32 * ti + nl]
    d["bd"] = np.ascontiguousarray(bd[:, 0:4].reshape(12, 128, 128).transpose(1, 0, 2)).reshape(128, 12 * 128)
    d["bdT"] = np.ascontiguousarray(bd.transpose(0, 1, 3, 2).reshape(24, 128, 128).transpose(1, 0, 2)).reshape(128, 24 * 128)
    w_if = np.asarray(inp["w_if"][0], f32)
    gcols = [2 * hh, 2 * hh + 1, 4 + 2 * hh, 4 + 2 * hh + 1]
    wif = np.stack([w_if[1024 * wi + ch_order][:, gcols] for wi in range(3)], axis=0)
    d["wif"] = np.ascontiguousarray(wif.reshape(3, 8, 128, 4).transpose(2, 0, 1, 3)).reshape(128, 96)
    b_if = np.asarray(inp["b_if"][0], f32)
    d["bg"] = np.ascontiguousarray(np.stack([b_if[gcols[0:2]], b_if[gcols[2:4]]], axis=1))
    d["mhn"] = np.ascontiguousarray(np.asarray(inp["mh_norm"][0], f32)[ch_order[:512]].reshape(4, 128).T)
    d["skp"] = np.ascontiguousarray(np.asarray(inp["skip"][0], f32)[ch_order[:512]].reshape(4, 128).T)
    for k in ("identb", "identf", "causb", "emat", "negm", "sel"):
        d[k] = C[k]
    return d


_PROG_CACHE = {}


def run(inputs, S, dbg=()):
    key = (S, tuple(dbg))
    if key not in _PROG_CACHE:
        _PROG_CACHE[key] = build_program(S, dbg)[0]
    nc = _PROG_CACHE[key]
    in_maps = [core_inputs(inputs, c // 2, c % 2, S) for c in range(8)]
    res = run_bass_kernel_spmd(nc, in_maps, core_ids=list(range(8)))
    return res.results


def kernel(**inputs):
    S = inputs["x"].shape[1]
    res = run(inputs, S)
    outp = np.empty((4, S, D), np.float32)
    for b in range(4):
        outp[b, :S // 2] = res[2 * b]["out"][:S // 2]
        outp[b, S // 2:] = res[2 * b + 1]["out"][S // 2:]
    return outp
```
